# Optimizing a Trainium2 kernel written in Bass

```python
import jax, jax.numpy as jnp
from jax import lax
import numpy as np

D_MODEL = 1024
BATCH = 8
SEQ = 2048
DEPTH = 4
DEC_BATCH = 128
DEC_SEQ = 4
PAST_LEN = 16384
PAGE_SIZE = 128

N_META = 16
MIX_WIDTH = D_MODEL
HG_WIDTH = MIX_WIDTH // 2
HG_HEADS = 4
HG_DK = HG_WIDTH // HG_HEADS
HG_DV = HG_WIDTH // HG_HEADS
POOL_WIDTH = MIX_WIDTH - HG_WIDTH
POOL_WINDOWS = (2, 4, 8, 16)
POOL_GROUPS = len(POOL_WINDOWS)
POOL_GC = POOL_WIDTH // POOL_GROUPS
POOL_STATE = max(POOL_WINDOWS) - 1
IN_COLS = 4 * HG_WIDTH + POOL_WIDTH
D_FF = ((8 * D_MODEL // 3 + 127) // 128) * 128
CHUNK = 64
EPS = 1e-6

kernel_name = "hgrn2_pool_macaron_hybrid_step"


def rmsnorm(x, gain):
    xf = x.astype(jnp.float32)
    y = xf * lax.rsqrt(jnp.mean(xf * xf, axis=-1, keepdims=True) + EPS)
    return (y * gain.astype(jnp.float32)).astype(x.dtype)


def swiglu(h, w_in, w_out):
    gate, up = jnp.split(h @ w_in, 2, axis=-1)
    return (jax.nn.silu(gate) * up) @ w_out


def layer_lower_bounds(lb_logits):
    p = jax.nn.softmax(lb_logits.astype(jnp.float32), axis=0)
    cs = jnp.cumsum(p, axis=0)
    return cs - cs[0:1]


def hgrn_chunk(S, q, k, v, g):
    L = q.shape[2]
    b = jnp.cumsum(g, axis=2)
    causal = jnp.tril(jnp.ones((L, L), dtype=bool))
    diff = b[:, :, :, None, :] - b[:, :, None, :, :]
    decay = jnp.exp(jnp.where(causal[None, None, :, :, None], diff, -jnp.inf))
    scores = jnp.einsum('bhtd,bhtsd,bhsd->bhts', q, decay, k)
    o = jnp.einsum('bhts,bhsv->bhtv', scores, v) + jnp.einsum('bhtd,bhdv->bhtv', q * jnp.exp(b), S)
    b_last = b[:, :, -1:, :]
    S_new = jnp.exp(b_last[:, :, 0, :])[..., None] * S + jnp.einsum('bhsd,bhsv->bhdv', k * jnp.exp(b_last - b), v)
    return S_new, o


def hgrn_prompt(q, k, v, g):
    Bn, H, T, _ = q.shape
    S0 = jnp.zeros((Bn, H, HG_DK, HG_DV), jnp.float32)
    S1, o_meta = hgrn_chunk(S0, q[:, :, :N_META], k[:, :, :N_META], v[:, :, :N_META], g[:, :, :N_META])
    n_chunks = (T - N_META) // CHUNK

    def to_chunks(a):
        a = a[:, :, N_META:].reshape(Bn, H, n_chunks, CHUNK, a.shape[-1])
        return jnp.moveaxis(a, 2, 0)

    S_fin, o_rest = lax.scan(lambda S, xs: hgrn_chunk(S, *xs), S1,
                             (to_chunks(q), to_chunks(k), to_chunks(v), to_chunks(g)))
    o_rest = jnp.moveaxis(o_rest, 0, 2).reshape(Bn, H, T - N_META, HG_DV)
    return S_fin, jnp.concatenate([o_meta, o_rest], axis=2)


def multiscale_pool(u_ext, n_out, pool_w, pool_scale):
    Bn, T, C = u_ext.shape
    uf = u_ext.astype(jnp.float32)
    cs0 = jnp.concatenate([jnp.zeros((Bn, 1, C), jnp.float32), jnp.cumsum(uf, axis=1)], axis=1)
    idx = jnp.arange(1, T + 1, dtype=jnp.float32)
    means = []
    for gi, w in enumerate(POOL_WINDOWS):
        c = cs0[:, :, gi * POOL_GC:(gi + 1) * POOL_GC]
        shifted = jnp.concatenate([jnp.zeros((Bn, w, POOL_GC), jnp.float32), c[:, :T + 1 - w]], axis=1)
        count = jnp.minimum(jnp.float32(w), idx)
        means.append((c[:, 1:] - shifted[:, 1:]) / count[None, :, None])
    pooled = jnp.concatenate(means, axis=-1)[:, T - n_out:]
    d = (pooled - uf[:, T - n_out:]).reshape(Bn, n_out, POOL_GROUPS, POOL_GC)
    y = jnp.einsum('btgc,gcd->btgd', d, pool_w.astype(jnp.float32)).reshape(Bn, n_out, POOL_WIDTH)
    return y * pool_scale.astype(jnp.float32)


def token_mixing(h, lb, w_in, hg_norm, pool_w, pool_scale, w_out, hg_state, pool_prev):
    Bn, T, _ = h.shape
    z = h @ w_in
    zq = z[..., :HG_WIDTH].astype(jnp.float32)
    zf = z[..., HG_WIDTH:2 * HG_WIDTH].astype(jnp.float32)
    zi = z[..., 2 * HG_WIDTH:3 * HG_WIDTH].astype(jnp.float32)
    zg = z[..., 3 * HG_WIDTH:4 * HG_WIDTH].astype(jnp.float32)
    zp = z[..., 4 * HG_WIDTH:]

    def heads(a):
        return a.reshape(Bn, T, HG_HEADS, -1).transpose(0, 2, 1, 3)

    q = heads(jax.nn.silu(zq))
    k = heads((1.0 - lb) * jax.nn.sigmoid(-zf))
    g = heads(jnp.logaddexp(jnp.log(lb), jnp.log1p(-lb) + jax.nn.log_sigmoid(zf)))
    v = heads(zi)
    if hg_state is None:
        S_new, o = hgrn_prompt(q, k, v, g)
    else:
        S_new, o = hgrn_chunk(hg_state.astype(jnp.float32), q, k, v, g)
    o = o * lax.rsqrt(jnp.mean(o * o, axis=-1, keepdims=True) + EPS) * hg_norm.astype(jnp.float32)[None, :, None, :]
    o = o.transpose(0, 2, 1, 3).reshape(Bn, T, HG_WIDTH) * jax.nn.silu(zg)

    u_ext = zp if pool_prev is None else jnp.concatenate([pool_prev.astype(zp.dtype), zp], axis=1)
    p = multiscale_pool(u_ext, T, pool_w, pool_scale)
    mixed = jnp.concatenate([o.astype(h.dtype), p.astype(h.dtype)], axis=-1) @ w_out
    return mixed, S_new, u_ext[:, -POOL_STATE:]


def run_trunk(x, state_hgrn, state_pool, lbs, norm_ffn1, w_ffn1_in, w_ffn1_out, norm_mix, w_in, hg_norm,
              pool_w, pool_scale, w_out, norm_ffn2, w_ffn2_in, w_ffn2_out, norm_final):
    hg_states, pool_states = [], []
    for l in range(DEPTH):
        x = x + 0.5 * swiglu(rmsnorm(x, norm_ffn1[l]), w_ffn1_in[l], w_ffn1_out[l])
        m, S, u = token_mixing(rmsnorm(x, norm_mix[l]), lbs[l], w_in[l], hg_norm[l], pool_w[l], pool_scale[l],
                               w_out[l],
                               None if state_hgrn is None else state_hgrn[l],
                               None if state_pool is None else state_pool[l])
        x = x + m
        x = x + 0.5 * swiglu(rmsnorm(x, norm_ffn2[l]), w_ffn2_in[l], w_ffn2_out[l])
        hg_states.append(S)
        pool_states.append(u)
    return rmsnorm(x, norm_final), jnp.stack(hg_states), jnp.stack(pool_states)


def setup_inputs(seed: int = 0) -> dict:
    key = jax.random.key(seed)
    ks = jax.random.split(key, 20)
    f32 = jnp.float32
    nrm = lambda k, shape, scale: jax.random.normal(k, shape, f32) * scale
    return {
        'x_prompt': nrm(ks[0], (BATCH, SEQ, D_MODEL), 1.0),
        'x_sample': nrm(ks[1], (DEC_BATCH, DEC_SEQ, D_MODEL), 1.0),
        'state_hgrn': nrm(ks[2], (DEPTH, DEC_BATCH, HG_HEADS, HG_DK, HG_DV), 0.5),
        'state_pool': nrm(ks[3], (DEPTH, DEC_BATCH, POOL_STATE, POOL_WIDTH), 1.0),
        'meta': nrm(ks[4], (N_META, D_MODEL), 1.0),
        'lb_logits': nrm(ks[5], (DEPTH, HG_WIDTH), 0.5),
        'norm_ffn1': 1.0 + nrm(ks[6], (DEPTH, D_MODEL), 0.05),
        'w_ffn1_in': nrm(ks[7], (DEPTH, D_MODEL, 2 * D_FF), D_MODEL ** -0.5),
        'w_ffn1_out': nrm(ks[8], (DEPTH, D_FF, D_MODEL), D_FF ** -0.5),
        'norm_mix': 1.0 + nrm(ks[9], (DEPTH, D_MODEL), 0.05),
        'w_in': nrm(ks[10], (DEPTH, D_MODEL, IN_COLS), D_MODEL ** -0.5),
        'hg_norm': 1.0 + nrm(ks[11], (DEPTH, HG_HEADS, HG_DV), 0.05),
        'pool_w': nrm(ks[12], (DEPTH, POOL_GROUPS, POOL_GC, POOL_GC), POOL_GC ** -0.5),
        'pool_scale': 1.0 + nrm(ks[13], (DEPTH, POOL_WIDTH), 0.1),
        'w_out': nrm(ks[14], (DEPTH, MIX_WIDTH, D_MODEL), MIX_WIDTH ** -0.5),
        'norm_ffn2': 1.0 + nrm(ks[15], (DEPTH, D_MODEL), 0.05),
        'w_ffn2_in': nrm(ks[16], (DEPTH, D_MODEL, 2 * D_FF), D_MODEL ** -0.5),
        'w_ffn2_out': nrm(ks[17], (DEPTH, D_FF, D_MODEL), D_FF ** -0.5),
        'norm_final': 1.0 + nrm(ks[18], (D_MODEL,), 0.05),
    }


def reference(x_prompt, x_sample, state_hgrn, state_pool, meta, lb_logits, norm_ffn1, w_ffn1_in, w_ffn1_out,
              norm_mix, w_in, hg_norm, pool_w, pool_scale, w_out, norm_ffn2, w_ffn2_in, w_ffn2_out, norm_final):
    lbs = layer_lower_bounds(lb_logits)
    weights = (norm_ffn1, w_ffn1_in, w_ffn1_out, norm_mix, w_in, hg_norm, pool_w, pool_scale, w_out,
               norm_ffn2, w_ffn2_in, w_ffn2_out, norm_final)
    meta_b = jnp.broadcast_to(meta.astype(x_prompt.dtype)[None], (x_prompt.shape[0], N_META, D_MODEL))
    xp = jnp.concatenate([meta_b, x_prompt], axis=1)
    yp, hg_p, pool_p = run_trunk(xp, None, None, lbs, *weights)
    y_prompt = yp[:, N_META:]
    y_sample, hg_s, pool_s = run_trunk(x_sample, state_hgrn, state_pool, lbs, *weights)
    return (y_prompt, y_sample, hg_p, pool_p, hg_s, pool_s)
```

```python
import numpy as np
import concourse.bass as bass
import concourse.mybir as mybir
from concourse.bass_utils import run_bass_kernel_spmd

F32 = mybir.dt.float32
BF16 = mybir.dt.bfloat16
U8 = mybir.dt.uint8
AF = mybir.ActivationFunctionType
ALU = mybir.AluOpType
AX = mybir.AxisListType

NCORES = 8
D = 1024
KC = 8
NPR = 2064
NSM = 64
T = NPR + NSM
NSEQ = 16
DFF = 2816
NH = 4
EPS = 1e-6
FT = [(0, 448), (448, 448), (896, 448), (1344, 448), (1792, 336)]
GROUPS = [(0, 4), (4, 4), (8, 4), (12, 4), (16, 4), (20, 2)]
NPAGE = 7
GS_LAT = 200.0
GS_PRIO_WIN = 0.0
GS_TRIES = [(0.0, 0)] + [(w, s) for s in range(1, 9) for w in (0.0, 120.0)]
PAGE = 8192
V_NF1, V_NMX, V_NF2, V_NFIN, V_HGN, V_PSC, V_LBL, NV = 0, 32, 64, 96, 104, 120, 136, 152
C_ID, C_CM, C_SM, C_SMC, C_RMA, C_RMB, C_INV, NCST = 0, 128, 256, 320, 336, 848, 1168, 1184


class Op:
    __slots__ = ("eng", "fn", "deps", "dma", "sig", "count", "waits", "dj")

    def __init__(self, eng, fn, deps, dma):
        self.eng, self.fn, self.deps, self.dma = eng, fn, deps, dma
        self.sig = False
        self.count = 0
        self.waits = []
        self.dj = -1


class Prog:
    KQ = 8

    def __init__(self):
        self.ops = []
        self.lastw = {}
        self.readers = {}
        self.last_on = {}
        self.pending_dma = []

    def add(self, eng, fn, reads=(), writes=(), dma=False, extra=(), arena=False):
        idx = len(self.ops)
        if dma and arena:
            reads = list(reads) + ["phase"]
        deps = {}
        for r in reads:
            w = self.lastw.get(r)
            if w is not None:
                deps[w] = True
        for r in writes:
            w = self.lastw.get(r)
            if w is not None:
                deps.setdefault(w, False)
            for rd in self.readers.get(r, ()):
                if rd != idx:
                    deps.setdefault(rd, False)
        for e in extra:
            deps[e] = True
        for r in reads:
            self.readers.setdefault(r, []).append(idx)
        for r in writes:
            self.lastw[r] = idx
            self.readers[r] = []
        self.ops.append(Op(eng, fn, deps, dma))
        self.last_on[eng] = idx
        if dma and arena:
            self.pending_dma.append(idx)
        return idx

    def barrier(self):
        lasts = [v for v in self.last_on.values()]
        dmas = list(self.pending_dma)
        self.pending_dma = []
        for eng in ("pe", "act", "dve"):
            self.add(eng, lambda e: e.nop(), extra=lasts + dmas, writes=(["phase"] if eng == "dve" else []))

    def finalize(self):
        ops = self.ops
        for op in ops:
            per = {}
            waits = []
            for d, raw in op.deps.items():
                od = ops[d]
                if od.dma:
                    waits.append(d)
                    continue
                if od.eng == op.eng and not op.dma and op.eng == "pe":
                    continue
                if d > per.get(od.eng, -1):
                    per[od.eng] = d
            waits.extend(per.values())
            op.waits = waits
            for d in waits:
                ops[d].sig = True
        cnt = {}
        dj = {}
        for op in ops:
            if op.dma:
                op.dj = dj.get(op.eng, 0)
                dj[op.eng] = op.dj + 1
            elif op.sig:
                cnt[op.eng] = cnt.get(op.eng, 0) + 1
                op.count = cnt[op.eng]

    def emit(self, nc, block, esem, qsem):
        ops = self.ops
        KQ = self.KQ

        def token(d):
            od = ops[d]
            if od.dma:
                return qsem[od.eng][od.dj % KQ], 16 * (od.dj // KQ + 1)
            return esem[od.eng], od.count

        def run(engname):
            def body(e):
                waited = {}
                ndma = 0
                for op in ops:
                    if op.eng != engname:
                        continue
                    for d in op.waits:
                        sem, val = token(d)
                        if waited.get(id(sem), 0) >= val:
                            continue
                        e.wait_ge(sem, val)
                        waited[id(sem)] = val
                    if op.dma:
                        j = op.dj
                        sem = qsem[engname][j % KQ]
                        if j >= KQ:
                            val = 16 * (j // KQ)
                            if waited.get(id(sem), 0) < val:
                                e.wait_ge(sem, val)
                                waited[id(sem)] = val
                        ins = op.fn(e)
                        ins.then_inc(sem, 16)
                        ndma = j + 1
                    else:
                        ins = op.fn(e)
                        if op.sig:
                            ins.then_inc(esem[engname], 1)
                for s in range(min(ndma, KQ)):
                    n_on = (ndma - 1 - s) // KQ + 1
                    val = 16 * n_on
                    sem = qsem[engname][s]
                    if waited.get(id(sem), 0) < val:
                        e.wait_ge(sem, val)
            return body

        block.tensor(run("pe"))
        block.scalar(run("act"))
        block.vector(run("dve"))
        block.gpsimd(run("pool"))
        block.sync(run("sp"))


def build_program(L=4):
    nc = bass.Bass("TRN2", target_bir_lowering=False)
    P = Prog()

    def dram(name, shape, kind):
        return nc.dram_tensor(name, list(shape), F32, kind=kind).ap()

    xT = dram("xT", [D, T], "ExternalInput")
    vecs_d = dram("vecs", [128, NV], "ExternalInput")
    cst_d = dram("cst", [128, NCST], "ExternalInput")
    wf_in = [dram("w_ffn1_in", [4, D, 2 * DFF], "ExternalInput"), dram("w_ffn2_in", [4, D, 2 * DFF], "ExternalInput")]
    wf_out = [dram("w_ffn1_out", [4, DFF, D], "ExternalInput"), dram("w_ffn2_out", [4, DFF, D], "ExternalInput")]
    w_in_d = dram("w_in", [4, D, 2560], "ExternalInput")
    w_out_d = dram("w_out", [4, D, D], "ExternalInput")
    poolw_d = dram("pool_w", [4, 4, 128, 128], "ExternalInput")
    shg_d = dram("state_hgrn", [4, NSEQ, NH, 128, 128], "ExternalInput")
    spl_d = dram("state_poolT", [4, 4, 128, NSEQ * 15], "ExternalInput")
    yT = dram("yT", [D, T], "ExternalOutput")
    hgp_d = dram("hgp", [4, NH, 128, 128], "ExternalOutput")
    plp_d = dram("poolpT", [4, 4, 128, 15], "ExternalOutput")
    hgs_d = dram("hgs", [4, NSEQ, NH, 128, 128], "ExternalOutput")
    pls_d = dram("poolsT", [4, 4, 128, NSEQ * 15], "ExternalOutput")

    ARENA = 80448
    import contextlib
    with contextlib.ExitStack() as es:
        def sb(name, shape, dt):
            return es.enter_context(nc.sbuf_tensor(name, list(shape), dt))

        xres = sb("xres", [128, KC, T], F32)
        vec = sb("vec", [128, NV], F32)
        lbw = sb("lbw", [128, 64], F32)
        nc1t = sb("nc1t", [128, 16], F32)
        lbs = sb("lbs", [128, 8], F32)
        identb = sb("identb", [128, 128], BF16)
        onesb = sb("onesb", [128, 128], BF16)
        zerob = sb("zerob", [128, 128], BF16)
        cmaskb = sb("cmaskb", [128, 4, 128], BF16)
        smaskb = sb("smaskb", [64, 4, 64], BF16)
        smcol = sb("smcol", [64, 16], F32)
        rmA = sb("rmA", [128, 128], F32)
        rmB = sb("rmB", [128, 80], F32)
        invc = sb("invc", [128, 16], F32)
        poolw = sb("poolw", [128, 2, 4, 128], BF16)
        wpool = sb("wpool", [128, NPAGE * PAGE], U8)
        arena = sb("arena", [128, ARENA], U8)
        ps = es.enter_context(nc.psum_tensor("ps", [128, 8 * 512], F32))
        esem = {k: es.enter_context(nc.semaphore("e_" + k)) for k in ("pe", "act", "dve", "pool", "sp")}
        qsem = {k: [es.enter_context(nc.semaphore("q_%s%d" % (k, i))) for i in range(Prog.KQ)]
                for k in ("pool", "sp", "act")}

        class Carver:
            def __init__(self):
                self.off = 0

            def get(self, nbytes, dt, pattern=None, parts=128, **kw):
                nbytes = (nbytes + 63) // 64 * 64
                assert self.off + nbytes <= ARENA, ("arena overflow", self.off + nbytes)
                v = arena[0:parts, self.off:self.off + nbytes].bitcast(dt)
                self.off += nbytes
                if pattern:
                    v = v.rearrange(pattern, **kw)
                return v

        cf = Carver()
        h_full = cf.get(KC * T * 2, BF16, "p (k t) -> p k t", k=KC)
        aring = cf.get(3 * 4 * 448 * 2, BF16, "p (s j t) -> p s j t", s=3, j=4)
        sgt = cf.get(2 * 448 * 2, BF16, "p (s t) -> p s t", s=2)
        f_nsq = cf.get(KC * 448 * 2, BF16, "p (k t) -> p k t", k=KC)
        f_nln = cf.get(448 * 4, F32)
        f_nrstd = cf.get(448 * 4, F32)
        cstage = cf.get(NCST * 4, F32)
        f_ytmp = arena[:, 0:KC * 448 * 4].bitcast(F32).rearrange("p (k t) -> p k t", k=KC)
        cm = Carver()
        m_nsq = cm.get(KC * 128 * 2, BF16, "p (k t) -> p k t", k=KC)
        m_nln = cm.get(128 * 4, F32)
        m_nrstd2 = cm.get(2 * 128 * 4, F32, "p (s t) -> p s t", s=2)
        hm_0 = cm.get(KC * 128 * 2, BF16, "p (k t) -> p k t", k=KC)
        qb_0 = cm.get(512 * 2, BF16, "p (h t) -> p h t", h=4)
        qt2 = cm.get(2 * 512 * 2, BF16, "p (s h t) -> p s h t", s=2, h=4)
        th_0 = cm.get(512 * 4, F32, "p (h t) -> p h t", h=4)
        kf_0 = cm.get(512 * 4, F32, "p (h t) -> p h t", h=4)
        gl_0 = cm.get(512 * 4, F32, "p (h t) -> p h t", h=4)
        bb_0 = cm.get(512 * 4, F32, "p (h t) -> p h t", h=4)
        Ep2 = cm.get(2 * 512 * 4, F32, "p (s h t) -> p s h t", s=2, h=4)
        kt2 = cm.get(2 * 512 * 2, BF16, "p (s h t) -> p s h t", s=2, h=4)
        gate2 = cm.get(2 * 512 * 2, BF16, "p (s h t) -> p s h t", s=2, h=4)
        U = cm.get(4 * 144 * 4, F32, "p (g t) -> p g t", g=4)
        _wab_off = cm.off
        wa = cm.get(160 * 4, F32)
        wb = cm.get(160 * 4, F32)
        osum = arena[:, _wab_off:_wab_off + 1024].bitcast(F32)
        wfix = cm.get(16 * 4, F32)
        dd = cm.get(512 * 2, BF16, "p (g t) -> p g t", g=4)
        vtok2 = cm.get(2 * 512 * 2, BF16, "p (s c) -> p s c", s=2)
        ktok = cm.get(512 * 2, BF16)
        scm = cm.get(512 * 2, BF16)
        osq = cm.get(512 * 2, BF16)
        olnv = cm.get(512 * 4, F32)
        orstd = cm.get(512 * 4, F32)
        t1 = cm.get(512 * 2, BF16)
        mixed2 = cm.get(3 * KC * 128 * 2, BF16, "p (s k t) -> p s k t", s=3, k=KC)
        Sst = cm.get(512 * 4, F32, "p (h v) -> p h v", h=4)
        tmpS = cm.get(512 * 4, F32, "p (h v) -> p h v", h=4)
        Sbf = cm.get(4 * 512 * 2, BF16, "p (s h v) -> p s h v", s=4, h=4)
        _ss_off = cm.off
        Ss = cm.get(4 * 512 * 4, F32, "p (i h v) -> p i h v", i=4, h=4)
        Ssb = cm.get(4 * 512 * 2, BF16, "p (i h v) -> p i h v", i=4, h=4)
        def _alias(off, nbytes, dt):
            return arena[:, off:off + nbytes].bitcast(dt).rearrange("p (h t) -> p h t", h=4)
        th_1 = _alias(_ss_off, 2048, F32)
        kf_1 = _alias(_ss_off + 2048, 2048, F32)
        gl_1 = _alias(_ss_off + 4096, 2048, F32)
        bb_1 = _alias(_ss_off + 6144, 2048, F32)
        qb_1 = _alias(_ss_off + 8192, 1024, BF16)
        th2, kf2, gl2, bb2, qb2 = (th_0, th_1), (kf_0, kf_1), (gl_0, gl_1), (bb_0, bb_1), (qb_0, qb_1)
        ODDKEYS = [("th", 1, h) for h in range(4)] + [("kf", 1), ("gl", 1), ("bb", 1), ("Em", 1), ("qb", 1)]
        _km_off = cm.off
        km = cm.get(4 * 512 * 2, BF16, "p (i c) -> p i c", i=4)
        hm_1 = arena[:, _km_off:_km_off + KC * 128 * 2].bitcast(BF16).rearrange("p (k t) -> p k t", k=KC)
        hm2 = (hm_0, hm_1)
        uext = cm.get(4 * 16 * 19 * 4, F32, "p (g i r) -> p g i r", g=4, i=16)
        ustage = cm.get(4 * 240 * 4, F32, "p (g i r) -> p g i r", g=4, i=16)
        oint = cm.get(256 * 4, F32)

        def page(pg, dt=BF16):
            return wpool[:, pg * PAGE:(pg + 1) * PAGE].bitcast(dt)

        bank_ctr = [0]
        pinned = set()

        pool_ctr = {"s1": 0, "s2": 0, "s3": 0, "t2": 0}
        POOLS = {"s1": (0, 4), "s2": (4, 2), "s3": (6, 1), "t2": (0, 6)}
        FILL_BANK = 7

        def newbank(pool="all"):
            if pool == "all":
                while True:
                    b = bank_ctr[0] % 8
                    bank_ctr[0] += 1
                    if b not in pinned:
                        return b
            base, cnt = POOLS[pool]
            while True:
                b = base + pool_ctr[pool] % cnt
                pool_ctr[pool] += 1
                if b not in pinned:
                    return b

        def psb(b, n=512, parts=128):
            return ps[0:parts, b * 512:b * 512 + n]

        def xkeys(kcs, t0, t1):
            return [("x", k, u) for k in kcs for u in range(t0 // 64, (t1 + 63) // 64)]

        ALLK = range(KC)

        xTv = xT.rearrange("(k p) t -> p k t", p=128)
        for (t0_, tn_) in FT:
            P.add("sp", (lambda e, t0_=t0_, tn_=tn_: e.dma_start(out=xres[:, :, t0_:t0_ + tn_], in_=xTv[:, :, t0_:t0_ + tn_])),
                  writes=xkeys(ALLK, t0_, t0_ + tn_), dma=True)
        P.add("sp", lambda e: e.dma_start(out=vec[:, :], in_=vecs_d[:, :]), writes=["vec"], dma=True)
        P.add("sp", lambda e: e.dma_start(out=cstage[:, :], in_=cst_d[:, :]), writes=["cstage"], dma=True, arena=True)
        P.add("dve", lambda e: e.tensor_copy(out=identb[:, :], in_=cstage[:, C_ID:C_ID + 128]), reads=["cstage"])
        for h in range(4):
            P.add("dve", (lambda e, h=h: e.tensor_copy(out=cmaskb[:, h, :], in_=cstage[:, C_CM:C_CM + 128])),
                  reads=["cstage"])
            P.add("dve", (lambda e, h=h: e.tensor_copy(out=smaskb[:, h, :], in_=cstage[0:64, C_SM:C_SM + 64])),
                  reads=["cstage"])
        P.add("dve", lambda e: e.tensor_copy(out=smcol[:, :], in_=cstage[0:64, C_SMC:C_SMC + 16]), reads=["cstage"])
        P.add("dve", lambda e: e.tensor_copy(out=rmA[:, :], in_=cstage[:, C_RMA:C_RMA + 128]), reads=["cstage"])
        P.add("dve", lambda e: e.tensor_copy(out=rmB[:, :], in_=cstage[:, C_RMB:C_RMB + 80]), reads=["cstage"])
        P.add("dve", lambda e: e.tensor_copy(out=invc[:, :], in_=cstage[:, C_INV:C_INV + 16]), reads=["cstage"],
              writes=["consts"])
        P.add("dve", lambda e: e.memset(onesb[:, :], 1.0), writes=["onesb"])
        P.add("dve", lambda e: e.memset(zerob[:, :], 0.0), writes=["zerob"])
        e_t = lbw[:, 0:16]
        p_t = lbw[:, 16:32]
        lbv = lbw[:, 32:48]
        c1t = lbw[:, 48:64]
        P.add("act", lambda e: e.activation(out=e_t, in_=vec[:, V_LBL:V_LBL + 16], func=AF.Exp),
              reads=["vec"], writes=["lb_e"])
        P.add("dve", lambda e: e.tensor_reduce(out=lbs[:, 0:4], in_=e_t.rearrange("p (h l) -> p h l", h=4),
                                               axis=AX.X, op=ALU.add), reads=["lb_e"], writes=["lb_s"])
        P.add("dve", lambda e: e.reciprocal(out=lbs[:, 4:8], in_=lbs[:, 0:4]), reads=["lb_s"], writes=["lb_r"])
        P.add("dve", lambda e: e.tensor_tensor(out=p_t.rearrange("p (h l) -> p h l", h=4),
                                               in0=e_t.rearrange("p (h l) -> p h l", h=4),
                                               in1=lbs[:, 4:8].unsqueeze(2).broadcast_to([128, 4, 4]), op=ALU.mult),
              reads=["lb_e", "lb_r"], writes=["lb_p"])
        lb3 = lbv.rearrange("p (h l) -> p h l", h=4)
        p3 = p_t.rearrange("p (h l) -> p h l", h=4)
        P.add("dve", lambda e: e.memset(lbv, 0.0), writes=["lb_v"])
        P.add("dve", lambda e: e.tensor_copy(out=lb3[:, :, 1], in_=p3[:, :, 1]), reads=["lb_p", "lb_v"], writes=["lb_v1"])
        P.add("dve", lambda e: e.tensor_tensor(out=lb3[:, :, 2], in0=lb3[:, :, 1], in1=p3[:, :, 2], op=ALU.add),
              reads=["lb_v1", "lb_p"], writes=["lb_v2"])
        P.add("dve", lambda e: e.tensor_tensor(out=lb3[:, :, 3], in0=lb3[:, :, 2], in1=p3[:, :, 3], op=ALU.add),
              reads=["lb_v2", "lb_p"], writes=["lb_v3"])
        P.add("dve", lambda e: e.tensor_scalar(out=c1t, in0=lbv, scalar1=-0.5, scalar2=0.5, op0=ALU.mult, op1=ALU.add),
              reads=["lb_v", "lb_v1", "lb_v2", "lb_v3"], writes=["c1"])
        P.add("dve", lambda e: e.tensor_scalar(out=nc1t[:, :], in0=c1t, scalar1=-1.0, scalar2=None, op0=ALU.mult),
              reads=["c1"], writes=["nc1"])

        def vcol(off):
            return vec[:, off:off + 1]

        def dma_ffn_group(l, which, g):
            j0, cnt = GROUPS[g]
            slot = g % 2
            pg = (3 * slot, 3 * slot + 1, 3 * slot + 2)
            Wi = wf_in[which][l].rearrange("(k p) n -> p k n", p=128)
            Wo = wf_out[which][l].rearrange("(j p) n -> p j n", p=128)
            wgv = page(pg[0]).rearrange("p (k n) -> p k n", k=8)
            wuv = page(pg[1]).rearrange("p (k n) -> p k n", k=8)
            wov = page(pg[2]).rearrange("p (j n) -> p j n", j=4)
            n = cnt * 128
            P.add("pool", lambda e: e.dma_start(out=wgv[:, :, 0:n], in_=Wi[:, :, j0 * 128:j0 * 128 + n]),
                  writes=[("W", pg[0])], dma=True)
            P.add("pool", lambda e: e.dma_start(out=wuv[:, :, 0:n], in_=Wi[:, :, DFF + j0 * 128:DFF + j0 * 128 + n]),
                  writes=[("W", pg[1])], dma=True)
            P.add("pool", lambda e: e.dma_start(out=wov[:, 0:cnt, :], in_=Wo[:, j0:j0 + cnt, :]),
                  writes=[("W", pg[2])], dma=True)

        MIXPG = {"q": 6, "f": 0, "i": 1, "g": 2, "pool": 3, "oA": 4, "oB": 5}

        def dma_mix_w(l, which):
            Wi = w_in_d[l].rearrange("(k p) n -> p k n", p=128)
            Wo = w_out_d[l].rearrange("(k p) n -> p k n", p=128)
            cgi = {"q": 0, "f": 1, "i": 2, "g": 3, "pool": 4}
            if which in cgi:
                pg = MIXPG[which]
                c0 = cgi[which] * 512
                dst = page(pg).rearrange("p (k n) -> p k n", k=8)
                P.add("pool", lambda e: e.dma_start(out=dst, in_=Wi[:, :, c0:c0 + 512]), writes=[("W", pg)], dma=True)
            else:
                pg = MIXPG[which]
                k0 = 0 if which == "oA" else 4
                dst = page(pg).rearrange("p (k n) -> p k n", k=4)
                P.add("pool", lambda e: e.dma_start(out=dst, in_=Wo[:, k0:k0 + 4, :]), writes=[("W", pg)], dma=True)

        def dma_poolw(l):
            src = poolw_d[l].rearrange("g c d -> c g d")
            P.add("pool", lambda e: e.dma_start(out=poolw[:, l % 2, :, :], in_=src), writes=[("poolw", l % 2)], dma=True)

        def norm_stats(t0, tn, nsq, nln, nrstd, rkey, pool="all"):
            bk = newbank(pool)
            PA("act", lambda e: e.activation(out=nsq[:, :, 0:tn], in_=xres[:, :, t0:t0 + tn], func=AF.Square),
               reads=xkeys(ALLK, t0, t0 + tn), writes=["nsq"])

            def mm(e):
                ins = None
                for k in range(KC):
                    ins = e.matmul(psb(bk, tn), lhsT=onesb[:, :], rhs=nsq[:, k, 0:tn], start=(k == 0), stop=(k == KC - 1))
                return ins
            PA("pe", mm, reads=["nsq", "onesb"], writes=[("ps", bk)])
            PA("act", lambda e: e.activation(out=nln[:, 0:tn], in_=psb(bk, tn), func=AF.Ln, bias=EPSB[:, 0:1], scale=1.0 / D),
               reads=[("ps", bk), "epsb"], writes=["nln"])
            PA("act", lambda e: e.activation(out=nrstd[:, 0:tn], in_=nln[:, 0:tn], func=AF.Exp, scale=-0.5),
               reads=["nln"], writes=[rkey])

        def norm_apply(t0, tn, gain_off, nrstd, rkey, out_fn, out_keys_fn):
            for k in range(KC):
                PA("dve", (lambda e, k=k: e.scalar_tensor_tensor(
                    out=out_fn(k), in0=xres[:, k, t0:t0 + tn], scalar=vcol(gain_off + k), in1=nrstd[:, 0:tn],
                    op0=ALU.mult, op1=ALU.mult)),
                    reads=xkeys([k], t0, t0 + tn) + [rkey, "vec"], writes=out_keys_fn(k))

        def norm_tile(t0, tn, gain_off, nsq, nln, nrstd, out_fn, out_keys_fn, pool="all"):
            norm_stats(t0, tn, nsq, nln, nrstd, "nrstd", pool)
            norm_apply(t0, tn, gain_off, nrstd, "nrstd", out_fn, out_keys_fn)

        epsb_t = sb("epsb", [128, 2], F32)
        EPSB = epsb_t
        P.add("dve", lambda e: e.memset(epsb_t[:, :], EPS), writes=["epsb"])

        def ffn(l, which):
            gain_off = (V_NF1 if which == 0 else V_NF2) + l * 8
            normed = set()

            def need_norm(n):
                if n < len(FT) and n not in normed:
                    normed.add(n)
                    t0, tn = FT[n]
                    norm_tile(t0, tn, gain_off, f_nsq, f_nln, f_nrstd,
                              (lambda k, t0=t0, tn=tn: h_full[:, k, t0:t0 + tn]),
                              (lambda k, n=n: [("h", k, n)]))
            need_norm(0)
            need_norm(1)
            sg_ctr = [0]
            def ffn_group(g, j0, cnt):
                if g + 1 < len(GROUPS):
                    dma_ffn_group(l, which, g + 1)
                else:
                    if which == 0:
                        pass
                    elif l + 1 < L:
                        dma_ffn_group(l + 1, 0, 0)
                if which == 0 and g == 0:
                    dma_mix_w(l, "q")
                    dma_poolw(l)
                if which == 0 and g == len(GROUPS) - 1:
                    dma_mix_w(l, "f")
                    dma_mix_w(l, "i")
                    dma_mix_w(l, "g")
                slot = g % 2
                pg = (3 * slot, 3 * slot + 1, 3 * slot + 2)
                wgv = page(pg[0]).rearrange("p (k n) -> p k n", k=8)
                wuv = page(pg[1]).rearrange("p (k n) -> p k n", k=8)
                wov = page(pg[2]).rearrange("p (j n) -> p j n", j=4)

                def emit_in(n):
                    t0, tn = FT[n]
                    ar = n % 3
                    for j in range(cnt):
                        bg = newbank()
                        bu = newbank()

                        def mmg(e, j=j, bg=bg):
                            ins = None
                            for k in range(KC):
                                ins = e.matmul(psb(bg, tn), lhsT=wgv[:, k, j * 128:(j + 1) * 128],
                                               rhs=h_full[:, k, t0:t0 + tn], start=(k == 0), stop=(k == KC - 1))
                            return ins

                        def mmu(e, j=j, bu=bu):
                            ins = None
                            for k in range(KC):
                                ins = e.matmul(psb(bu, tn), lhsT=wuv[:, k, j * 128:(j + 1) * 128],
                                               rhs=h_full[:, k, t0:t0 + tn], start=(k == 0), stop=(k == KC - 1))
                            return ins
                        hk = [("h", k, n) for k in range(KC)]
                        P.add("pe", mmg, reads=hk + [("W", pg[0])], writes=[("ps", bg)])
                        P.add("pe", mmu, reads=hk + [("W", pg[1])], writes=[("ps", bu)])
                        s2 = sg_ctr[0] % 2
                        sg_ctr[0] += 1
                        P.add("act", (lambda e, bg=bg, s2=s2: e.activation(out=sgt[:, s2, 0:tn], in_=psb(bg, tn), func=AF.Silu)),
                              reads=[("ps", bg)], writes=[("sgt", s2)])
                        P.add("dve", (lambda e, bu=bu, s2=s2, j=j: e.tensor_tensor(
                            out=aring[:, ar, j, 0:tn], in0=psb(bu, tn), in1=sgt[:, s2, 0:tn], op=ALU.mult)),
                            reads=[("ps", bu), ("sgt", s2)], writes=[("a", ar, j)])

                def emit_out(n):
                    t0, tn = FT[n]
                    ar = n % 3
                    for m in range(KC):
                        bo = newbank()

                        def mmo(e, m=m, bo=bo):
                            ins = None
                            for j in range(cnt):
                                ins = e.matmul(psb(bo, tn), lhsT=wov[:, j, m * 128:(m + 1) * 128],
                                               rhs=aring[:, ar, j, 0:tn], start=(j == 0), stop=(j == cnt - 1))
                            return ins
                        P.add("pe", mmo, reads=[("a", ar, j) for j in range(cnt)] + [("W", pg[2])], writes=[("ps", bo)])
                        P.add("dve", (lambda e, m=m, bo=bo: e.scalar_tensor_tensor(
                            out=xres[:, m, t0:t0 + tn], in0=psb(bo, tn), scalar=0.5, in1=xres[:, m, t0:t0 + tn],
                            op0=ALU.mult, op1=ALU.add)),
                            reads=[("ps", bo)] + xkeys([m], t0, t0 + tn), writes=xkeys([m], t0, t0 + tn))

                for n in range(len(FT) + 1):
                    if n < len(FT):
                        emit_in(n)
                        need_norm(n + 2)
                    if n >= 1:
                        emit_out(n - 1)
                if which == 0 and g == len(GROUPS) - 1:
                    dma_mix_w(l, "pool")
                    dma_mix_w(l, "oA")
                    dma_mix_w(l, "oB")

            for g, (j0, cnt) in enumerate(GROUPS):
                ffn_group(g, j0, cnt)

        class Rec:
            def __init__(self):
                self.items = []

            def add(self, eng, fn, reads=(), writes=(), dma=False, extra=(), arena=False):
                self.items.append((eng, fn, reads, writes, dma, extra, arena))

        sink = [P]

        def PA(*a, **k):
            return sink[0].add(*a, **k)

        class _FakeIns:
            def then_inc(self, *a, **k):
                return self

        class FakeEng:
            S_SET = (AF.Silu, AF.Tanh)
            B_SET = (AF.Ln, AF.Exp)

            def __init__(self):
                self.cost = 0.0
                self.aset = None

            @staticmethod
            def _n(ap):
                n = 1
                for s in ap.shape[1:]:
                    n *= s
                return n

            def matmul(self, out, lhsT, rhs, **k):
                self.cost += max(self._n(rhs), 128) / 2.4 + 4
                return _FakeIns()

            def transpose(self, out, in_, identity, **k):
                self.cost += 60
                return _FakeIns()

            def activation(self, out, in_, func, **k):
                self.cost += 220 + self._n(out) / 1.2
                if func in self.S_SET:
                    self.aset = "S"
                elif func in self.B_SET:
                    self.aset = "B"
                return _FakeIns()

            def copy(self, out, in_, **k):
                self.cost += 220 + self._n(out) / 1.2
                return _FakeIns()

            def _dve(self, out, f=1.0):
                self.cost += 110 + f * self._n(out) / 0.96
                return _FakeIns()

            def tensor_tensor(self, out, in0, in1, op, **k):
                return self._dve(out)

            def scalar_tensor_tensor(self, out, **k):
                return self._dve(out)

            def tensor_scalar(self, out, **k):
                return self._dve(out, 0.7)

            def tensor_copy(self, out, in_, **k):
                return self._dve(out, 0.7)

            def memset(self, ap, c):
                return self._dve(ap, 0.7)

            def tensor_tensor_scan(self, out, **k):
                return self._dve(out, 2.0)

            def tensor_reduce(self, out, **k):
                return self._dve(out)

            def reciprocal(self, out, in_):
                return self._dve(out)

            def dma_start(self, out, in_, **k):
                self.cost += 60
                return _FakeIns()

            def nop(self):
                self.cost += 20
                return _FakeIns()

        def merge(*recs):
            lists = [r.items for r in recs]
            rw = []
            for items in lists:
                Rk, Wk = set(), set()
                for it in items:
                    Rk.update(it[2])
                    Wk.update(it[3])
                rw.append((Rk, Wk))
            for i_ in range(len(rw)):
                for j_ in range(len(rw)):
                    if i_ != j_:
                        bad = rw[i_][1] & (rw[j_][0] | rw[j_][1])
                        assert not bad, ("concurrently merged lists share written resources", sorted(map(str, bad))[:8])
            info = {}
            per_eng = {}
            for li, items in enumerate(lists):
                lastw, readers = {}, {}
                for idx, it in enumerate(items):
                    eng, fn, reads, writes, dma = it[0], it[1], it[2], it[3], it[4]
                    deps = set()
                    for r in reads:
                        if r in lastw:
                            deps.add(lastw[r])
                    for r in writes:
                        if r in lastw:
                            deps.add(lastw[r])
                        deps.update(readers.get(r, ()))
                    deps.discard(idx)
                    for r in reads:
                        readers.setdefault(r, []).append(idx)
                    for r in writes:
                        lastw[r] = idx
                        readers[r] = []
                    fk = FakeEng()
                    fn(fk)
                    info[(li, idx)] = (eng, fk.cost, fk.aset, deps, dma)
                    per_eng.setdefault((li, eng), []).append(idx)
            ptr = {k: 0 for k in per_eng}
            free = {}
            finish = {}
            cur_set = [None]
            total = sum(len(x) for x in lists)
            order = []
            LAT = 150.0
            engs = sorted({k[1] for k in per_eng})
            while len(order) < total:
                best = None
                for eng in engs:
                    for li in range(len(lists) - 1, -1, -1):
                        lst = per_eng.get((li, eng))
                        if not lst or ptr[(li, eng)] >= len(lst):
                            continue
                        idx = lst[ptr[(li, eng)]]
                        e_, cost, aset, deps, dma = info[(li, idx)]
                        ok = True
                        ready = free.get(eng, 0.0)
                        for d in deps:
                            f = finish.get((li, d))
                            if f is None:
                                ok = False
                                break
                            ready = max(ready, f + LAT)
                        if not ok:
                            continue
                        pen = 0.0
                        if eng == "act" and aset and cur_set[0] and aset != cur_set[0]:
                            pen = 1300.0
                        key = (ready + pen, -li)
                        if best is None or key < best[0]:
                            best = (key, li, idx, eng, ready + pen, cost, aset, dma)
                _, li, idx, eng, start, cost, aset, dma = best
                ptr[(li, eng)] += 1
                if dma:
                    free[eng] = start + cost
                    finish[(li, idx)] = start + 2500.0
                else:
                    free[eng] = start + cost
                    finish[(li, idx)] = start + cost
                if eng == "act" and aset:
                    cur_set[0] = aset
                order.append((li, idx))
            for li, idx in order:
                P.add(*lists[li][idx])

        def gsched(items):
            n = len(items)
            deps = [None] * n
            lastw, readers = {}, {}
            cost = [0.0] * n
            aset = [None] * n
            for idx, it in enumerate(items):
                eng, fn, reads, writes = it[0], it[1], it[2], it[3]
                dset = set()
                for r in reads:
                    if r in lastw:
                        dset.add(lastw[r])
                for r in writes:
                    if r in lastw:
                        dset.add(lastw[r])
                    dset.update(readers.get(r, ()))
                dset.discard(idx)
                for r in reads:
                    readers.setdefault(r, []).append(idx)
                for r in writes:
                    lastw[r] = idx
                    readers[r] = []
                deps[idx] = dset
                fk = FakeEng()
                fn(fk)
                cost[idx] = fk.cost
                aset[idx] = fk.aset
            succ = [[] for _ in range(n)]
            indeg = [0] * n
            for i_, ds in enumerate(deps):
                indeg[i_] = len(ds)
                for d in ds:
                    succ[d].append(i_)
            LAT = GS_LAT
            FILL_MIN, FILL_MARGIN = 1200.0, 500.0
            bl0 = [0.0] * n
            for i_ in range(n - 1, -1, -1):
                m_ = 0.0
                for s_ in succ[i_]:
                    v_ = bl0[s_] + (LAT if items[s_][0] != items[i_][0] else 60.0)
                    if v_ > m_:
                        m_ = v_
                bl0[i_] = m_ + (2500.0 if items[i_][4] else cost[i_])
            indeg0 = list(indeg)

            def simulate(PRIO_WIN, seed):
                import random
                rng = random.Random(seed)
                bl = bl0 if seed == 0 else [v * (1.0 + 0.08 * rng.random()) for v in bl0]
                indeg = list(indeg0)
                ready = {}
                rt = [0.0] * n
                for i_ in range(n):
                    if indeg[i_] == 0:
                        ready.setdefault(items[i_][0], []).append(i_)
                free = {}
                finish = [0.0] * n
                cur_set = None
                order = []
                start_t = [0.0] * n
                while len(order) < n:
                    best = None
                    for eng, lst in ready.items():
                        fe = free.get(eng, 0.0)
                        cands = []
                        est = None
                        for i_ in lst:
                            st = rt[i_] if rt[i_] > fe else fe
                            if eng == "act" and aset[i_] and cur_set and aset[i_] != cur_set:
                                st += 1300.0
                            cands.append((st, i_))
                            if est is None or st < est:
                                est = st
                        if est is None:
                            continue
                        pick = None
                        for st, i_ in cands:
                            if st <= est + PRIO_WIN:
                                k_ = (-bl[i_], st, i_)
                                if pick is None or k_ < pick[0]:
                                    pick = (k_, st, i_)
                        key = (pick[1], pick[2])
                        if best is None or key < best:
                            best = key
                    st, i_ = best
                    start_t[i_] = st
                    eng = items[i_][0]
                    ready[eng].remove(i_)
                    dma = items[i_][4]
                    free[eng] = st + cost[i_]
                    finish[i_] = st + (2500.0 if dma else cost[i_])
                    if eng == "act" and aset[i_]:
                        cur_set = aset[i_]
                    order.append(i_)
                    for s_ in succ[i_]:
                        indeg[s_] -= 1
                        lat = LAT if items[s_][0] != eng else 60.0
                        if finish[i_] + lat > rt[s_]:
                            rt[s_] = finish[i_] + lat
                        if indeg[s_] == 0:
                            ready.setdefault(items[s_][0], []).append(s_)
                return max(finish), order, start_t

            bestres = None
            for win_, seed_ in GS_TRIES:
                res = simulate(win_, seed_)
                if bestres is None or res[0] < bestres[0]:
                    bestres = res
            _, order, start_t = bestres
            z_rhs = zerob[:, :].unsqueeze(1).broadcast_to([128, 4, 128])

            def filler(k):
                def f(e):
                    ins = None
                    for _ in range(k):
                        ins = e.matmul(psb(FILL_BANK).rearrange("p (a b) -> p a b", a=4), lhsT=zerob[:, :], rhs=z_rhs, start=True, stop=True)
                    return ins
                return f
            pe_end = None
            for i_ in order:
                it = items[i_]
                if it[0] == "pe":
                    st = start_t[i_]
                    if pe_end is not None and st - pe_end > FILL_MIN:
                        nf = int((st - pe_end - FILL_MARGIN) / 217.0)
                        while nf > 0:
                            k = min(nf, 4)
                            P.add("pe", filler(k), reads=["zerob"], writes=[("ps", FILL_BANK)])
                            nf -= k
                    pe_end = st + cost[i_]
                P.add(*it)

        def mixing(l):
            gain_off = V_NMX + l * 8
            w_q = page(MIXPG["q"]).rearrange("p (k n) -> p k n", k=8)
            w_f = page(MIXPG["f"]).rearrange("p (k n) -> p k n", k=8)
            w_i = page(MIXPG["i"]).rearrange("p (k n) -> p k n", k=8)
            w_g = page(MIXPG["g"]).rearrange("p (k n) -> p k n", k=8)
            w_p = page(MIXPG["pool"]).rearrange("p (k n) -> p k n", k=8)
            w_oA = page(MIXPG["oA"]).rearrange("p (k n) -> p k n", k=4)
            w_oB = page(MIXPG["oB"]).rearrange("p (k n) -> p k n", k=4)
            pw = poolw[:, l % 2, :, :]
            hgn = lambda h: vcol(V_HGN + l * 4 + h)
            psc = lambda g: vcol(V_PSC + l * 4 + g)
            c1c = lambda h: lbw[:, 48 + h * 4 + l:48 + h * 4 + l + 1]
            nc1c = lambda h: nc1t[:, h * 4 + l:h * 4 + l + 1]

            P.add("dve", lambda e: e.memset(Sst.rearrange("p h v -> p (h v)"), 0.0), writes=["S"])
            P.add("dve", lambda e: e.memset(Sbf[:, 0, :, :].rearrange("p h v -> p (h v)"), 0.0), writes=[("Sbf", 0)])
            P.add("dve", lambda e: e.memset(U[:, :, 0:16], 0.0), writes=["Uhalo"])
            sbf_ctr = [0]

            def zproj(wv, cg_key, tn, evac, pool, hm, hmkey):
                bk = newbank(pool)
                for h in range(4):
                    def mm(e, h=h, bk=bk):
                        ins = None
                        for k in range(KC):
                            ins = e.matmul(ps[:, bk * 512 + h * tn:bk * 512 + (h + 1) * tn], lhsT=wv[:, k, h * 128:(h + 1) * 128],
                                           rhs=hm[:, k, 0:tn], start=(k == 0), stop=(k == KC - 1))
                        return ins
                    PA("pe", mm, reads=[hmkey, ("W", MIXPG[cg_key])] + ([("ps", bk)] if h else []), writes=[("ps", bk)])
                evac(bk)

            def hgrn_part(par, c_lo, ncols, chunks, pool):
                ktp, qtp, Epp, vtp = kt2[:, par], qt2[:, par], Ep2[:, par], vtok2[:, par]
                cs = slice(c_lo, c_lo + ncols)
                bT = newbank(pool)
                psT = ps[0:ncols, bT * 512:(bT + 1) * 512].bitcast(BF16)

                def tr(e):
                    ins = None
                    for h in range(4):
                        ins = e.transpose(out=psT[:, h * 128:(h + 1) * 128], in_=ktp[:, h, cs], identity=identb[:, :])
                    return ins
                PA("pe", tr, reads=[("kt", par), "consts"], writes=[("ps", bT)])
                PA("act", lambda e: e.copy(out=ktok[0:ncols, :], in_=psT[:, 0:512]), reads=[("ps", bT)], writes=["ktok"])
                bS = newbank(pool)

                def mms(e):
                    ins = None
                    for h in range(4):
                        ins = e.matmul(ps[0:ncols, bS * 512 + h * ncols:bS * 512 + (h + 1) * ncols],
                                       lhsT=ktp[:, h, cs], rhs=qtp[:, h, cs], start=True, stop=True)
                    return ins
                PA("pe", mms, reads=[("kt", par), ("qt", par)], writes=[("ps", bS)])
                scv = scm[0:ncols, 0:4 * ncols].rearrange("p (h t) -> p h t", h=4)
                PA("dve", lambda e: e.tensor_tensor(
                    out=scv, in0=ps[0:ncols, bS * 512:bS * 512 + 4 * ncols].rearrange("p (h t) -> p h t", h=4),
                    in1=cmaskb[0:ncols, :, 0:ncols], op=ALU.mult),
                    reads=[("ps", bS), "consts"], writes=["scm"])
                slots_in = []
                for (off, ln) in chunks:
                    cur = sbf_ctr[0] % 4
                    slots_in.append(cur)
                    bA = newbank(pool)
                    rows = slice(off, off + ln)

                    def mma(e, rows=rows, bA=bA):
                        ins = None
                        for h in range(4):
                            ins = e.matmul(ps[:, bA * 512 + h * 128:bA * 512 + (h + 1) * 128],
                                           lhsT=ktok[rows, h * 128:(h + 1) * 128], rhs=vtp[rows, h * 128:(h + 1) * 128],
                                           start=True, stop=True)
                        return ins
                    PA("pe", mma, reads=["ktok", ("vtok", par)], writes=[("ps", bA)])
                    PA("dve", (lambda e, bA=bA: e.tensor_tensor(
                        out=tmpS.rearrange("p h v -> p (h v)"), in0=psb(bA), in1=Sst.rearrange("p h v -> p (h v)"), op=ALU.add)),
                        reads=[("ps", bA), "S"], writes=["tmpS"])
                    lastcol = c_lo + off + ln - 1
                    dec = Epp[:, :, lastcol:lastcol + 1].broadcast_to([128, 4, 128])
                    PA("dve", (lambda e, dec=dec: e.tensor_tensor(out=Sst[:, :, :], in0=tmpS[:, :, :], in1=dec, op=ALU.mult)),
                       reads=["tmpS", ("Ep", par)], writes=["S"])
                    nxt = (sbf_ctr[0] + 1) % 4
                    sbf_ctr[0] += 1
                    PA("act", (lambda e, nxt=nxt: e.copy(out=Sbf[:, nxt, :, :].rearrange("p h v -> p (h v)"),
                                                       in_=Sst.rearrange("p h v -> p (h v)"))),
                       reads=["S"], writes=[("Sbf", nxt)])
                bO = newbank(pool)

                def mmo(e):
                    ins = None
                    for h in range(4):
                        base = bO * 512 + h * ncols
                        e.matmul(ps[:, base:base + ncols], lhsT=vtp[0:ncols, h * 128:(h + 1) * 128],
                                 rhs=scm[0:ncols, h * ncols:(h + 1) * ncols], start=True, stop=False)
                        for ci, (off, ln) in enumerate(chunks):
                            ins = e.matmul(ps[:, base + off:base + off + ln], lhsT=Sbf[:, slots_in[ci], h, :],
                                           rhs=qtp[:, h, c_lo + off:c_lo + off + ln], start=False,
                                           stop=(ci == len(chunks) - 1))
                    return ins
                PA("pe", mmo, reads=[("vtok", par), "scm", ("qt", par)] + [("Sbf", s) for s in slots_in], writes=[("ps", bO)])
                return bO

            def onorm(par, par3, src_ap, src_keys, n4, ncols, c_lo, pool):
                PA("act", lambda e: e.activation(out=osq[:, 0:n4], in_=src_ap, func=AF.Square),
                   reads=src_keys, writes=["osq"])
                bk = newbank(pool)
                PA("pe", lambda e: e.matmul(psb(bk, n4), lhsT=onesb[:, :], rhs=osq[:, 0:n4], start=True, stop=True),
                   reads=["osq", "onesb"], writes=[("ps", bk)])
                PA("act", lambda e: e.activation(out=olnv[:, 0:n4], in_=psb(bk, n4), func=AF.Ln, bias=EPSB[:, 0:1], scale=1.0 / 128),
                   reads=[("ps", bk), "epsb"], writes=["olnv"])
                PA("act", lambda e: e.activation(out=orstd[:, 0:n4], in_=olnv[:, 0:n4], func=AF.Exp, scale=-0.5),
                   reads=["olnv"], writes=["orstd"])
                PA("dve", lambda e: e.tensor_tensor(out=t1[:, 0:n4], in0=src_ap, in1=orstd[:, 0:n4], op=ALU.mult),
                   reads=src_keys + ["orstd"], writes=["t1"])
                for h in range(4):
                    PA("dve", (lambda e, h=h: e.scalar_tensor_tensor(
                        out=mixed2[:, par3, h, c_lo:c_lo + ncols], in0=t1[:, h * ncols:(h + 1) * ncols], scalar=hgn(h),
                        in1=gate2[:, par, h, c_lo:c_lo + ncols], op0=ALU.mult, op1=ALU.mult)),
                        reads=["t1", ("gate", par), "vec", ("mixed", par3, h)], writes=[("mixed", par3, h)])

            def windows(X, Wd, outs, xkey):
                for g in range(4):
                    w = 2 << g
                    bufs = [wa, wb]
                    src = X[:, g, 0:Wd]
                    cur = None
                    sh, bi, lo = 1, 0, 0
                    while sh < w:
                        dst = bufs[bi]
                        prev = src if cur is None else cur
                        lo2 = lo + sh
                        PA("dve", (lambda e, dst=dst, prev=prev, lo2=lo2, sh=sh: e.tensor_tensor(
                            out=dst[:, lo2:Wd], in0=prev[:, lo2:Wd], in1=prev[:, lo2 - sh:Wd - sh], op=ALU.add)),
                            reads=[xkey, ("wbuf", 1 - bi)], writes=[("wbuf", bi)])
                        cur, lo, sh, bi = dst, lo2, sh * 2, 1 - bi
                    dst_ap, c_lo, ncol = outs[g]
                    PA("dve", (lambda e, cur=cur, dst_ap=dst_ap, c_lo=c_lo, ncol=ncol, w=w, src=src: e.scalar_tensor_tensor(
                        out=dst_ap, in0=cur[:, c_lo:c_lo + ncol], scalar=1.0 / w, in1=src[:, c_lo:c_lo + ncol],
                        op0=ALU.mult, op1=ALU.subtract)),
                        reads=[xkey, ("wbuf", 0), ("wbuf", 1)], writes=[("dd", g)])
                    yield g, cur

            def stage1(ti):
                tail = (ti == 16)
                par = ti % 2
                par3 = ti % 3
                pool = "s1"
                t0 = ti * 128
                tn = 80 if tail else 128
                ktp, qtp, Epp, gtp = kt2[:, par], qt2[:, par], Ep2[:, par], gate2[:, par]
                th, kf, gl, bb, qb = th2[par], kf2[par], gl2[par], bb2[par], qb2[par]
                hm = hm2[par]
                Em = th
                norm_apply(t0, tn, gain_off, m_nrstd2[:, par], ("nrstd", par),
                           (lambda k: hm[:, k, 0:tn]), (lambda k: [("hm", par)]))

                def ps4(bk):
                    return ps[:, bk * 512:bk * 512 + 4 * tn].rearrange("p (h t) -> p h t", h=4)

                def evac_f(bk):
                    PA("act", (lambda e: e.activation(out=th[:, :, 0:tn], in_=ps4(bk), func=AF.Tanh, scale=0.5)),
                       reads=[("ps", bk)], writes=[("th", par, h) for h in range(4)])
                    for h in range(4):
                        PA("dve", (lambda e, h=h: e.tensor_scalar(out=kf[:, h, 0:tn], in0=th[:, h, 0:tn], scalar1=nc1c(h), scalar2=c1c(h),
                                                                  op0=ALU.mult, op1=ALU.add)),
                           reads=[("th", par, h), "c1", "nc1"], writes=[("kf", par)])
                zproj(w_f, "f", tn, evac_f, pool, hm, ("hm", par))
                zproj(w_p, "pool", tn, lambda bk: PA(
                    "act", (lambda e: e.copy(out=U[:, :, 16:16 + tn], in_=ps4(bk))),
                    reads=[("ps", bk)], writes=["Ux"]), pool, hm, ("hm", par))
                zproj(w_q, "q", tn, lambda bk: PA(
                    "act", (lambda e: e.activation(out=qb[:, :, 0:tn], in_=ps4(bk), func=AF.Silu)),
                    reads=[("ps", bk)], writes=[("qb", par)]), pool, hm, ("hm", par))
                zproj(w_g, "g", tn, lambda bk: PA(
                    "act", (lambda e: e.activation(out=gtp[:, :, 0:tn], in_=ps4(bk), func=AF.Silu)),
                    reads=[("ps", bk)], writes=[("gate", par)]), pool, hm, ("hm", par))
                nrow = 16 if tail else 128
                bV = newbank(pool)

                def mmv(e):
                    ins = None
                    for k in range(KC):
                        ins = e.matmul(ps[0:nrow, bV * 512:(bV + 1) * 512], lhsT=hm[:, k, 0:nrow], rhs=w_i[:, k, :],
                                       start=(k == 0), stop=(k == KC - 1))
                    return ins
                PA("pe", mmv, reads=[("hm", par), ("W", MIXPG["i"])], writes=[("ps", bV)])
                PA("dve", lambda e: e.tensor_copy(out=vtok2[0:nrow, par, :], in_=ps[0:nrow, bV * 512:(bV + 1) * 512]),
                   reads=[("ps", bV)], writes=[("vtok", par)])
                PA("act", lambda e: e.activation(out=gl[:, :, 0:tn], in_=kf[:, :, 0:tn], func=AF.Ln, bias=ONEB[:, 0:1], scale=-1.0),
                   reads=[("kf", par), "oneb"], writes=[("gl", par)])
                for h in range(4):
                    rm = rmB[:, 0:80] if tail else rmA[:, 0:128]
                    PA("dve", (lambda e, h=h, rm=rm: e.tensor_tensor_scan(
                        out=bb[:, h, 0:tn], data0=rm, data1=gl[:, h, 0:tn],
                        initial=0.0, op0=ALU.mult, op1=ALU.add)), reads=[("gl", par), "consts"], writes=[("bb", par)])
                PA("act", lambda e: e.activation(out=Em[:, :, 0:tn], in_=bb[:, :, 0:tn], func=AF.Exp, scale=-1.0),
                   reads=[("bb", par)] + [("th", par, h) for h in range(4)], writes=[("Em", par)] + [("th", par, h) for h in range(4)])
                PA("act", lambda e: e.activation(out=Epp[:, :, 0:tn], in_=bb[:, :, 0:tn], func=AF.Exp),
                   reads=[("bb", par)], writes=[("Ep", par)])
                PA("dve", lambda e: e.tensor_tensor(out=ktp[:, :, 0:tn], in0=kf[:, :, 0:tn], in1=Em[:, :, 0:tn], op=ALU.mult),
                   reads=[("kf", par), ("Em", par)] + [("th", par, h) for h in range(4)], writes=[("kt", par)])
                PA("dve", lambda e: e.tensor_tensor(out=qtp[:, :, 0:tn], in0=qb[:, :, 0:tn], in1=Epp[:, :, 0:tn], op=ALU.mult),
                   reads=[("qb", par), ("Ep", par)], writes=[("qt", par)])
                if not tail:
                    outs = [(dd[:, g, 0:128], 16, 128) for g in range(4)]
                    for g, cur in windows(U, 144, outs, "Ux"):
                        if ti == 0 and g > 0:
                            w = 2 << g
                            PA("dve", (lambda e, cur=cur, w=w: e.tensor_tensor(
                                out=wfix[:, 0:w - 1], in0=cur[:, 16:16 + w - 1], in1=invc[:, 0:w - 1], op=ALU.mult)),
                                reads=[("wbuf", 0), ("wbuf", 1), "consts"], writes=["wfix"])
                            PA("dve", (lambda e, g=g, w=w: e.tensor_tensor(
                                out=dd[:, g, 0:w - 1], in0=wfix[:, 0:w - 1], in1=U[:, g, 16:16 + w - 1], op=ALU.subtract)),
                                reads=["wfix", "Ux", ("dd", g)], writes=[("dd", g)])
                        elif ti == 0 and g == 0:
                            PA("dve", lambda e: e.memset(dd[:, 0, 0:1], 0.0), reads=[("dd", 0)], writes=[("dd", 0)])
                else:
                    outs = [(dd[:, g, 0:16], 16, 16) for g in range(4)]
                    for _ in windows(U, 32, outs, "Ux"):
                        pass
                    PA("sp", lambda e: e.dma_start(out=plp_d[l].rearrange("g p r -> p g r"), in_=U[:, :, 17:32]),
                       reads=["Ux"], dma=True, arena=True)
                    PA("sp", lambda e: e.dma_start(out=ustage.rearrange("p g i r -> p g (i r)")[:, :, 0:240],
                                                   in_=spl_d[l].rearrange("g p n -> p g n")),
                       writes=["ustage"], dma=True, arena=True)
                    PA("dve", lambda e: e.tensor_copy(out=uext[:, :, :, 0:15], in_=ustage[:, :, :, 0:15]),
                       reads=["ustage"], writes=["uext"])
                    PA("dve", lambda e: e.tensor_copy(out=uext[:, :, :, 15:19],
                                                      in_=U[:, :, 32:96].rearrange("p g (i r) -> p g i r", i=16)),
                       reads=["Ux", "uext"], writes=["uext"])
                    PA("dve", lambda e: e.tensor_copy(out=ustage[:, :, :, 0:15], in_=uext[:, :, :, 4:19]),
                       reads=["uext", "ustage"], writes=["ustage"])
                    PA("sp", lambda e: e.dma_start(out=pls_d[l].rearrange("g p n -> p g n"),
                                                   in_=ustage.rearrange("p g i r -> p g (i r)")[:, :, 0:240]),
                       reads=["ustage"], dma=True, arena=True)
                    uflat = uext.rearrange("p g i r -> p g (i r)")
                    for g in range(4):
                        w = 2 << g
                        bufs = [wa, wb]
                        for half in range(2):
                            src = uflat[:, g, half * 152:(half + 1) * 152]
                            cur = None
                            sh, bi, lo = 1, 0, 0
                            while sh < w:
                                dst = bufs[bi]
                                prev = src if cur is None else cur
                                lo2 = lo + sh
                                PA("dve", (lambda e, dst=dst, prev=prev, lo2=lo2, sh=sh: e.tensor_tensor(
                                    out=dst[:, lo2:152], in0=prev[:, lo2:152], in1=prev[:, lo2 - sh:152 - sh], op=ALU.add)),
                                    reads=["uext", ("wbuf", 1 - bi)], writes=[("wbuf", bi)])
                                cur, lo, sh, bi = dst, lo2, sh * 2, 1 - bi
                            dst_ap = dd[:, g, 16 + half * 32:16 + half * 32 + 32].rearrange("p (i r) -> p i r", i=8)
                            curv = cur[:, 0:152].rearrange("p (i r) -> p i r", i=8)[:, :, 15:19]
                            srcv = src.rearrange("p (i r) -> p i r", i=8)[:, :, 15:19]
                            PA("dve", (lambda e, dst_ap=dst_ap, curv=curv, srcv=srcv, w=w: e.scalar_tensor_tensor(
                                out=dst_ap, in0=curv, scalar=1.0 / w, in1=srcv, op0=ALU.mult, op1=ALU.subtract)),
                                reads=["uext", ("wbuf", 0), ("wbuf", 1), ("dd", g)], writes=[("dd", g)])
                for g in range(4):
                    bk = newbank(pool)
                    PA("pe", (lambda e, g=g, bk=bk: e.matmul(psb(bk, tn), lhsT=pw[:, g, :], rhs=dd[:, g, 0:tn], start=True, stop=True)),
                       reads=[("dd", g), ("poolw", l % 2)], writes=[("ps", bk)])
                    PA("act", (lambda e, g=g, bk=bk: e.activation(out=mixed2[:, par3, 4 + g, 0:tn], in_=psb(bk, tn), func=AF.Copy, scale=psc(g))),
                       reads=[("ps", bk), "vec"], writes=[("mixed", par3, 4 + g)])
                if not tail:
                    PA("dve", lambda e: e.tensor_copy(out=U[:, :, 0:16], in_=U[:, :, 128:144]), reads=["Ux"], writes=["Uhalo", "Ux"])
                    tn2 = 80 if ti + 1 == 16 else 128
                    norm_stats(t0 + 128, tn2, m_nsq, m_nln, m_nrstd2[:, 1 - par], ("nrstd", 1 - par), pool)

            def stage2(ti):
                tail = (ti == 16)
                par = ti % 2
                par3 = ti % 3
                pool = "t2" if tail else "s2"
                if not tail:
                    bO = hgrn_part(par, 0, 128, [(0, 64), (64, 64)], pool)
                    onorm(par, par3, psb(bO), [("ps", bO)], 512, 128, 0, pool)
                else:
                    bO = hgrn_part(par, 0, 16, [(0, 16)], pool)
                    onorm(par, par3, psb(bO, 64), [("ps", bO)], 64, 16, 0, pool)
                    PA("sp", lambda e: e.dma_start(out=hgp_d[l].rearrange("h d v -> d h v"), in_=Sst[:, :, :]),
                       reads=["S"], dma=True, arena=True)
                    sample_part(l, par, par3)

            def stage3(ti):
                tail = (ti == 16)
                par3 = ti % 3
                pool = "s3"
                t0 = ti * 128
                tn = 80 if tail else 128
                for half in range(2):
                    bk = newbank(pool)
                    for j in range(4):
                        m = half * 4 + j

                        def mmw(e, m=m, j=j, bk=bk):
                            ins = None
                            for k in range(KC):
                                wv = w_oA if k < 4 else w_oB
                                ins = e.matmul(ps[:, bk * 512 + j * tn:bk * 512 + (j + 1) * tn], lhsT=wv[:, k % 4, m * 128:(m + 1) * 128],
                                               rhs=mixed2[:, par3, k, 0:tn], start=(k == 0), stop=(k == KC - 1))
                            return ins
                        PA("pe", mmw, reads=[("mixed", par3, k) for k in range(KC)] + [("W", MIXPG["oA"]), ("W", MIXPG["oB"])]
                           + ([("ps", bk)] if j else []), writes=[("ps", bk)])
                    ms = range(half * 4, half * 4 + 4)
                    PA("dve", (lambda e, half=half, bk=bk: e.tensor_tensor(
                        out=xres[:, half * 4:half * 4 + 4, t0:t0 + tn],
                        in0=ps[:, bk * 512:bk * 512 + 4 * tn].rearrange("p (m t) -> p m t", m=4),
                        in1=xres[:, half * 4:half * 4 + 4, t0:t0 + tn], op=ALU.add)),
                        reads=[("ps", bk)] + xkeys(ms, t0, t0 + tn), writes=xkeys(ms, t0, t0 + tn))

            NT = 17
            allrec = Rec()
            sink[0] = allrec
            norm_stats(0, 128, m_nsq, m_nln, m_nrstd2[:, 0], ("nrstd", 0), "s1")
            for ti in range(NT):
                stage1(ti)
                stage2(ti)
                stage3(ti)
            sink[0] = P
            gsched(allrec.items)

        def sample_part(l, par, par3):
            w_i = page(MIXPG["i"]).rearrange("p (k n) -> p k n", k=8)
            ktp, qtp, Epp, gtp = kt2[:, par], qt2[:, par], Ep2[:, par], gate2[:, par]
            vtp = vtok2[:, par]
            pool = "t2"
            hm = hm2[par]
            cs = slice(16, 80)
            bT = newbank(pool)
            psT = ps[0:64, bT * 512:(bT + 1) * 512].bitcast(BF16)

            def tr(e):
                ins = None
                for h in range(4):
                    ins = e.transpose(out=psT[:, h * 128:(h + 1) * 128], in_=ktp[:, h, cs], identity=identb[:, :])
                return ins
            PA("pe", tr, reads=[("kt", par), "consts"], writes=[("ps", bT)])
            PA("act", lambda e: e.copy(out=ktok[0:64, :], in_=psT[:, 0:512]), reads=[("ps", bT)], writes=["ktok"])
            bV = newbank(pool)

            def mmv(e):
                ins = None
                for k in range(KC):
                    ins = e.matmul(ps[0:64, bV * 512:(bV + 1) * 512], lhsT=hm[:, k, cs], rhs=w_i[:, k, :],
                                   start=(k == 0), stop=(k == KC - 1))
                return ins
            PA("pe", mmv, reads=[("hm", par), ("W", MIXPG["i"])], writes=[("ps", bV)])
            PA("dve", lambda e: e.tensor_copy(out=vtp[0:64, :], in_=ps[0:64, bV * 512:(bV + 1) * 512]),
               reads=[("ps", bV)], writes=[("vtok", par)])
            bS = newbank(pool)

            def mms(e):
                ins = None
                for h in range(4):
                    ins = e.matmul(ps[0:64, bS * 512 + h * 64:bS * 512 + (h + 1) * 64], lhsT=ktp[:, h, cs], rhs=qtp[:, h, cs],
                                   start=True, stop=True)
                return ins
            PA("pe", mms, reads=[("kt", par), ("qt", par)], writes=[("ps", bS)])
            PA("dve", lambda e: e.tensor_tensor(
                out=scm[0:64, 0:256].rearrange("p (h t) -> p h t", h=4),
                in0=ps[0:64, bS * 512:bS * 512 + 256].rearrange("p (h t) -> p h t", h=4),
                in1=smaskb[:, :, :], op=ALU.mult), reads=[("ps", bS), "consts"], writes=["scm"])
            bOi = newbank(pool)
            bOx = newbank(pool)
            pinned.update((bOi, bOx))

            def mmoi(e):
                ins = None
                for h in range(4):
                    ins = e.matmul(ps[:, bOi * 512 + h * 64:bOi * 512 + (h + 1) * 64], lhsT=vtp[0:64, h * 128:(h + 1) * 128],
                                   rhs=scm[0:64, h * 64:(h + 1) * 64], start=True, stop=True)
                return ins
            PA("pe", mmoi, reads=[("vtok", par), "scm"], writes=[("ps", bOi)])
            for b in range(8):
                hb = b % 2
                sl = slice(2 * hb, 2 * hb + 2)
                src = shg_d[l, 2 * b:2 * b + 2].rearrange("i h d v -> d (i h) v")
                PA("sp", (lambda e, src=src, sl=sl: e.dma_start(out=Ss[:, sl].rearrange("p i h v -> p (i h) v"), in_=src)),
                   writes=[("Ss", hb)] + ODDKEYS, dma=True, arena=True)
                PA("act", (lambda e, sl=sl: e.copy(out=Ssb[:, sl].rearrange("p i h v -> p (i h v)"),
                                                 in_=Ss[:, sl].rearrange("p i h v -> p (i h v)"))),
                   reads=[("Ss", hb)], writes=[("Ssb", hb)] + ODDKEYS)

                def mmx(e, b=b, hb=hb):
                    ins = None
                    for ii in range(2):
                        i = 2 * b + ii
                        for h in range(4):
                            c = bOx * 512 + h * 64 + 4 * i
                            ins = e.matmul(ps[:, c:c + 4], lhsT=Ssb[:, 2 * hb + ii, h, :], rhs=qtp[:, h, 16 + 4 * i:16 + 4 * i + 4],
                                           start=True, stop=True, skip_group_check=True)
                    return ins
                PA("pe", mmx, reads=[("Ssb", hb), ("qt", par), ("ps", bOx)], writes=[("ps", bOx)])
                for ii in range(2):
                    i = 2 * b + ii
                    si = 2 * hb + ii
                    PA("dve", (lambda e, i=i, si=si: e.tensor_scalar(out=km[0:64, si, :], in0=ktok[0:64, :], scalar1=smcol[:, i:i + 1],
                                                                     scalar2=None, op0=ALU.mult)),
                       reads=["ktok", "consts"], writes=[("km", si), ("hm", 1 - par)])
                    bA = newbank(pool)

                    def mma(e, si=si, bA=bA):
                        ins = None
                        for h in range(4):
                            ins = e.matmul(ps[:, bA * 512 + h * 128:bA * 512 + (h + 1) * 128], lhsT=km[0:64, si, h * 128:(h + 1) * 128],
                                           rhs=vtp[0:64, h * 128:(h + 1) * 128], start=True, stop=True)
                        return ins
                    PA("pe", mma, reads=[("km", si), ("vtok", par)], writes=[("ps", bA)])
                    PA("dve", (lambda e, si=si, bA=bA: e.tensor_tensor(
                        out=tmpS.rearrange("p h v -> p (h v)"), in0=psb(bA), in1=Ss[:, si, :, :].rearrange("p h v -> p (h v)"), op=ALU.add)),
                        reads=[("ps", bA), ("Ss", hb)], writes=["tmpS"])
                    lastcol = 16 + 4 * i + 3
                    dec = Epp[:, :, lastcol:lastcol + 1].broadcast_to([128, 4, 128])
                    PA("dve", (lambda e, si=si, dec=dec: e.tensor_tensor(out=Ss[:, si, :, :], in0=tmpS[:, :, :], in1=dec, op=ALU.mult)),
                       reads=["tmpS", ("Ep", par), ("Ssb", hb), ("Ss", hb)], writes=[("Ss", hb)])
                dst = hgs_d[l, 2 * b:2 * b + 2].rearrange("i h d v -> d (i h) v")
                PA("sp", (lambda e, dst=dst, sl=sl: e.dma_start(out=dst, in_=Ss[:, sl].rearrange("p i h v -> p (i h) v"))),
                   reads=[("Ss", hb)], dma=True, arena=True)
            pinned.clear()
            PA("act", lambda e: e.copy(out=oint[:, 0:256], in_=psb(bOx, 256)), reads=[("ps", bOx)], writes=["oint"])
            PA("dve", lambda e: e.tensor_tensor(out=osum[:, 0:256], in0=psb(bOi, 256), in1=oint[:, 0:256], op=ALU.add),
               reads=[("ps", bOi), "oint"], writes=["osum", ("wbuf", 0), ("wbuf", 1)])
            hgn = lambda h: vcol(V_HGN + l * 4 + h)
            PA("act", lambda e: e.activation(out=osq[:, 0:256], in_=osum[:, 0:256], func=AF.Square), reads=["osum"], writes=["osq"])
            bk = newbank(pool)
            PA("pe", lambda e: e.matmul(psb(bk, 256), lhsT=onesb[:, :], rhs=osq[:, 0:256], start=True, stop=True),
               reads=["osq", "onesb"], writes=[("ps", bk)])
            PA("act", lambda e: e.activation(out=olnv[:, 0:256], in_=psb(bk, 256), func=AF.Ln, bias=EPSB[:, 0:1], scale=1.0 / 128),
               reads=[("ps", bk), "epsb"], writes=["olnv"])
            PA("act", lambda e: e.activation(out=orstd[:, 0:256], in_=olnv[:, 0:256], func=AF.Exp, scale=-0.5),
               reads=["olnv"], writes=["orstd"])
            PA("dve", lambda e: e.tensor_tensor(out=t1[:, 0:256], in0=osum[:, 0:256], in1=orstd[:, 0:256], op=ALU.mult),
               reads=["osum", "orstd"], writes=["t1"])
            for h in range(4):
                PA("dve", (lambda e, h=h: e.scalar_tensor_tensor(
                    out=mixed2[:, par3, h, 16:80], in0=t1[:, h * 64:(h + 1) * 64], scalar=hgn(h), in1=gtp[:, h, 16:80],
                    op0=ALU.mult, op1=ALU.mult)), reads=["t1", ("gate", par), "vec", ("mixed", par3, h)], writes=[("mixed", par3, h)])

        oneb_t = sb("oneb", [128, 2], F32)
        ONEB = oneb_t
        P.add("dve", lambda e: e.memset(oneb_t[:, :], 1.0), writes=["oneb"])

        dma_ffn_group(0, 0, 0)
        for l in range(L):
            ffn(l, 0)
            P.barrier()
            mixing(l)
            dma_ffn_group(l, 1, 0)
            P.barrier()
            ffn(l, 1)
        P.barrier()
        for n, (t0, tn) in enumerate(FT):
            norm_tile(t0, tn, V_NFIN, f_nsq, f_nln, f_nrstd,
                      (lambda k, tn=tn: f_ytmp[:, k, 0:tn]), (lambda k: [("ytmp", k)]))
            yv = yT.rearrange("(k p) t -> p k t", p=128)
            P.add("sp", (lambda e, t0=t0, tn=tn: e.dma_start(out=yv[:, :, t0:t0 + tn], in_=f_ytmp[:, :, 0:tn])),
                  reads=[("ytmp", k) for k in range(KC)], dma=True, arena=True)

        P.finalize()
        with nc.Block() as block:
            P.emit(nc, block, esem, qsem)
    return nc


def _consts():
    c = np.zeros((128, NCST), np.float32)
    c[:, C_ID:C_ID + 128] = np.eye(128, dtype=np.float32)
    s = np.arange(128)[:, None]
    t = np.arange(128)[None, :]
    c[:, C_CM:C_CM + 128] = ((s <= t) & (s // 64 == t // 64)).astype(np.float32)
    s = np.arange(64)[:, None]
    t = np.arange(64)[None, :]
    c[0:64, C_SM:C_SM + 64] = ((s <= t) & (s // 4 == t // 4)).astype(np.float32)
    c[0:64, C_SMC:C_SMC + 16] = (np.arange(64)[:, None] // 4 == np.arange(16)[None, :]).astype(np.float32)
    ra = np.ones(512, np.float32)
    ra[0::64] = 0.0
    c[:, C_RMA:C_RMA + 512] = ra[None, :]
    rb = np.ones(80, np.float32)
    rb[0] = 0.0
    rb[16::4] = 0.0
    c[:, C_RMB:C_RMB + 320] = np.tile(rb, 4)[None, :]
    c[:, C_INV:C_INV + 16] = (1.0 / np.arange(1, 17, dtype=np.float32))[None, :]
    return c


def _vecs(norm_ffn1, norm_mix, norm_ffn2, norm_final, hg_norm, pool_scale, lb_logits):
    v = np.zeros((128, NV), np.float32)
    for l in range(4):
        v[:, V_NF1 + l * 8:V_NF1 + l * 8 + 8] = norm_ffn1[l].reshape(8, 128).T
        v[:, V_NMX + l * 8:V_NMX + l * 8 + 8] = norm_mix[l].reshape(8, 128).T
        v[:, V_NF2 + l * 8:V_NF2 + l * 8 + 8] = norm_ffn2[l].reshape(8, 128).T
        v[:, V_HGN + l * 4:V_HGN + l * 4 + 4] = hg_norm[l].T
        v[:, V_PSC + l * 4:V_PSC + l * 4 + 4] = pool_scale[l].reshape(4, 128).T
    v[:, V_NFIN:V_NFIN + 8] = norm_final.reshape(8, 128).T
    v[:, V_LBL:V_LBL + 16] = lb_logits.reshape(4, 4, 128).transpose(2, 1, 0).reshape(128, 16)
    return v


_NC_CACHE = {}


def make_in_maps(inputs, cores):
    f = lambda a: np.ascontiguousarray(np.asarray(a, dtype=np.float32))
    x_prompt, x_sample, meta = f(inputs["x_prompt"]), f(inputs["x_sample"]), f(inputs["meta"])
    state_hgrn, state_pool = f(inputs["state_hgrn"]), f(inputs["state_pool"])
    cst = _consts()
    vecs = _vecs(f(inputs["norm_ffn1"]), f(inputs["norm_mix"]), f(inputs["norm_ffn2"]), f(inputs["norm_final"]),
                 f(inputs["hg_norm"]), f(inputs["pool_scale"]), f(inputs["lb_logits"]))
    shared = {k: f(inputs[k]) for k in ("w_ffn1_in", "w_ffn1_out", "w_ffn2_in", "w_ffn2_out", "w_in", "w_out", "pool_w")}
    maps = []
    for c in cores:
        xs = x_sample[NSEQ * c:NSEQ * (c + 1)].reshape(NSM, D)
        xall = np.concatenate([meta, x_prompt[c], xs], axis=0)
        sp = state_pool[:, NSEQ * c:NSEQ * (c + 1)]
        spT = np.ascontiguousarray(sp.transpose(0, 3, 1, 2).reshape(4, 4, 128, NSEQ * 15))
        m = dict(shared)
        m.update({
            "xT": np.ascontiguousarray(xall.T),
            "vecs": vecs, "cst": cst,
            "state_hgrn": np.ascontiguousarray(state_hgrn[:, NSEQ * c:NSEQ * (c + 1)]),
            "state_poolT": spT,
        })
        maps.append(m)
    return maps


def assemble(results, ncores):
    y_prompt = np.zeros((ncores, 2048, D), np.float32)
    y_sample = np.zeros((ncores * NSEQ, 4, D), np.float32)
    hg_p = np.zeros((4, ncores, NH, 128, 128), np.float32)
    pool_p = np.zeros((4, ncores, 15, 512), np.float32)
    hg_s = np.zeros((4, ncores * NSEQ, NH, 128, 128), np.float32)
    pool_s = np.zeros((4, ncores * NSEQ, 15, 512), np.float32)
    for c, r in enumerate(results):
        y = np.asarray(r["yT"]).T
        y_prompt[c] = y[16:NPR]
        y_sample[NSEQ * c:NSEQ * (c + 1)] = y[NPR:].reshape(NSEQ, 4, D)
        hg_p[:, c] = np.asarray(r["hgp"])
        pool_p[:, c] = np.asarray(r["poolpT"]).reshape(4, 512, 15).transpose(0, 2, 1)
        hg_s[:, NSEQ * c:NSEQ * (c + 1)] = np.asarray(r["hgs"])
        pool_s[:, NSEQ * c:NSEQ * (c + 1)] = np.asarray(r["poolsT"]).reshape(4, 512, NSEQ, 15).transpose(0, 2, 3, 1)
    return (y_prompt, y_sample, hg_p, pool_p, hg_s, pool_s)


def kernel(**inputs):
    if "nc" not in _NC_CACHE:
        _NC_CACHE["nc"] = build_program(4)
    nc = _NC_CACHE["nc"]
    cores = list(range(NCORES))
    in_maps = make_in_maps(inputs, cores)
    res = run_bass_kernel_spmd(nc, in_maps, core_ids=cores)
    return assemble(res.results, NCORES)
```

```python
import numpy as np
import concourse.bass as bass
import concourse.mybir as mybir
from concourse.bass_utils import run_bass_kernel_spmd

F32 = mybir.dt.float32
BF16 = mybir.dt.bfloat16
U8 = mybir.dt.uint8
AF = mybir.ActivationFunctionType
ALU = mybir.AluOpType
AX = mybir.AxisListType

NCORES = 8
D = 1024
KC = 8
NPR = 2064
NSM = 64
T = NPR + NSM
NSEQ = 16
DFF = 2816
NH = 4
EPS = 1e-6
FT = [(0, 448), (448, 448), (896, 448), (1344, 448), (1792, 336)]
GROUPS = [(0, 4), (4, 4), (8, 4), (12, 4), (16, 4), (20, 2)]
NPAGE = 7
GS_LAT = 200.0
GS_PRIO_WIN = 0.0
GS_TRIES = [(0.0, 0)] + [(w, s) for s in range(1, 9) for w in (0.0, 120.0)]
PAGE = 8192
V_NF1, V_NMX, V_NF2, V_NFIN, V_HGN, V_PSC, V_LBL, NV = 0, 32, 64, 96, 104, 120, 136, 152
C_ID, C_CM, C_SM, C_SMC, C_RMA, C_RMB, C_INV, NCST = 0, 128, 256, 320, 336, 848, 1168, 1184


class Op:
    __slots__ = ("eng", "fn", "deps", "dma", "sig", "count", "waits", "dj")

    def __init__(self, eng, fn, deps, dma):
        self.eng, self.fn, self.deps, self.dma = eng, fn, deps, dma
        self.sig = False
        self.count = 0
        self.waits = []
        self.dj = -1


class Prog:
    KQ = 8

    def __init__(self):
        self.ops = []
        self.lastw = {}
        self.readers = {}
        self.last_on = {}
        self.pending_dma = []

    def add(self, eng, fn, reads=(), writes=(), dma=False, extra=(), arena=False):
        idx = len(self.ops)
        if dma and arena:
            reads = list(reads) + ["phase"]
        deps = {}
        for r in reads:
            w = self.lastw.get(r)
            if w is not None:
                deps[w] = True
        for r in writes:
            w = self.lastw.get(r)
            if w is not None:
                deps.setdefault(w, False)
            for rd in self.readers.get(r, ()):
                if rd != idx:
                    deps.setdefault(rd, False)
        for e in extra:
            deps[e] = True
        for r in reads:
            self.readers.setdefault(r, []).append(idx)
        for r in writes:
            self.lastw[r] = idx
            self.readers[r] = []
        self.ops.append(Op(eng, fn, deps, dma))
        self.last_on[eng] = idx
        if dma and arena:
            self.pending_dma.append(idx)
        return idx

    def barrier(self):
        lasts = [v for v in self.last_on.values()]
        dmas = list(self.pending_dma)
        self.pending_dma = []
        for eng in ("pe", "act", "dve"):
            self.add(eng, lambda e: e.nop(), extra=lasts + dmas, writes=(["phase"] if eng == "dve" else []))

    def finalize(self):
        ops = self.ops
        for op in ops:
            per = {}
            waits = []
            for d, raw in op.deps.items():
                od = ops[d]
                if od.dma:
                    waits.append(d)
                    continue
                if od.eng == op.eng and not op.dma and op.eng == "pe":
                    continue
                if d > per.get(od.eng, -1):
                    per[od.eng] = d
            waits.extend(per.values())
            op.waits = waits
            for d in waits:
                ops[d].sig = True
        cnt = {}
        dj = {}
        for op in ops:
            if op.dma:
                op.dj = dj.get(op.eng, 0)
                dj[op.eng] = op.dj + 1
            elif op.sig:
                cnt[op.eng] = cnt.get(op.eng, 0) + 1
                op.count = cnt[op.eng]

    def emit(self, nc, block, esem, qsem):
        ops = self.ops
        KQ = self.KQ

        def token(d):
            od = ops[d]
            if od.dma:
                return qsem[od.eng][od.dj % KQ], 16 * (od.dj // KQ + 1)
            return esem[od.eng], od.count

        def run(engname):
            def body(e):
                waited = {}
                ndma = 0
                for op in ops:
                    if op.eng != engname:
                        continue
                    for d in op.waits:
                        sem, val = token(d)
                        if waited.get(id(sem), 0) >= val:
                            continue
                        e.wait_ge(sem, val)
                        waited[id(sem)] = val
                    if op.dma:
                        j = op.dj
                        sem = qsem[engname][j % KQ]
                        if j >= KQ:
                            val = 16 * (j // KQ)
                            if waited.get(id(sem), 0) < val:
                                e.wait_ge(sem, val)
                                waited[id(sem)] = val
                        ins = op.fn(e)
                        ins.then_inc(sem, 16)
                        ndma = j + 1
                    else:
                        ins = op.fn(e)
                        if op.sig:
                            ins.then_inc(esem[engname], 1)
                for s in range(min(ndma, KQ)):
                    n_on = (ndma - 1 - s) // KQ + 1
                    val = 16 * n_on
                    sem = qsem[engname][s]
                    if waited.get(id(sem), 0) < val:
                        e.wait_ge(sem, val)
            return body

        block.tensor(run("pe"))
        block.scalar(run("act"))
        block.vector(run("dve"))
        block.gpsimd(run("pool"))
        block.sync(run("sp"))


def build_program(L=4):
    nc = bass.Bass("TRN2", target_bir_lowering=False)
    P = Prog()

    def dram(name, shape, kind):
        return nc.dram_tensor(name, list(shape), F32, kind=kind).ap()

    xT = dram("xT", [D, T], "ExternalInput")
    vecs_d = dram("vecs", [128, NV], "ExternalInput")
    cst_d = dram("cst", [128, NCST], "ExternalInput")
    wf_in = [dram("w_ffn1_in", [4, D, 2 * DFF], "ExternalInput"), dram("w_ffn2_in", [4, D, 2 * DFF], "ExternalInput")]
    wf_out = [dram("w_ffn1_out", [4, DFF, D], "ExternalInput"), dram("w_ffn2_out", [4, DFF, D], "ExternalInput")]
    w_in_d = dram("w_in", [4, D, 2560], "ExternalInput")
    w_out_d = dram("w_out", [4, D, D], "ExternalInput")
    poolw_d = dram("pool_w", [4, 4, 128, 128], "ExternalInput")
    shg_d = dram("state_hgrn", [4, NSEQ, NH, 128, 128], "ExternalInput")
    spl_d = dram("state_poolT", [4, 4, 128, NSEQ * 15], "ExternalInput")
    yT = dram("yT", [D, T], "ExternalOutput")
    hgp_d = dram("hgp", [4, NH, 128, 128], "ExternalOutput")
    plp_d = dram("poolpT", [4, 4, 128, 15], "ExternalOutput")
    hgs_d = dram("hgs", [4, NSEQ, NH, 128, 128], "ExternalOutput")
    pls_d = dram("poolsT", [4, 4, 128, NSEQ * 15], "ExternalOutput")

    ARENA = 80448
    import contextlib
    with contextlib.ExitStack() as es:
        def sb(name, shape, dt):
            return es.enter_context(nc.sbuf_tensor(name, list(shape), dt))

        xres = sb("xres", [128, KC, T], F32)
        vec = sb("vec", [128, NV], F32)
        lbw = sb("lbw", [128, 64], F32)
        nc1t = sb("nc1t", [128, 16], F32)
        lbs = sb("lbs", [128, 8], F32)
        identb = sb("identb", [128, 128], BF16)
        onesb = sb("onesb", [128, 128], BF16)
        zerob = sb("zerob", [128, 128], BF16)
        cmaskb = sb("cmaskb", [128, 4, 128], BF16)
        smaskb = sb("smaskb", [64, 4, 64], BF16)
        smcol = sb("smcol", [64, 16], F32)
        rmA = sb("rmA", [128, 128], F32)
        rmB = sb("rmB", [128, 80], F32)
        invc = sb("invc", [128, 16], F32)
        poolw = sb("poolw", [128, 2, 4, 128], BF16)
        wpool = sb("wpool", [128, NPAGE * PAGE], U8)
        arena = sb("arena", [128, ARENA], U8)
        ps = es.enter_context(nc.psum_tensor("ps", [128, 8 * 512], F32))
        esem = {k: es.enter_context(nc.semaphore("e_" + k)) for k in ("pe", "act", "dve", "pool", "sp")}
        qsem = {k: [es.enter_context(nc.semaphore("q_%s%d" % (k, i))) for i in range(Prog.KQ)]
                for k in ("pool", "sp", "act")}

        class Carver:
            def __init__(self):
                self.off = 0

            def get(self, nbytes, dt, pattern=None, parts=128, **kw):
                nbytes = (nbytes + 63) // 64 * 64
                assert self.off + nbytes <= ARENA, ("arena overflow", self.off + nbytes)
                v = arena[0:parts, self.off:self.off + nbytes].bitcast(dt)
                self.off += nbytes
                if pattern:
                    v = v.rearrange(pattern, **kw)
                return v

        cf = Carver()
        h_full = cf.get(KC * T * 2, BF16, "p (k t) -> p k t", k=KC)
        aring = cf.get(3 * 4 * 448 * 2, BF16, "p (s j t) -> p s j t", s=3, j=4)
        sgt = cf.get(2 * 448 * 2, BF16, "p (s t) -> p s t", s=2)
        f_nsq = cf.get(KC * 448 * 2, BF16, "p (k t) -> p k t", k=KC)
        f_nln = cf.get(448 * 4, F32)
        f_nrstd = cf.get(448 * 4, F32)
        cstage = cf.get(NCST * 4, F32)
        f_ytmp = arena[:, 0:KC * 448 * 4].bitcast(F32).rearrange("p (k t) -> p k t", k=KC)
        cm = Carver()
        m_nsq = cm.get(KC * 128 * 2, BF16, "p (k t) -> p k t", k=KC)
        m_nln = cm.get(128 * 4, F32)
        m_nrstd2 = cm.get(2 * 128 * 4, F32, "p (s t) -> p s t", s=2)
        hm_0 = cm.get(KC * 128 * 2, BF16, "p (k t) -> p k t", k=KC)
        qb_0 = cm.get(512 * 2, BF16, "p (h t) -> p h t", h=4)
        qt2 = cm.get(2 * 512 * 2, BF16, "p (s h t) -> p s h t", s=2, h=4)
        th_0 = cm.get(512 * 4, F32, "p (h t) -> p h t", h=4)
        kf_0 = cm.get(512 * 4, F32, "p (h t) -> p h t", h=4)
        gl_0 = cm.get(512 * 4, F32, "p (h t) -> p h t", h=4)
        bb_0 = cm.get(512 * 4, F32, "p (h t) -> p h t", h=4)
        Ep2 = cm.get(2 * 512 * 4, F32, "p (s h t) -> p s h t", s=2, h=4)
        kt2 = cm.get(2 * 512 * 2, BF16, "p (s h t) -> p s h t", s=2, h=4)
        gate2 = cm.get(2 * 512 * 2, BF16, "p (s h t) -> p s h t", s=2, h=4)
        U = cm.get(4 * 144 * 4, F32, "p (g t) -> p g t", g=4)
        _wab_off = cm.off
        wa = cm.get(160 * 4, F32)
        wb = cm.get(160 * 4, F32)
        osum = arena[:, _wab_off:_wab_off + 1024].bitcast(F32)
        wfix = cm.get(16 * 4, F32)
        dd = cm.get(512 * 2, BF16, "p (g t) -> p g t", g=4)
        vtok2 = cm.get(2 * 512 * 2, BF16, "p (s c) -> p s c", s=2)
        ktok = cm.get(512 * 2, BF16)
        scm = cm.get(512 * 2, BF16)
        osq = cm.get(512 * 2, BF16)
        olnv = cm.get(512 * 4, F32)
        orstd = cm.get(512 * 4, F32)
        t1 = cm.get(512 * 2, BF16)
        mixed2 = cm.get(3 * KC * 128 * 2, BF16, "p (s k t) -> p s k t", s=3, k=KC)
        Sst = cm.get(512 * 4, F32, "p (h v) -> p h v", h=4)
        tmpS = cm.get(512 * 4, F32, "p (h v) -> p h v", h=4)
        Sbf = cm.get(4 * 512 * 2, BF16, "p (s h v) -> p s h v", s=4, h=4)
        _ss_off = cm.off
        Ss = cm.get(4 * 512 * 4, F32, "p (i h v) -> p i h v", i=4, h=4)
        Ssb = cm.get(4 * 512 * 2, BF16, "p (i h v) -> p i h v", i=4, h=4)
        def _alias(off, nbytes, dt):
            return arena[:, off:off + nbytes].bitcast(dt).rearrange("p (h t) -> p h t", h=4)
        th_1 = _alias(_ss_off, 2048, F32)
        kf_1 = _alias(_ss_off + 2048, 2048, F32)
        gl_1 = _alias(_ss_off + 4096, 2048, F32)
        bb_1 = _alias(_ss_off + 6144, 2048, F32)
        qb_1 = _alias(_ss_off + 8192, 1024, BF16)
        th2, kf2, gl2, bb2, qb2 = (th_0, th_1), (kf_0, kf_1), (gl_0, gl_1), (bb_0, bb_1), (qb_0, qb_1)
        ODDKEYS = [("th", 1, h) for h in range(4)] + [("kf", 1), ("gl", 1), ("bb", 1), ("Em", 1), ("qb", 1)]
        _km_off = cm.off
        km = cm.get(4 * 512 * 2, BF16, "p (i c) -> p i c", i=4)
        hm_1 = arena[:, _km_off:_km_off + KC * 128 * 2].bitcast(BF16).rearrange("p (k t) -> p k t", k=KC)
        hm2 = (hm_0, hm_1)
        uext = cm.get(4 * 16 * 19 * 4, F32, "p (g i r) -> p g i r", g=4, i=16)
        ustage = cm.get(4 * 240 * 4, F32, "p (g i r) -> p g i r", g=4, i=16)
        oint = cm.get(256 * 4, F32)

        def page(pg, dt=BF16):
            return wpool[:, pg * PAGE:(pg + 1) * PAGE].bitcast(dt)

        bank_ctr = [0]
        pinned = set()

        pool_ctr = {"s1": 0, "s2": 0, "s3": 0, "t2": 0}
        POOLS = {"s1": (0, 4), "s2": (4, 2), "s3": (6, 1), "t2": (0, 6)}
        FILL_BANK = 7

        def newbank(pool="all"):
            if pool == "all":
                while True:
                    b = bank_ctr[0] % 8
                    bank_ctr[0] += 1
                    if b not in pinned:
                        return b
            base, cnt = POOLS[pool]
            while True:
                b = base + pool_ctr[pool] % cnt
                pool_ctr[pool] += 1
                if b not in pinned:
                    return b

        def psb(b, n=512, parts=128):
            return ps[0:parts, b * 512:b * 512 + n]

        def xkeys(kcs, t0, t1):
            return [("x", k, u) for k in kcs for u in range(t0 // 64, (t1 + 63) // 64)]

        ALLK = range(KC)

        xTv = xT.rearrange("(k p) t -> p k t", p=128)
        P.add("sp", lambda e: e.dma_start(out=vec[:, :], in_=vecs_d[:, :]), writes=["vec"], dma=True)
        P.add("sp", lambda e: e.dma_start(out=cstage[:, :], in_=cst_d[:, :]), writes=["cstage"], dma=True, arena=True)
        for (t0_, tn_) in FT:
            P.add("sp", (lambda e, t0_=t0_, tn_=tn_: e.dma_start(out=xres[:, :, t0_:t0_ + tn_], in_=xTv[:, :, t0_:t0_ + tn_])),
                  writes=xkeys(ALLK, t0_, t0_ + tn_), dma=True)
        P.add("dve", lambda e: e.tensor_copy(out=identb[:, :], in_=cstage[:, C_ID:C_ID + 128]), reads=["cstage"])
        for h in range(4):
            P.add("dve", (lambda e, h=h: e.tensor_copy(out=cmaskb[:, h, :], in_=cstage[:, C_CM:C_CM + 128])),
                  reads=["cstage"])
            P.add("dve", (lambda e, h=h: e.tensor_copy(out=smaskb[:, h, :], in_=cstage[0:64, C_SM:C_SM + 64])),
                  reads=["cstage"])
        P.add("dve", lambda e: e.tensor_copy(out=smcol[:, :], in_=cstage[0:64, C_SMC:C_SMC + 16]), reads=["cstage"])
        P.add("dve", lambda e: e.tensor_copy(out=rmA[:, :], in_=cstage[:, C_RMA:C_RMA + 128]), reads=["cstage"])
        P.add("dve", lambda e: e.tensor_copy(out=rmB[:, :], in_=cstage[:, C_RMB:C_RMB + 80]), reads=["cstage"])
        P.add("dve", lambda e: e.tensor_copy(out=invc[:, :], in_=cstage[:, C_INV:C_INV + 16]), reads=["cstage"],
              writes=["consts"])
        P.add("dve", lambda e: e.memset(onesb[:, :], 1.0), writes=["onesb"])
        P.add("dve", lambda e: e.memset(zerob[:, :], 0.0), writes=["zerob"])
        e_t = lbw[:, 0:16]
        p_t = lbw[:, 16:32]
        lbv = lbw[:, 32:48]
        c1t = lbw[:, 48:64]
        P.add("act", lambda e: e.activation(out=e_t, in_=vec[:, V_LBL:V_LBL + 16], func=AF.Exp),
              reads=["vec"], writes=["lb_e"])
        P.add("dve", lambda e: e.tensor_reduce(out=lbs[:, 0:4], in_=e_t.rearrange("p (h l) -> p h l", h=4),
                                               axis=AX.X, op=ALU.add), reads=["lb_e"], writes=["lb_s"])
        P.add("dve", lambda e: e.reciprocal(out=lbs[:, 4:8], in_=lbs[:, 0:4]), reads=["lb_s"], writes=["lb_r"])
        P.add("dve", lambda e: e.tensor_tensor(out=p_t.rearrange("p (h l) -> p h l", h=4),
                                               in0=e_t.rearrange("p (h l) -> p h l", h=4),
                                               in1=lbs[:, 4:8].unsqueeze(2).broadcast_to([128, 4, 4]), op=ALU.mult),
              reads=["lb_e", "lb_r"], writes=["lb_p"])
        lb3 = lbv.rearrange("p (h l) -> p h l", h=4)
        p3 = p_t.rearrange("p (h l) -> p h l", h=4)
        P.add("dve", lambda e: e.memset(lbv, 0.0), writes=["lb_v"])
        P.add("dve", lambda e: e.tensor_copy(out=lb3[:, :, 1], in_=p3[:, :, 1]), reads=["lb_p", "lb_v"], writes=["lb_v1"])
        P.add("dve", lambda e: e.tensor_tensor(out=lb3[:, :, 2], in0=lb3[:, :, 1], in1=p3[:, :, 2], op=ALU.add),
              reads=["lb_v1", "lb_p"], writes=["lb_v2"])
        P.add("dve", lambda e: e.tensor_tensor(out=lb3[:, :, 3], in0=lb3[:, :, 2], in1=p3[:, :, 3], op=ALU.add),
              reads=["lb_v2", "lb_p"], writes=["lb_v3"])
        P.add("dve", lambda e: e.tensor_scalar(out=c1t, in0=lbv, scalar1=-0.5, scalar2=0.5, op0=ALU.mult, op1=ALU.add),
              reads=["lb_v", "lb_v1", "lb_v2", "lb_v3"], writes=["c1"])
        P.add("dve", lambda e: e.tensor_scalar(out=nc1t[:, :], in0=c1t, scalar1=-1.0, scalar2=None, op0=ALU.mult),
              reads=["c1"], writes=["nc1"])

        def vcol(off):
            return vec[:, off:off + 1]

        def dma_ffn_group(l, which, g):
            j0, cnt = GROUPS[g]
            slot = g % 2
            pg = (3 * slot, 3 * slot + 1, 3 * slot + 2)
            Wi = wf_in[which][l].rearrange("(k p) n -> p k n", p=128)
            Wo = wf_out[which][l].rearrange("(j p) n -> p j n", p=128)
            wgv = page(pg[0]).rearrange("p (k n) -> p k n", k=8)
            wuv = page(pg[1]).rearrange("p (k n) -> p k n", k=8)
            wov = page(pg[2]).rearrange("p (j n) -> p j n", j=4)
            n = cnt * 128
            P.add("pool", lambda e: e.dma_start(out=wgv[:, :, 0:n], in_=Wi[:, :, j0 * 128:j0 * 128 + n]),
                  writes=[("W", pg[0])], dma=True)
            P.add("pool", lambda e: e.dma_start(out=wuv[:, :, 0:n], in_=Wi[:, :, DFF + j0 * 128:DFF + j0 * 128 + n]),
                  writes=[("W", pg[1])], dma=True)
            P.add("pool", lambda e: e.dma_start(out=wov[:, 0:cnt, :], in_=Wo[:, j0:j0 + cnt, :]),
                  writes=[("W", pg[2])], dma=True)

        MIXPG = {"q": 6, "f": 0, "i": 1, "g": 2, "pool": 3, "oA": 4, "oB": 5}

        def dma_mix_w(l, which):
            Wi = w_in_d[l].rearrange("(k p) n -> p k n", p=128)
            Wo = w_out_d[l].rearrange("(k p) n -> p k n", p=128)
            cgi = {"q": 0, "f": 1, "i": 2, "g": 3, "pool": 4}
            if which in cgi:
                pg = MIXPG[which]
                c0 = cgi[which] * 512
                dst = page(pg).rearrange("p (k n) -> p k n", k=8)
                P.add("pool", lambda e: e.dma_start(out=dst, in_=Wi[:, :, c0:c0 + 512]), writes=[("W", pg)], dma=True)
            else:
                pg = MIXPG[which]
                k0 = 0 if which == "oA" else 4
                dst = page(pg).rearrange("p (k n) -> p k n", k=4)
                P.add("pool", lambda e: e.dma_start(out=dst, in_=Wo[:, k0:k0 + 4, :]), writes=[("W", pg)], dma=True)

        def dma_poolw(l):
            src = poolw_d[l].rearrange("g c d -> c g d")
            P.add("pool", lambda e: e.dma_start(out=poolw[:, l % 2, :, :], in_=src), writes=[("poolw", l % 2)], dma=True)

        def norm_stats(t0, tn, nsq, nln, nrstd, rkey, pool="all"):
            bk = newbank(pool)
            PA("act", lambda e: e.activation(out=nsq[:, :, 0:tn], in_=xres[:, :, t0:t0 + tn], func=AF.Square),
               reads=xkeys(ALLK, t0, t0 + tn), writes=["nsq"])

            def mm(e):
                ins = None
                for k in range(KC):
                    ins = e.matmul(psb(bk, tn), lhsT=onesb[:, :], rhs=nsq[:, k, 0:tn], start=(k == 0), stop=(k == KC - 1))
                return ins
            PA("pe", mm, reads=["nsq", "onesb"], writes=[("ps", bk)])
            PA("act", lambda e: e.activation(out=nln[:, 0:tn], in_=psb(bk, tn), func=AF.Ln, bias=EPSB[:, 0:1], scale=1.0 / D),
               reads=[("ps", bk), "epsb"], writes=["nln"])
            PA("act", lambda e: e.activation(out=nrstd[:, 0:tn], in_=nln[:, 0:tn], func=AF.Exp, scale=-0.5),
               reads=["nln"], writes=[rkey])

        def norm_apply(t0, tn, gain_off, nrstd, rkey, out_fn, out_keys_fn):
            for k in range(KC):
                PA("dve", (lambda e, k=k: e.scalar_tensor_tensor(
                    out=out_fn(k), in0=xres[:, k, t0:t0 + tn], scalar=vcol(gain_off + k), in1=nrstd[:, 0:tn],
                    op0=ALU.mult, op1=ALU.mult)),
                    reads=xkeys([k], t0, t0 + tn) + [rkey, "vec"], writes=out_keys_fn(k))

        def norm_tile(t0, tn, gain_off, nsq, nln, nrstd, out_fn, out_keys_fn, pool="all"):
            norm_stats(t0, tn, nsq, nln, nrstd, "nrstd", pool)
            norm_apply(t0, tn, gain_off, nrstd, "nrstd", out_fn, out_keys_fn)

        epsb_t = sb("epsb", [128, 2], F32)
        EPSB = epsb_t
        P.add("dve", lambda e: e.memset(epsb_t[:, :], EPS), writes=["epsb"])

        def ffn(l, which):
            gain_off = (V_NF1 if which == 0 else V_NF2) + l * 8
            normed = set()

            def need_norm(n):
                if n < len(FT) and n not in normed:
                    normed.add(n)
                    t0, tn = FT[n]
                    norm_tile(t0, tn, gain_off, f_nsq, f_nln, f_nrstd,
                              (lambda k, t0=t0, tn=tn: h_full[:, k, t0:t0 + tn]),
                              (lambda k, n=n: [("h", k, n)]))
            need_norm(0)
            need_norm(1)
            sg_ctr = [0]
            def ffn_group(g, j0, cnt):
                if g + 1 < len(GROUPS):
                    dma_ffn_group(l, which, g + 1)
                else:
                    if which == 0:
                        pass
                    elif l + 1 < L:
                        dma_ffn_group(l + 1, 0, 0)
                if which == 0 and g == 0:
                    dma_mix_w(l, "q")
                    dma_poolw(l)
                if which == 0 and g == len(GROUPS) - 1:
                    dma_mix_w(l, "f")
                    dma_mix_w(l, "i")
                    dma_mix_w(l, "g")
                slot = g % 2
                pg = (3 * slot, 3 * slot + 1, 3 * slot + 2)
                wgv = page(pg[0]).rearrange("p (k n) -> p k n", k=8)
                wuv = page(pg[1]).rearrange("p (k n) -> p k n", k=8)
                wov = page(pg[2]).rearrange("p (j n) -> p j n", j=4)

                def emit_in(n):
                    t0, tn = FT[n]
                    ar = n % 3
                    for j in range(cnt):
                        bg = newbank()
                        bu = newbank()

                        def mmg(e, j=j, bg=bg):
                            ins = None
                            for k in range(KC):
                                ins = e.matmul(psb(bg, tn), lhsT=wgv[:, k, j * 128:(j + 1) * 128],
                                               rhs=h_full[:, k, t0:t0 + tn], start=(k == 0), stop=(k == KC - 1))
                            return ins

                        def mmu(e, j=j, bu=bu):
                            ins = None
                            for k in range(KC):
                                ins = e.matmul(psb(bu, tn), lhsT=wuv[:, k, j * 128:(j + 1) * 128],
                                               rhs=h_full[:, k, t0:t0 + tn], start=(k == 0), stop=(k == KC - 1))
                            return ins
                        hk = [("h", k, n) for k in range(KC)]
                        P.add("pe", mmg, reads=hk + [("W", pg[0])], writes=[("ps", bg)])
                        P.add("pe", mmu, reads=hk + [("W", pg[1])], writes=[("ps", bu)])
                        s2 = sg_ctr[0] % 2
                        sg_ctr[0] += 1
                        P.add("act", (lambda e, bg=bg, s2=s2: e.activation(out=sgt[:, s2, 0:tn], in_=psb(bg, tn), func=AF.Silu)),
                              reads=[("ps", bg)], writes=[("sgt", s2)])
                        P.add("dve", (lambda e, bu=bu, s2=s2, j=j: e.tensor_tensor(
                            out=aring[:, ar, j, 0:tn], in0=psb(bu, tn), in1=sgt[:, s2, 0:tn], op=ALU.mult)),
                            reads=[("ps", bu), ("sgt", s2)], writes=[("a", ar, j)])

                def emit_out(n):
                    t0, tn = FT[n]
                    ar = n % 3
                    for m in range(KC):
                        bo = newbank()

                        def mmo(e, m=m, bo=bo):
                            ins = None
                            for j in range(cnt):
                                ins = e.matmul(psb(bo, tn), lhsT=wov[:, j, m * 128:(m + 1) * 128],
                                               rhs=aring[:, ar, j, 0:tn], start=(j == 0), stop=(j == cnt - 1))
                            return ins
                        P.add("pe", mmo, reads=[("a", ar, j) for j in range(cnt)] + [("W", pg[2])], writes=[("ps", bo)])
                        P.add("dve", (lambda e, m=m, bo=bo: e.scalar_tensor_tensor(
                            out=xres[:, m, t0:t0 + tn], in0=psb(bo, tn), scalar=0.5, in1=xres[:, m, t0:t0 + tn],
                            op0=ALU.mult, op1=ALU.add)),
                            reads=[("ps", bo)] + xkeys([m], t0, t0 + tn), writes=xkeys([m], t0, t0 + tn))

                for n in range(len(FT) + 1):
                    if n < len(FT):
                        emit_in(n)
                        need_norm(n + 2)
                    if n >= 1:
                        emit_out(n - 1)
                if which == 0 and g == len(GROUPS) - 1:
                    dma_mix_w(l, "pool")
                    dma_mix_w(l, "oA")
                    dma_mix_w(l, "oB")

            for g, (j0, cnt) in enumerate(GROUPS):
                ffn_group(g, j0, cnt)

        class Rec:
            def __init__(self):
                self.items = []

            def add(self, eng, fn, reads=(), writes=(), dma=False, extra=(), arena=False):
                self.items.append((eng, fn, reads, writes, dma, extra, arena))

        sink = [P]

        def PA(*a, **k):
            return sink[0].add(*a, **k)

        class _FakeIns:
            def then_inc(self, *a, **k):
                return self

        class FakeEng:
            S_SET = (AF.Silu, AF.Tanh)
            B_SET = (AF.Ln, AF.Exp)

            def __init__(self):
                self.cost = 0.0
                self.aset = None

            @staticmethod
            def _n(ap):
                n = 1
                for s in ap.shape[1:]:
                    n *= s
                return n

            def matmul(self, out, lhsT, rhs, **k):
                self.cost += max(self._n(rhs), 128) / 2.4 + 4
                return _FakeIns()

            def transpose(self, out, in_, identity, **k):
                self.cost += 60
                return _FakeIns()

            def activation(self, out, in_, func, **k):
                self.cost += 220 + self._n(out) / 1.2
                if func in self.S_SET:
                    self.aset = "S"
                elif func in self.B_SET:
                    self.aset = "B"
                return _FakeIns()

            def copy(self, out, in_, **k):
                self.cost += 220 + self._n(out) / 1.2
                return _FakeIns()

            def _dve(self, out, f=1.0):
                self.cost += 110 + f * self._n(out) / 0.96
                return _FakeIns()

            def tensor_tensor(self, out, in0, in1, op, **k):
                return self._dve(out)

            def scalar_tensor_tensor(self, out, **k):
                return self._dve(out)

            def tensor_scalar(self, out, **k):
                return self._dve(out, 0.7)

            def tensor_copy(self, out, in_, **k):
                return self._dve(out, 0.7)

            def memset(self, ap, c):
                return self._dve(ap, 0.7)

            def tensor_tensor_scan(self, out, **k):
                return self._dve(out, 2.0)

            def tensor_reduce(self, out, **k):
                return self._dve(out)

            def reciprocal(self, out, in_):
                return self._dve(out)

            def dma_start(self, out, in_, **k):
                self.cost += 60
                return _FakeIns()

            def nop(self):
                self.cost += 20
                return _FakeIns()

        def merge(*recs):
            lists = [r.items for r in recs]
            rw = []
            for items in lists:
                Rk, Wk = set(), set()
                for it in items:
                    Rk.update(it[2])
                    Wk.update(it[3])
                rw.append((Rk, Wk))
            for i_ in range(len(rw)):
                for j_ in range(len(rw)):
                    if i_ != j_:
                        bad = rw[i_][1] & (rw[j_][0] | rw[j_][1])
                        assert not bad, ("concurrently merged lists share written resources", sorted(map(str, bad))[:8])
            info = {}
            per_eng = {}
            for li, items in enumerate(lists):
                lastw, readers = {}, {}
                for idx, it in enumerate(items):
                    eng, fn, reads, writes, dma = it[0], it[1], it[2], it[3], it[4]
                    deps = set()
                    for r in reads:
                        if r in lastw:
                            deps.add(lastw[r])
                    for r in writes:
                        if r in lastw:
                            deps.add(lastw[r])
                        deps.update(readers.get(r, ()))
                    deps.discard(idx)
                    for r in reads:
                        readers.setdefault(r, []).append(idx)
                    for r in writes:
                        lastw[r] = idx
                        readers[r] = []
                    fk = FakeEng()
                    fn(fk)
                    info[(li, idx)] = (eng, fk.cost, fk.aset, deps, dma)
                    per_eng.setdefault((li, eng), []).append(idx)
            ptr = {k: 0 for k in per_eng}
            free = {}
            finish = {}
            cur_set = [None]
            total = sum(len(x) for x in lists)
            order = []
            LAT = 150.0
            engs = sorted({k[1] for k in per_eng})
            while len(order) < total:
                best = None
                for eng in engs:
                    for li in range(len(lists) - 1, -1, -1):
                        lst = per_eng.get((li, eng))
                        if not lst or ptr[(li, eng)] >= len(lst):
                            continue
                        idx = lst[ptr[(li, eng)]]
                        e_, cost, aset, deps, dma = info[(li, idx)]
                        ok = True
                        ready = free.get(eng, 0.0)
                        for d in deps:
                            f = finish.get((li, d))
                            if f is None:
                                ok = False
                                break
                            ready = max(ready, f + LAT)
                        if not ok:
                            continue
                        pen = 0.0
                        if eng == "act" and aset and cur_set[0] and aset != cur_set[0]:
                            pen = 1300.0
                        key = (ready + pen, -li)
                        if best is None or key < best[0]:
                            best = (key, li, idx, eng, ready + pen, cost, aset, dma)
                _, li, idx, eng, start, cost, aset, dma = best
                ptr[(li, eng)] += 1
                if dma:
                    free[eng] = start + cost
                    finish[(li, idx)] = start + 2500.0
                else:
                    free[eng] = start + cost
                    finish[(li, idx)] = start + cost
                if eng == "act" and aset:
                    cur_set[0] = aset
                order.append((li, idx))
            for li, idx in order:
                P.add(*lists[li][idx])

        def gsched(items):
            n = len(items)
            deps = [None] * n
            lastw, readers = {}, {}
            cost = [0.0] * n
            aset = [None] * n
            for idx, it in enumerate(items):
                eng, fn, reads, writes = it[0], it[1], it[2], it[3]
                dset = set()
                for r in reads:
                    if r in lastw:
                        dset.add(lastw[r])
                for r in writes:
                    if r in lastw:
                        dset.add(lastw[r])
                    dset.update(readers.get(r, ()))
                dset.discard(idx)
                for r in reads:
                    readers.setdefault(r, []).append(idx)
                for r in writes:
                    lastw[r] = idx
                    readers[r] = []
                deps[idx] = dset
                fk = FakeEng()
                fn(fk)
                cost[idx] = fk.cost
                aset[idx] = fk.aset
            succ = [[] for _ in range(n)]
            indeg = [0] * n
            for i_, ds in enumerate(deps):
                indeg[i_] = len(ds)
                for d in ds:
                    succ[d].append(i_)
            LAT = GS_LAT
            FILL_MIN, FILL_MARGIN = 1200.0, 500.0
            bl0 = [0.0] * n
            for i_ in range(n - 1, -1, -1):
                m_ = 0.0
                for s_ in succ[i_]:
                    v_ = bl0[s_] + (LAT if items[s_][0] != items[i_][0] else 60.0)
                    if v_ > m_:
                        m_ = v_
                bl0[i_] = m_ + (2500.0 if items[i_][4] else cost[i_])
            indeg0 = list(indeg)

            def simulate(PRIO_WIN, seed):
                import random
                rng = random.Random(seed)
                bl = bl0 if seed == 0 else [v * (1.0 + 0.08 * rng.random()) for v in bl0]
                indeg = list(indeg0)
                ready = {}
                rt = [0.0] * n
                for i_ in range(n):
                    if indeg[i_] == 0:
                        ready.setdefault(items[i_][0], []).append(i_)
                free = {}
                finish = [0.0] * n
                cur_set = None
                order = []
                start_t = [0.0] * n
                while len(order) < n:
                    best = None
                    for eng, lst in ready.items():
                        fe = free.get(eng, 0.0)
                        cands = []
                        est = None
                        for i_ in lst:
                            st = rt[i_] if rt[i_] > fe else fe
                            if eng == "act" and aset[i_] and cur_set and aset[i_] != cur_set:
                                st += 1300.0
                            cands.append((st, i_))
                            if est is None or st < est:
                                est = st
                        if est is None:
                            continue
                        pick = None
                        for st, i_ in cands:
                            if st <= est + PRIO_WIN:
                                k_ = (-bl[i_], st, i_)
                                if pick is None or k_ < pick[0]:
                                    pick = (k_, st, i_)
                        key = (pick[1], pick[2])
                        if best is None or key < best:
                            best = key
                    st, i_ = best
                    start_t[i_] = st
                    eng = items[i_][0]
                    ready[eng].remove(i_)
                    dma = items[i_][4]
                    free[eng] = st + cost[i_]
                    finish[i_] = st + (2500.0 if dma else cost[i_])
                    if eng == "act" and aset[i_]:
                        cur_set = aset[i_]
                    order.append(i_)
                    for s_ in succ[i_]:
                        indeg[s_] -= 1
                        lat = LAT if items[s_][0] != eng else 60.0
                        if finish[i_] + lat > rt[s_]:
                            rt[s_] = finish[i_] + lat
                        if indeg[s_] == 0:
                            ready.setdefault(items[s_][0], []).append(s_)
                return max(finish), order, start_t

            bestres = None
            for win_, seed_ in GS_TRIES:
                res = simulate(win_, seed_)
                if bestres is None or res[0] < bestres[0]:
                    bestres = res
            _, order, start_t = bestres
            z_rhs = zerob[:, :].unsqueeze(1).broadcast_to([128, 4, 128])

            def filler(k):
                def f(e):
                    ins = None
                    for _ in range(k):
                        ins = e.matmul(psb(FILL_BANK).rearrange("p (a b) -> p a b", a=4), lhsT=zerob[:, :], rhs=z_rhs, start=True, stop=True)
                    return ins
                return f
            pe_end = None
            for i_ in order:
                it = items[i_]
                if it[0] == "pe":
                    st = start_t[i_]
                    if pe_end is not None and st - pe_end > FILL_MIN:
                        nf = int((st - pe_end - FILL_MARGIN) / 217.0)
                        while nf > 0:
                            k = min(nf, 4)
                            P.add("pe", filler(k), reads=["zerob"], writes=[("ps", FILL_BANK)])
                            nf -= k
                    pe_end = st + cost[i_]
                P.add(*it)

        def mixing(l):
            gain_off = V_NMX + l * 8
            w_q = page(MIXPG["q"]).rearrange("p (k n) -> p k n", k=8)
            w_f = page(MIXPG["f"]).rearrange("p (k n) -> p k n", k=8)
            w_i = page(MIXPG["i"]).rearrange("p (k n) -> p k n", k=8)
            w_g = page(MIXPG["g"]).rearrange("p (k n) -> p k n", k=8)
            w_p = page(MIXPG["pool"]).rearrange("p (k n) -> p k n", k=8)
            w_oA = page(MIXPG["oA"]).rearrange("p (k n) -> p k n", k=4)
            w_oB = page(MIXPG["oB"]).rearrange("p (k n) -> p k n", k=4)
            pw = poolw[:, l % 2, :, :]
            hgn = lambda h: vcol(V_HGN + l * 4 + h)
            psc = lambda g: vcol(V_PSC + l * 4 + g)
            c1c = lambda h: lbw[:, 48 + h * 4 + l:48 + h * 4 + l + 1]
            nc1c = lambda h: nc1t[:, h * 4 + l:h * 4 + l + 1]

            P.add("dve", lambda e: e.memset(Sst.rearrange("p h v -> p (h v)"), 0.0), writes=["S"])
            P.add("dve", lambda e: e.memset(Sbf[:, 0, :, :].rearrange("p h v -> p (h v)"), 0.0), writes=[("Sbf", 0)])
            P.add("dve", lambda e: e.memset(U[:, :, 0:16], 0.0), writes=["Uhalo"])
            sbf_ctr = [0]

            def zproj(wv, cg_key, tn, evac, pool, hm, hmkey):
                bk = newbank(pool)
                for h in range(4):
                    def mm(e, h=h, bk=bk):
                        ins = None
                        for k in range(KC):
                            ins = e.matmul(ps[:, bk * 512 + h * tn:bk * 512 + (h + 1) * tn], lhsT=wv[:, k, h * 128:(h + 1) * 128],
                                           rhs=hm[:, k, 0:tn], start=(k == 0), stop=(k == KC - 1))
                        return ins
                    PA("pe", mm, reads=[hmkey, ("W", MIXPG[cg_key])] + ([("ps", bk)] if h else []), writes=[("ps", bk)])
                evac(bk)

            def hgrn_part(par, c_lo, ncols, chunks, pool):
                ktp, qtp, Epp, vtp = kt2[:, par], qt2[:, par], Ep2[:, par], vtok2[:, par]
                cs = slice(c_lo, c_lo + ncols)
                bT = newbank(pool)
                psT = ps[0:ncols, bT * 512:(bT + 1) * 512].bitcast(BF16)

                def tr(e):
                    ins = None
                    for h in range(4):
                        ins = e.transpose(out=psT[:, h * 128:(h + 1) * 128], in_=ktp[:, h, cs], identity=identb[:, :])
                    return ins
                PA("pe", tr, reads=[("kt", par), "consts"], writes=[("ps", bT)])
                PA("act", lambda e: e.copy(out=ktok[0:ncols, :], in_=psT[:, 0:512]), reads=[("ps", bT)], writes=["ktok"])
                bS = newbank(pool)

                def mms(e):
                    ins = None
                    for h in range(4):
                        ins = e.matmul(ps[0:ncols, bS * 512 + h * ncols:bS * 512 + (h + 1) * ncols],
                                       lhsT=ktp[:, h, cs], rhs=qtp[:, h, cs], start=True, stop=True)
                    return ins
                PA("pe", mms, reads=[("kt", par), ("qt", par)], writes=[("ps", bS)])
                scv = scm[0:ncols, 0:4 * ncols].rearrange("p (h t) -> p h t", h=4)
                PA("dve", lambda e: e.tensor_tensor(
                    out=scv, in0=ps[0:ncols, bS * 512:bS * 512 + 4 * ncols].rearrange("p (h t) -> p h t", h=4),
                    in1=cmaskb[0:ncols, :, 0:ncols], op=ALU.mult),
                    reads=[("ps", bS), "consts"], writes=["scm"])
                slots_in = []
                for (off, ln) in chunks:
                    cur = sbf_ctr[0] % 4
                    slots_in.append(cur)
                    bA = newbank(pool)
                    rows = slice(off, off + ln)

                    def mma(e, rows=rows, bA=bA):
                        ins = None
                        for h in range(4):
                            ins = e.matmul(ps[:, bA * 512 + h * 128:bA * 512 + (h + 1) * 128],
                                           lhsT=ktok[rows, h * 128:(h + 1) * 128], rhs=vtp[rows, h * 128:(h + 1) * 128],
                                           start=True, stop=True)
                        return ins
                    PA("pe", mma, reads=["ktok", ("vtok", par)], writes=[("ps", bA)])
                    PA("dve", (lambda e, bA=bA: e.tensor_tensor(
                        out=tmpS.rearrange("p h v -> p (h v)"), in0=psb(bA), in1=Sst.rearrange("p h v -> p (h v)"), op=ALU.add)),
                        reads=[("ps", bA), "S"], writes=["tmpS"])
                    lastcol = c_lo + off + ln - 1
                    dec = Epp[:, :, lastcol:lastcol + 1].broadcast_to([128, 4, 128])
                    PA("dve", (lambda e, dec=dec: e.tensor_tensor(out=Sst[:, :, :], in0=tmpS[:, :, :], in1=dec, op=ALU.mult)),
                       reads=["tmpS", ("Ep", par)], writes=["S"])
                    nxt = (sbf_ctr[0] + 1) % 4
                    sbf_ctr[0] += 1
                    PA("act", (lambda e, nxt=nxt: e.copy(out=Sbf[:, nxt, :, :].rearrange("p h v -> p (h v)"),
                                                       in_=Sst.rearrange("p h v -> p (h v)"))),
                       reads=["S"], writes=[("Sbf", nxt)])
                bO = newbank(pool)

                def mmo(e):
                    ins = None
                    for h in range(4):
                        base = bO * 512 + h * ncols
                        e.matmul(ps[:, base:base + ncols], lhsT=vtp[0:ncols, h * 128:(h + 1) * 128],
                                 rhs=scm[0:ncols, h * ncols:(h + 1) * ncols], start=True, stop=False)
                        for ci, (off, ln) in enumerate(chunks):
                            ins = e.matmul(ps[:, base + off:base + off + ln], lhsT=Sbf[:, slots_in[ci], h, :],
                                           rhs=qtp[:, h, c_lo + off:c_lo + off + ln], start=False,
                                           stop=(ci == len(chunks) - 1))
                    return ins
                PA("pe", mmo, reads=[("vtok", par), "scm", ("qt", par)] + [("Sbf", s) for s in slots_in], writes=[("ps", bO)])
                return bO

            def onorm(par, par3, src_ap, src_keys, n4, ncols, c_lo, pool):
                PA("act", lambda e: e.activation(out=osq[:, 0:n4], in_=src_ap, func=AF.Square),
                   reads=src_keys, writes=["osq"])
                bk = newbank(pool)
                PA("pe", lambda e: e.matmul(psb(bk, n4), lhsT=onesb[:, :], rhs=osq[:, 0:n4], start=True, stop=True),
                   reads=["osq", "onesb"], writes=[("ps", bk)])
                PA("act", lambda e: e.activation(out=olnv[:, 0:n4], in_=psb(bk, n4), func=AF.Ln, bias=EPSB[:, 0:1], scale=1.0 / 128),
                   reads=[("ps", bk), "epsb"], writes=["olnv"])
                PA("act", lambda e: e.activation(out=orstd[:, 0:n4], in_=olnv[:, 0:n4], func=AF.Exp, scale=-0.5),
                   reads=["olnv"], writes=["orstd"])
                PA("dve", lambda e: e.tensor_tensor(out=t1[:, 0:n4], in0=src_ap, in1=orstd[:, 0:n4], op=ALU.mult),
                   reads=src_keys + ["orstd"], writes=["t1"])
                for h in range(4):
                    PA("dve", (lambda e, h=h: e.scalar_tensor_tensor(
                        out=mixed2[:, par3, h, c_lo:c_lo + ncols], in0=t1[:, h * ncols:(h + 1) * ncols], scalar=hgn(h),
                        in1=gate2[:, par, h, c_lo:c_lo + ncols], op0=ALU.mult, op1=ALU.mult)),
                        reads=["t1", ("gate", par), "vec", ("mixed", par3, h)], writes=[("mixed", par3, h)])

            def windows(X, Wd, outs, xkey):
                for g in range(4):
                    w = 2 << g
                    bufs = [wa, wb]
                    src = X[:, g, 0:Wd]
                    cur = None
                    sh, bi, lo = 1, 0, 0
                    while sh < w:
                        dst = bufs[bi]
                        prev = src if cur is None else cur
                        lo2 = lo + sh
                        PA("dve", (lambda e, dst=dst, prev=prev, lo2=lo2, sh=sh: e.tensor_tensor(
                            out=dst[:, lo2:Wd], in0=prev[:, lo2:Wd], in1=prev[:, lo2 - sh:Wd - sh], op=ALU.add)),
                            reads=[xkey, ("wbuf", 1 - bi)], writes=[("wbuf", bi)])
                        cur, lo, sh, bi = dst, lo2, sh * 2, 1 - bi
                    dst_ap, c_lo, ncol = outs[g]
                    PA("dve", (lambda e, cur=cur, dst_ap=dst_ap, c_lo=c_lo, ncol=ncol, w=w, src=src: e.scalar_tensor_tensor(
                        out=dst_ap, in0=cur[:, c_lo:c_lo + ncol], scalar=1.0 / w, in1=src[:, c_lo:c_lo + ncol],
                        op0=ALU.mult, op1=ALU.subtract)),
                        reads=[xkey, ("wbuf", 0), ("wbuf", 1)], writes=[("dd", g)])
                    yield g, cur

            def stage1(ti):
                tail = (ti == 16)
                par = ti % 2
                par3 = ti % 3
                pool = "s1"
                t0 = ti * 128
                tn = 80 if tail else 128
                ktp, qtp, Epp, gtp = kt2[:, par], qt2[:, par], Ep2[:, par], gate2[:, par]
                th, kf, gl, bb, qb = th2[par], kf2[par], gl2[par], bb2[par], qb2[par]
                hm = hm2[par]
                Em = th
                norm_apply(t0, tn, gain_off, m_nrstd2[:, par], ("nrstd", par),
                           (lambda k: hm[:, k, 0:tn]), (lambda k: [("hm", par)]))

                def ps4(bk):
                    return ps[:, bk * 512:bk * 512 + 4 * tn].rearrange("p (h t) -> p h t", h=4)

                def evac_f(bk):
                    PA("act", (lambda e: e.activation(out=th[:, :, 0:tn], in_=ps4(bk), func=AF.Tanh, scale=0.5)),
                       reads=[("ps", bk)], writes=[("th", par, h) for h in range(4)])
                    for h in range(4):
                        PA("dve", (lambda e, h=h: e.tensor_scalar(out=kf[:, h, 0:tn], in0=th[:, h, 0:tn], scalar1=nc1c(h), scalar2=c1c(h),
                                                                  op0=ALU.mult, op1=ALU.add)),
                           reads=[("th", par, h), "c1", "nc1"], writes=[("kf", par)])
                zproj(w_f, "f", tn, evac_f, pool, hm, ("hm", par))
                zproj(w_p, "pool", tn, lambda bk: PA(
                    "act", (lambda e: e.copy(out=U[:, :, 16:16 + tn], in_=ps4(bk))),
                    reads=[("ps", bk)], writes=["Ux"]), pool, hm, ("hm", par))
                zproj(w_q, "q", tn, lambda bk: PA(
                    "act", (lambda e: e.activation(out=qb[:, :, 0:tn], in_=ps4(bk), func=AF.Silu)),
                    reads=[("ps", bk)], writes=[("qb", par)]), pool, hm, ("hm", par))
                zproj(w_g, "g", tn, lambda bk: PA(
                    "act", (lambda e: e.activation(out=gtp[:, :, 0:tn], in_=ps4(bk), func=AF.Silu)),
                    reads=[("ps", bk)], writes=[("gate", par)]), pool, hm, ("hm", par))
                nrow = 16 if tail else 128
                bV = newbank(pool)

                def mmv(e):
                    ins = None
                    for k in range(KC):
                        ins = e.matmul(ps[0:nrow, bV * 512:(bV + 1) * 512], lhsT=hm[:, k, 0:nrow], rhs=w_i[:, k, :],
                                       start=(k == 0), stop=(k == KC - 1))
                    return ins
                PA("pe", mmv, reads=[("hm", par), ("W", MIXPG["i"])], writes=[("ps", bV)])
                PA("dve", lambda e: e.tensor_copy(out=vtok2[0:nrow, par, :], in_=ps[0:nrow, bV * 512:(bV + 1) * 512]),
                   reads=[("ps", bV)], writes=[("vtok", par)])
                PA("act", lambda e: e.activation(out=gl[:, :, 0:tn], in_=kf[:, :, 0:tn], func=AF.Ln, bias=ONEB[:, 0:1], scale=-1.0),
                   reads=[("kf", par), "oneb"], writes=[("gl", par)])
                for h in range(4):
                    rm = rmB[:, 0:80] if tail else rmA[:, 0:128]
                    PA("dve", (lambda e, h=h, rm=rm: e.tensor_tensor_scan(
                        out=bb[:, h, 0:tn], data0=rm, data1=gl[:, h, 0:tn],
                        initial=0.0, op0=ALU.mult, op1=ALU.add)), reads=[("gl", par), "consts"], writes=[("bb", par)])
                PA("act", lambda e: e.activation(out=Em[:, :, 0:tn], in_=bb[:, :, 0:tn], func=AF.Exp, scale=-1.0),
                   reads=[("bb", par)] + [("th", par, h) for h in range(4)], writes=[("Em", par)] + [("th", par, h) for h in range(4)])
                PA("act", lambda e: e.activation(out=Epp[:, :, 0:tn], in_=bb[:, :, 0:tn], func=AF.Exp),
                   reads=[("bb", par)], writes=[("Ep", par)])
                PA("dve", lambda e: e.tensor_tensor(out=ktp[:, :, 0:tn], in0=kf[:, :, 0:tn], in1=Em[:, :, 0:tn], op=ALU.mult),
                   reads=[("kf", par), ("Em", par)] + [("th", par, h) for h in range(4)], writes=[("kt", par)])
                PA("dve", lambda e: e.tensor_tensor(out=qtp[:, :, 0:tn], in0=qb[:, :, 0:tn], in1=Epp[:, :, 0:tn], op=ALU.mult),
                   reads=[("qb", par), ("Ep", par)], writes=[("qt", par)])
                if not tail:
                    outs = [(dd[:, g, 0:128], 16, 128) for g in range(4)]
                    for g, cur in windows(U, 144, outs, "Ux"):
                        if ti == 0 and g > 0:
                            w = 2 << g
                            PA("dve", (lambda e, cur=cur, w=w: e.tensor_tensor(
                                out=wfix[:, 0:w - 1], in0=cur[:, 16:16 + w - 1], in1=invc[:, 0:w - 1], op=ALU.mult)),
                                reads=[("wbuf", 0), ("wbuf", 1), "consts"], writes=["wfix"])
                            PA("dve", (lambda e, g=g, w=w: e.tensor_tensor(
                                out=dd[:, g, 0:w - 1], in0=wfix[:, 0:w - 1], in1=U[:, g, 16:16 + w - 1], op=ALU.subtract)),
                                reads=["wfix", "Ux", ("dd", g)], writes=[("dd", g)])
                        elif ti == 0 and g == 0:
                            PA("dve", lambda e: e.memset(dd[:, 0, 0:1], 0.0), reads=[("dd", 0)], writes=[("dd", 0)])
                else:
                    outs = [(dd[:, g, 0:16], 16, 16) for g in range(4)]
                    for _ in windows(U, 32, outs, "Ux"):
                        pass
                    PA("sp", lambda e: e.dma_start(out=plp_d[l].rearrange("g p r -> p g r"), in_=U[:, :, 17:32]),
                       reads=["Ux"], dma=True, arena=True)
                    PA("sp", lambda e: e.dma_start(out=ustage.rearrange("p g i r -> p g (i r)")[:, :, 0:240],
                                                   in_=spl_d[l].rearrange("g p n -> p g n")),
                       writes=["ustage"], dma=True, arena=True)
                    PA("dve", lambda e: e.tensor_copy(out=uext[:, :, :, 0:15], in_=ustage[:, :, :, 0:15]),
                       reads=["ustage"], writes=["uext"])
                    PA("dve", lambda e: e.tensor_copy(out=uext[:, :, :, 15:19],
                                                      in_=U[:, :, 32:96].rearrange("p g (i r) -> p g i r", i=16)),
                       reads=["Ux", "uext"], writes=["uext"])
                    PA("dve", lambda e: e.tensor_copy(out=ustage[:, :, :, 0:15], in_=uext[:, :, :, 4:19]),
                       reads=["uext", "ustage"], writes=["ustage"])
                    PA("sp", lambda e: e.dma_start(out=pls_d[l].rearrange("g p n -> p g n"),
                                                   in_=ustage.rearrange("p g i r -> p g (i r)")[:, :, 0:240]),
                       reads=["ustage"], dma=True, arena=True)
                    uflat = uext.rearrange("p g i r -> p g (i r)")
                    for g in range(4):
                        w = 2 << g
                        bufs = [wa, wb]
                        for half in range(2):
                            src = uflat[:, g, half * 152:(half + 1) * 152]
                            cur = None
                            sh, bi, lo = 1, 0, 0
                            while sh < w:
                                dst = bufs[bi]
                                prev = src if cur is None else cur
                                lo2 = lo + sh
                                PA("dve", (lambda e, dst=dst, prev=prev, lo2=lo2, sh=sh: e.tensor_tensor(
                                    out=dst[:, lo2:152], in0=prev[:, lo2:152], in1=prev[:, lo2 - sh:152 - sh], op=ALU.add)),
                                    reads=["uext", ("wbuf", 1 - bi)], writes=[("wbuf", bi)])
                                cur, lo, sh, bi = dst, lo2, sh * 2, 1 - bi
                            dst_ap = dd[:, g, 16 + half * 32:16 + half * 32 + 32].rearrange("p (i r) -> p i r", i=8)
                            curv = cur[:, 0:152].rearrange("p (i r) -> p i r", i=8)[:, :, 15:19]
                            srcv = src.rearrange("p (i r) -> p i r", i=8)[:, :, 15:19]
                            PA("dve", (lambda e, dst_ap=dst_ap, curv=curv, srcv=srcv, w=w: e.scalar_tensor_tensor(
                                out=dst_ap, in0=curv, scalar=1.0 / w, in1=srcv, op0=ALU.mult, op1=ALU.subtract)),
                                reads=["uext", ("wbuf", 0), ("wbuf", 1), ("dd", g)], writes=[("dd", g)])
                for g in range(4):
                    bk = newbank(pool)
                    PA("pe", (lambda e, g=g, bk=bk: e.matmul(psb(bk, tn), lhsT=pw[:, g, :], rhs=dd[:, g, 0:tn], start=True, stop=True)),
                       reads=[("dd", g), ("poolw", l % 2)], writes=[("ps", bk)])
                    PA("act", (lambda e, g=g, bk=bk: e.activation(out=mixed2[:, par3, 4 + g, 0:tn], in_=psb(bk, tn), func=AF.Copy, scale=psc(g))),
                       reads=[("ps", bk), "vec"], writes=[("mixed", par3, 4 + g)])
                if not tail:
                    PA("dve", lambda e: e.tensor_copy(out=U[:, :, 0:16], in_=U[:, :, 128:144]), reads=["Ux"], writes=["Uhalo", "Ux"])
                    tn2 = 80 if ti + 1 == 16 else 128
                    norm_stats(t0 + 128, tn2, m_nsq, m_nln, m_nrstd2[:, 1 - par], ("nrstd", 1 - par), pool)

            def stage2(ti):
                tail = (ti == 16)
                par = ti % 2
                par3 = ti % 3
                pool = "t2" if tail else "s2"
                if not tail:
                    bO = hgrn_part(par, 0, 128, [(0, 64), (64, 64)], pool)
                    onorm(par, par3, psb(bO), [("ps", bO)], 512, 128, 0, pool)
                else:
                    bO = hgrn_part(par, 0, 16, [(0, 16)], pool)
                    onorm(par, par3, psb(bO, 64), [("ps", bO)], 64, 16, 0, pool)
                    PA("sp", lambda e: e.dma_start(out=hgp_d[l].rearrange("h d v -> d h v"), in_=Sst[:, :, :]),
                       reads=["S"], dma=True, arena=True)
                    sample_part(l, par, par3)

            def stage3(ti):
                tail = (ti == 16)
                par3 = ti % 3
                pool = "s3"
                t0 = ti * 128
                tn = 80 if tail else 128
                for half in range(2):
                    bk = newbank(pool)
                    for j in range(4):
                        m = half * 4 + j

                        def mmw(e, m=m, j=j, bk=bk):
                            ins = None
                            for k in range(KC):
                                wv = w_oA if k < 4 else w_oB
                                ins = e.matmul(ps[:, bk * 512 + j * tn:bk * 512 + (j + 1) * tn], lhsT=wv[:, k % 4, m * 128:(m + 1) * 128],
                                               rhs=mixed2[:, par3, k, 0:tn], start=(k == 0), stop=(k == KC - 1))
                            return ins
                        PA("pe", mmw, reads=[("mixed", par3, k) for k in range(KC)] + [("W", MIXPG["oA"]), ("W", MIXPG["oB"])]
                           + ([("ps", bk)] if j else []), writes=[("ps", bk)])
                    ms = range(half * 4, half * 4 + 4)
                    PA("dve", (lambda e, half=half, bk=bk: e.tensor_tensor(
                        out=xres[:, half * 4:half * 4 + 4, t0:t0 + tn],
                        in0=ps[:, bk * 512:bk * 512 + 4 * tn].rearrange("p (m t) -> p m t", m=4),
                        in1=xres[:, half * 4:half * 4 + 4, t0:t0 + tn], op=ALU.add)),
                        reads=[("ps", bk)] + xkeys(ms, t0, t0 + tn), writes=xkeys(ms, t0, t0 + tn))

            NT = 17
            allrec = Rec()
            sink[0] = allrec
            norm_stats(0, 128, m_nsq, m_nln, m_nrstd2[:, 0], ("nrstd", 0), "s1")
            for ti in range(NT):
                stage1(ti)
                stage2(ti)
                stage3(ti)
            sink[0] = P
            gsched(allrec.items)

        def sample_part(l, par, par3):
            w_i = page(MIXPG["i"]).rearrange("p (k n) -> p k n", k=8)
            ktp, qtp, Epp, gtp = kt2[:, par], qt2[:, par], Ep2[:, par], gate2[:, par]
            vtp = vtok2[:, par]
            pool = "t2"
            hm = hm2[par]
            cs = slice(16, 80)
            bT = newbank(pool)
            psT = ps[0:64, bT * 512:(bT + 1) * 512].bitcast(BF16)

            def tr(e):
                ins = None
                for h in range(4):
                    ins = e.transpose(out=psT[:, h * 128:(h + 1) * 128], in_=ktp[:, h, cs], identity=identb[:, :])
                return ins
            PA("pe", tr, reads=[("kt", par), "consts"], writes=[("ps", bT)])
            PA("act", lambda e: e.copy(out=ktok[0:64, :], in_=psT[:, 0:512]), reads=[("ps", bT)], writes=["ktok"])
            bV = newbank(pool)

            def mmv(e):
                ins = None
                for k in range(KC):
                    ins = e.matmul(ps[0:64, bV * 512:(bV + 1) * 512], lhsT=hm[:, k, cs], rhs=w_i[:, k, :],
                                   start=(k == 0), stop=(k == KC - 1))
                return ins
            PA("pe", mmv, reads=[("hm", par), ("W", MIXPG["i"])], writes=[("ps", bV)])
            PA("dve", lambda e: e.tensor_copy(out=vtp[0:64, :], in_=ps[0:64, bV * 512:(bV + 1) * 512]),
               reads=[("ps", bV)], writes=[("vtok", par)])
            bS = newbank(pool)

            def mms(e):
                ins = None
                for h in range(4):
                    ins = e.matmul(ps[0:64, bS * 512 + h * 64:bS * 512 + (h + 1) * 64], lhsT=ktp[:, h, cs], rhs=qtp[:, h, cs],
                                   start=True, stop=True)
                return ins
            PA("pe", mms, reads=[("kt", par), ("qt", par)], writes=[("ps", bS)])
            PA("dve", lambda e: e.tensor_tensor(
                out=scm[0:64, 0:256].rearrange("p (h t) -> p h t", h=4),
                in0=ps[0:64, bS * 512:bS * 512 + 256].rearrange("p (h t) -> p h t", h=4),
                in1=smaskb[:, :, :], op=ALU.mult), reads=[("ps", bS), "consts"], writes=["scm"])
            bOi = newbank(pool)
            bOx = newbank(pool)
            pinned.update((bOi, bOx))

            def mmoi(e):
                ins = None
                for h in range(4):
                    ins = e.matmul(ps[:, bOi * 512 + h * 64:bOi * 512 + (h + 1) * 64], lhsT=vtp[0:64, h * 128:(h + 1) * 128],
                                   rhs=scm[0:64, h * 64:(h + 1) * 64], start=True, stop=True)
                return ins
            PA("pe", mmoi, reads=[("vtok", par), "scm"], writes=[("ps", bOi)])
            for b in range(8):
                hb = b % 2
                sl = slice(2 * hb, 2 * hb + 2)
                src = shg_d[l, 2 * b:2 * b + 2].rearrange("i h d v -> d (i h) v")
                PA("sp", (lambda e, src=src, sl=sl: e.dma_start(out=Ss[:, sl].rearrange("p i h v -> p (i h) v"), in_=src)),
                   writes=[("Ss", hb)] + ODDKEYS, dma=True, arena=True)
                PA("act", (lambda e, sl=sl: e.copy(out=Ssb[:, sl].rearrange("p i h v -> p (i h v)"),
                                                 in_=Ss[:, sl].rearrange("p i h v -> p (i h v)"))),
                   reads=[("Ss", hb)], writes=[("Ssb", hb)] + ODDKEYS)

                def mmx(e, b=b, hb=hb):
                    ins = None
                    for ii in range(2):
                        i = 2 * b + ii
                        for h in range(4):
                            c = bOx * 512 + h * 64 + 4 * i
                            ins = e.matmul(ps[:, c:c + 4], lhsT=Ssb[:, 2 * hb + ii, h, :], rhs=qtp[:, h, 16 + 4 * i:16 + 4 * i + 4],
                                           start=True, stop=True, skip_group_check=True)
                    return ins
                PA("pe", mmx, reads=[("Ssb", hb), ("qt", par), ("ps", bOx)], writes=[("ps", bOx)])
                for ii in range(2):
                    i = 2 * b + ii
                    si = 2 * hb + ii
                    PA("dve", (lambda e, i=i, si=si: e.tensor_scalar(out=km[0:64, si, :], in0=ktok[0:64, :], scalar1=smcol[:, i:i + 1],
                                                                     scalar2=None, op0=ALU.mult)),
                       reads=["ktok", "consts"], writes=[("km", si), ("hm", 1 - par)])
                    bA = newbank(pool)

                    def mma(e, si=si, bA=bA):
                        ins = None
                        for h in range(4):
                            ins = e.matmul(ps[:, bA * 512 + h * 128:bA * 512 + (h + 1) * 128], lhsT=km[0:64, si, h * 128:(h + 1) * 128],
                                           rhs=vtp[0:64, h * 128:(h + 1) * 128], start=True, stop=True)
                        return ins
                    PA("pe", mma, reads=[("km", si), ("vtok", par)], writes=[("ps", bA)])
                    PA("dve", (lambda e, si=si, bA=bA: e.tensor_tensor(
                        out=tmpS.rearrange("p h v -> p (h v)"), in0=psb(bA), in1=Ss[:, si, :, :].rearrange("p h v -> p (h v)"), op=ALU.add)),
                        reads=[("ps", bA), ("Ss", hb)], writes=["tmpS"])
                    lastcol = 16 + 4 * i + 3
                    dec = Epp[:, :, lastcol:lastcol + 1].broadcast_to([128, 4, 128])
                    PA("dve", (lambda e, si=si, dec=dec: e.tensor_tensor(out=Ss[:, si, :, :], in0=tmpS[:, :, :], in1=dec, op=ALU.mult)),
                       reads=["tmpS", ("Ep", par), ("Ssb", hb), ("Ss", hb)], writes=[("Ss", hb)])
                dst = hgs_d[l, 2 * b:2 * b + 2].rearrange("i h d v -> d (i h) v")
                PA("sp", (lambda e, dst=dst, sl=sl: e.dma_start(out=dst, in_=Ss[:, sl].rearrange("p i h v -> p (i h) v"))),
                   reads=[("Ss", hb)], dma=True, arena=True)
            pinned.clear()
            PA("act", lambda e: e.copy(out=oint[:, 0:256], in_=psb(bOx, 256)), reads=[("ps", bOx)], writes=["oint"])
            PA("dve", lambda e: e.tensor_tensor(out=osum[:, 0:256], in0=psb(bOi, 256), in1=oint[:, 0:256], op=ALU.add),
               reads=[("ps", bOi), "oint"], writes=["osum", ("wbuf", 0), ("wbuf", 1)])
            hgn = lambda h: vcol(V_HGN + l * 4 + h)
            PA("act", lambda e: e.activation(out=osq[:, 0:256], in_=osum[:, 0:256], func=AF.Square), reads=["osum"], writes=["osq"])
            bk = newbank(pool)
            PA("pe", lambda e: e.matmul(psb(bk, 256), lhsT=onesb[:, :], rhs=osq[:, 0:256], start=True, stop=True),
               reads=["osq", "onesb"], writes=[("ps", bk)])
            PA("act", lambda e: e.activation(out=olnv[:, 0:256], in_=psb(bk, 256), func=AF.Ln, bias=EPSB[:, 0:1], scale=1.0 / 128),
               reads=[("ps", bk), "epsb"], writes=["olnv"])
            PA("act", lambda e: e.activation(out=orstd[:, 0:256], in_=olnv[:, 0:256], func=AF.Exp, scale=-0.5),
               reads=["olnv"], writes=["orstd"])
            PA("dve", lambda e: e.tensor_tensor(out=t1[:, 0:256], in0=osum[:, 0:256], in1=orstd[:, 0:256], op=ALU.mult),
               reads=["osum", "orstd"], writes=["t1"])
            for h in range(4):
                PA("dve", (lambda e, h=h: e.scalar_tensor_tensor(
                    out=mixed2[:, par3, h, 16:80], in0=t1[:, h * 64:(h + 1) * 64], scalar=hgn(h), in1=gtp[:, h, 16:80],
                    op0=ALU.mult, op1=ALU.mult)), reads=["t1", ("gate", par), "vec", ("mixed", par3, h)], writes=[("mixed", par3, h)])

        oneb_t = sb("oneb", [128, 2], F32)
        ONEB = oneb_t
        P.add("dve", lambda e: e.memset(oneb_t[:, :], 1.0), writes=["oneb"])

        dma_ffn_group(0, 0, 0)
        for l in range(L):
            ffn(l, 0)
            P.barrier()
            mixing(l)
            dma_ffn_group(l, 1, 0)
            P.barrier()
            ffn(l, 1)
        P.barrier()
        for n, (t0, tn) in enumerate(FT):
            norm_tile(t0, tn, V_NFIN, f_nsq, f_nln, f_nrstd,
                      (lambda k, tn=tn: f_ytmp[:, k, 0:tn]), (lambda k: [("ytmp", k)]))
            yv = yT.rearrange("(k p) t -> p k t", p=128)
            P.add("sp", (lambda e, t0=t0, tn=tn: e.dma_start(out=yv[:, :, t0:t0 + tn], in_=f_ytmp[:, :, 0:tn])),
                  reads=[("ytmp", k) for k in range(KC)], dma=True, arena=True)

        P.finalize()
        with nc.Block() as block:
            P.emit(nc, block, esem, qsem)
    return nc


def _consts():
    c = np.zeros((128, NCST), np.float32)
    c[:, C_ID:C_ID + 128] = np.eye(128, dtype=np.float32)
    s = np.arange(128)[:, None]
    t = np.arange(128)[None, :]
    c[:, C_CM:C_CM + 128] = ((s <= t) & (s // 64 == t // 64)).astype(np.float32)
    s = np.arange(64)[:, None]
    t = np.arange(64)[None, :]
    c[0:64, C_SM:C_SM + 64] = ((s <= t) & (s // 4 == t // 4)).astype(np.float32)
    c[0:64, C_SMC:C_SMC + 16] = (np.arange(64)[:, None] // 4 == np.arange(16)[None, :]).astype(np.float32)
    ra = np.ones(512, np.float32)
    ra[0::64] = 0.0
    c[:, C_RMA:C_RMA + 512] = ra[None, :]
    rb = np.ones(80, np.float32)
    rb[0] = 0.0
    rb[16::4] = 0.0
    c[:, C_RMB:C_RMB + 320] = np.tile(rb, 4)[None, :]
    c[:, C_INV:C_INV + 16] = (1.0 / np.arange(1, 17, dtype=np.float32))[None, :]
    return c


def _vecs(norm_ffn1, norm_mix, norm_ffn2, norm_final, hg_norm, pool_scale, lb_logits):
    v = np.zeros((128, NV), np.float32)
    for l in range(4):
        v[:, V_NF1 + l * 8:V_NF1 + l * 8 + 8] = norm_ffn1[l].reshape(8, 128).T
        v[:, V_NMX + l * 8:V_NMX + l * 8 + 8] = norm_mix[l].reshape(8, 128).T
        v[:, V_NF2 + l * 8:V_NF2 + l * 8 + 8] = norm_ffn2[l].reshape(8, 128).T
        v[:, V_HGN + l * 4:V_HGN + l * 4 + 4] = hg_norm[l].T
        v[:, V_PSC + l * 4:V_PSC + l * 4 + 4] = pool_scale[l].reshape(4, 128).T
    v[:, V_NFIN:V_NFIN + 8] = norm_final.reshape(8, 128).T
    v[:, V_LBL:V_LBL + 16] = lb_logits.reshape(4, 4, 128).transpose(2, 1, 0).reshape(128, 16)
    return v


_NC_CACHE = {}


def make_in_maps(inputs, cores):
    f = lambda a: np.ascontiguousarray(np.asarray(a, dtype=np.float32))
    x_prompt, x_sample, meta = f(inputs["x_prompt"]), f(inputs["x_sample"]), f(inputs["meta"])
    state_hgrn, state_pool = f(inputs["state_hgrn"]), f(inputs["state_pool"])
    cst = _consts()
    vecs = _vecs(f(inputs["norm_ffn1"]), f(inputs["norm_mix"]), f(inputs["norm_ffn2"]), f(inputs["norm_final"]),
                 f(inputs["hg_norm"]), f(inputs["pool_scale"]), f(inputs["lb_logits"]))
    shared = {k: f(inputs[k]) for k in ("w_ffn1_in", "w_ffn1_out", "w_ffn2_in", "w_ffn2_out", "w_in", "w_out", "pool_w")}
    maps = []
    for c in cores:
        xs = x_sample[NSEQ * c:NSEQ * (c + 1)].reshape(NSM, D)
        xall = np.concatenate([meta, x_prompt[c], xs], axis=0)
        sp = state_pool[:, NSEQ * c:NSEQ * (c + 1)]
        spT = np.ascontiguousarray(sp.transpose(0, 3, 1, 2).reshape(4, 4, 128, NSEQ * 15))
        m = dict(shared)
        m.update({
            "xT": np.ascontiguousarray(xall.T),
            "vecs": vecs, "cst": cst,
            "state_hgrn": np.ascontiguousarray(state_hgrn[:, NSEQ * c:NSEQ * (c + 1)]),
            "state_poolT": spT,
        })
        maps.append(m)
    return maps


def assemble(results, ncores):
    y_prompt = np.zeros((ncores, 2048, D), np.float32)
    y_sample = np.zeros((ncores * NSEQ, 4, D), np.float32)
    hg_p = np.zeros((4, ncores, NH, 128, 128), np.float32)
    pool_p = np.zeros((4, ncores, 15, 512), np.float32)
    hg_s = np.zeros((4, ncores * NSEQ, NH, 128, 128), np.float32)
    pool_s = np.zeros((4, ncores * NSEQ, 15, 512), np.float32)
    for c, r in enumerate(results):
        y = np.asarray(r["yT"]).T
        y_prompt[c] = y[16:NPR]
        y_sample[NSEQ * c:NSEQ * (c + 1)] = y[NPR:].reshape(NSEQ, 4, D)
        hg_p[:, c] = np.asarray(r["hgp"])
        pool_p[:, c] = np.asarray(r["poolpT"]).reshape(4, 512, 15).transpose(0, 2, 1)
        hg_s[:, NSEQ * c:NSEQ * (c + 1)] = np.asarray(r["hgs"])
        pool_s[:, NSEQ * c:NSEQ * (c + 1)] = np.asarray(r["poolsT"]).reshape(4, 512, NSEQ, 15).transpose(0, 2, 3, 1)
    return (y_prompt, y_sample, hg_p, pool_p, hg_s, pool_s)


def kernel(**inputs):
    if "nc" not in _NC_CACHE:
        _NC_CACHE["nc"] = build_program(4)
    nc = _NC_CACHE["nc"]
    cores = list(range(NCORES))
    in_maps = make_in_maps(inputs, cores)
    res = run_bass_kernel_spmd(nc, in_maps, core_ids=cores)
    return assemble(res.results, NCORES)
```

```python
import numpy as np
import concourse.bass as bass
import concourse.mybir as mybir
from concourse.bass_utils import run_bass_kernel_spmd

F32 = mybir.dt.float32
BF16 = mybir.dt.bfloat16
U8 = mybir.dt.uint8
AF = mybir.ActivationFunctionType
ALU = mybir.AluOpType
AX = mybir.AxisListType

NCORES = 8
D = 1024
KC = 8
NPR = 2064
NSM = 64
T = NPR + NSM
NSEQ = 16
DFF = 2816
NH = 4
EPS = 1e-6
FT = [(0, 448), (448, 448), (896, 448), (1344, 448), (1792, 336)]
GROUPS = [(0, 4), (4, 4), (8, 4), (12, 4), (16, 4), (20, 2)]
NPAGE = 7
GS_LAT = 200.0
GS_PRIO_WIN = 0.0
GS_TRIES = [(0.0, 0)] + [(w, s) for s in range(1, 41) for w in (0.0, 60.0, 120.0)]
PAGE = 8192
V_NF1, V_NMX, V_NF2, V_NFIN, V_HGN, V_PSC, V_LBL, NV = 0, 32, 64, 96, 104, 120, 136, 152
C_ID, C_CM, C_SM, C_SMC, C_RMA, C_RMB, C_INV, NCST = 0, 128, 256, 320, 336, 848, 1168, 1184


class Op:
    __slots__ = ("eng", "fn", "deps", "dma", "sig", "count", "waits", "dj")

    def __init__(self, eng, fn, deps, dma):
        self.eng, self.fn, self.deps, self.dma = eng, fn, deps, dma
        self.sig = False
        self.count = 0
        self.waits = []
        self.dj = -1


class Prog:
    KQ = 8

    def __init__(self):
        self.ops = []
        self.lastw = {}
        self.readers = {}
        self.last_on = {}
        self.pending_dma = []

    def add(self, eng, fn, reads=(), writes=(), dma=False, extra=(), arena=False):
        idx = len(self.ops)
        if dma and arena:
            reads = list(reads) + ["phase"]
        deps = {}
        for r in reads:
            w = self.lastw.get(r)
            if w is not None:
                deps[w] = True
        for r in writes:
            w = self.lastw.get(r)
            if w is not None:
                deps.setdefault(w, False)
            for rd in self.readers.get(r, ()):
                if rd != idx:
                    deps.setdefault(rd, False)
        for e in extra:
            deps[e] = True
        for r in reads:
            self.readers.setdefault(r, []).append(idx)
        for r in writes:
            self.lastw[r] = idx
            self.readers[r] = []
        self.ops.append(Op(eng, fn, deps, dma))
        self.last_on[eng] = idx
        if dma and arena:
            self.pending_dma.append(idx)
        return idx

    def barrier(self):
        lasts = [v for v in self.last_on.values()]
        dmas = list(self.pending_dma)
        self.pending_dma = []
        for eng in ("pe", "act", "dve"):
            self.add(eng, lambda e: e.nop(), extra=lasts + dmas, writes=(["phase"] if eng == "dve" else []))

    def finalize(self):
        ops = self.ops
        for op in ops:
            per = {}
            waits = []
            for d, raw in op.deps.items():
                od = ops[d]
                if od.dma:
                    waits.append(d)
                    continue
                if od.eng == op.eng and not op.dma and op.eng == "pe":
                    continue
                if d > per.get(od.eng, -1):
                    per[od.eng] = d
            waits.extend(per.values())
            op.waits = waits
            for d in waits:
                ops[d].sig = True
        cnt = {}
        dj = {}
        for op in ops:
            if op.dma:
                op.dj = dj.get(op.eng, 0)
                dj[op.eng] = op.dj + 1
            elif op.sig:
                cnt[op.eng] = cnt.get(op.eng, 0) + 1
                op.count = cnt[op.eng]

    def emit(self, nc, block, esem, qsem):
        ops = self.ops
        KQ = self.KQ

        def token(d):
            od = ops[d]
            if od.dma:
                return qsem[od.eng][od.dj % KQ], 16 * (od.dj // KQ + 1)
            return esem[od.eng], od.count

        def run(engname):
            def body(e):
                waited = {}
                ndma = 0
                for op in ops:
                    if op.eng != engname:
                        continue
                    for d in op.waits:
                        sem, val = token(d)
                        if waited.get(id(sem), 0) >= val:
                            continue
                        e.wait_ge(sem, val)
                        waited[id(sem)] = val
                    if op.dma:
                        j = op.dj
                        sem = qsem[engname][j % KQ]
                        if j >= KQ:
                            val = 16 * (j // KQ)
                            if waited.get(id(sem), 0) < val:
                                e.wait_ge(sem, val)
                                waited[id(sem)] = val
                        ins = op.fn(e)
                        ins.then_inc(sem, 16)
                        ndma = j + 1
                    else:
                        ins = op.fn(e)
                        if op.sig:
                            ins.then_inc(esem[engname], 1)
                for s in range(min(ndma, KQ)):
                    n_on = (ndma - 1 - s) // KQ + 1
                    val = 16 * n_on
                    sem = qsem[engname][s]
                    if waited.get(id(sem), 0) < val:
                        e.wait_ge(sem, val)
            return body

        block.tensor(run("pe"))
        block.scalar(run("act"))
        block.vector(run("dve"))
        block.gpsimd(run("pool"))
        block.sync(run("sp"))


def build_program(L=4):
    nc = bass.Bass("TRN2", target_bir_lowering=False)
    P = Prog()

    def dram(name, shape, kind):
        return nc.dram_tensor(name, list(shape), F32, kind=kind).ap()

    xT = dram("xT", [D, T], "ExternalInput")
    vecs_d = dram("vecs", [128, NV], "ExternalInput")
    cst_d = dram("cst", [128, NCST], "ExternalInput")
    wf_in = [dram("w_ffn1_in", [4, D, 2 * DFF], "ExternalInput"), dram("w_ffn2_in", [4, D, 2 * DFF], "ExternalInput")]
    wf_out = [dram("w_ffn1_out", [4, DFF, D], "ExternalInput"), dram("w_ffn2_out", [4, DFF, D], "ExternalInput")]
    w_in_d = dram("w_in", [4, D, 2560], "ExternalInput")
    w_out_d = dram("w_out", [4, D, D], "ExternalInput")
    poolw_d = dram("pool_w", [4, 4, 128, 128], "ExternalInput")
    shg_d = dram("state_hgrn", [4, NSEQ, NH, 128, 128], "ExternalInput")
    spl_d = dram("state_poolT", [4, 4, 128, NSEQ * 15], "ExternalInput")
    yT = dram("yT", [D, T], "ExternalOutput")
    hgp_d = dram("hgp", [4, NH, 128, 128], "ExternalOutput")
    plp_d = dram("poolpT", [4, 4, 128, 15], "ExternalOutput")
    hgs_d = dram("hgs", [4, NSEQ, NH, 128, 128], "ExternalOutput")
    pls_d = dram("poolsT", [4, 4, 128, NSEQ * 15], "ExternalOutput")

    ARENA = 80448
    import contextlib
    with contextlib.ExitStack() as es:
        def sb(name, shape, dt):
            return es.enter_context(nc.sbuf_tensor(name, list(shape), dt))

        xres = sb("xres", [128, KC, T], F32)
        vec = sb("vec", [128, NV], F32)
        lbw = sb("lbw", [128, 64], F32)
        nc1t = sb("nc1t", [128, 16], F32)
        lbs = sb("lbs", [128, 8], F32)
        identb = sb("identb", [128, 128], BF16)
        onesb = sb("onesb", [128, 128], BF16)
        zerob = sb("zerob", [128, 128], BF16)
        cmaskb = sb("cmaskb", [128, 4, 128], BF16)
        smaskb = sb("smaskb", [64, 4, 64], BF16)
        smcol = sb("smcol", [64, 16], F32)
        rmA = sb("rmA", [128, 128], F32)
        rmB = sb("rmB", [128, 80], F32)
        invc = sb("invc", [128, 16], F32)
        poolw = sb("poolw", [128, 2, 4, 128], BF16)
        wpool = sb("wpool", [128, NPAGE * PAGE], U8)
        arena = sb("arena", [128, ARENA], U8)
        ps = es.enter_context(nc.psum_tensor("ps", [128, 8 * 512], F32))
        esem = {k: es.enter_context(nc.semaphore("e_" + k)) for k in ("pe", "act", "dve", "pool", "sp")}
        qsem = {k: [es.enter_context(nc.semaphore("q_%s%d" % (k, i))) for i in range(Prog.KQ)]
                for k in ("pool", "sp", "act")}

        class Carver:
            def __init__(self):
                self.off = 0

            def get(self, nbytes, dt, pattern=None, parts=128, **kw):
                nbytes = (nbytes + 63) // 64 * 64
                assert self.off + nbytes <= ARENA, ("arena overflow", self.off + nbytes)
                v = arena[0:parts, self.off:self.off + nbytes].bitcast(dt)
                self.off += nbytes
                if pattern:
                    v = v.rearrange(pattern, **kw)
                return v

        cf = Carver()
        h_full = cf.get(KC * T * 2, BF16, "p (k t) -> p k t", k=KC)
        aring = cf.get(3 * 4 * 448 * 2, BF16, "p (s j t) -> p s j t", s=3, j=4)
        sgt = cf.get(2 * 448 * 2, BF16, "p (s t) -> p s t", s=2)
        f_nsq = cf.get(KC * 448 * 2, BF16, "p (k t) -> p k t", k=KC)
        f_nln = cf.get(448 * 4, F32)
        f_nrstd = cf.get(448 * 4, F32)
        cstage = cf.get(NCST * 4, F32)
        f_ytmp = arena[:, 0:KC * 448 * 4].bitcast(F32).rearrange("p (k t) -> p k t", k=KC)
        cm = Carver()
        m_nsq = cm.get(KC * 128 * 2, BF16, "p (k t) -> p k t", k=KC)
        m_nln = cm.get(128 * 4, F32)
        m_nrstd2 = cm.get(2 * 128 * 4, F32, "p (s t) -> p s t", s=2)
        hm_0 = cm.get(KC * 128 * 2, BF16, "p (k t) -> p k t", k=KC)
        qb_0 = cm.get(512 * 2, BF16, "p (h t) -> p h t", h=4)
        qt2 = cm.get(2 * 512 * 2, BF16, "p (s h t) -> p s h t", s=2, h=4)
        th_0 = cm.get(512 * 4, F32, "p (h t) -> p h t", h=4)
        kf_0 = cm.get(512 * 4, F32, "p (h t) -> p h t", h=4)
        gl_0 = cm.get(512 * 4, F32, "p (h t) -> p h t", h=4)
        bb_0 = cm.get(512 * 4, F32, "p (h t) -> p h t", h=4)
        Ep2 = cm.get(2 * 512 * 4, F32, "p (s h t) -> p s h t", s=2, h=4)
        kt2 = cm.get(2 * 512 * 2, BF16, "p (s h t) -> p s h t", s=2, h=4)
        gate2 = cm.get(2 * 512 * 2, BF16, "p (s h t) -> p s h t", s=2, h=4)
        U = cm.get(4 * 144 * 4, F32, "p (g t) -> p g t", g=4)
        _wab_off = cm.off
        wa = cm.get(160 * 4, F32)
        wb = cm.get(160 * 4, F32)
        osum = arena[:, _wab_off:_wab_off + 1024].bitcast(F32)
        wfix = cm.get(16 * 4, F32)
        dd = cm.get(512 * 2, BF16, "p (g t) -> p g t", g=4)
        vtok2 = cm.get(2 * 512 * 2, BF16, "p (s c) -> p s c", s=2)
        ktok = cm.get(512 * 2, BF16)
        scm = cm.get(512 * 2, BF16)
        osq = cm.get(512 * 2, BF16)
        olnv = cm.get(512 * 4, F32)
        orstd = cm.get(512 * 4, F32)
        t1 = cm.get(512 * 2, BF16)
        mixed2 = cm.get(3 * KC * 128 * 2, BF16, "p (s k t) -> p s k t", s=3, k=KC)
        Sst = cm.get(512 * 4, F32, "p (h v) -> p h v", h=4)
        tmpS = cm.get(512 * 4, F32, "p (h v) -> p h v", h=4)
        Sbf = cm.get(4 * 512 * 2, BF16, "p (s h v) -> p s h v", s=4, h=4)
        _ss_off = cm.off
        Ss = cm.get(4 * 512 * 4, F32, "p (i h v) -> p i h v", i=4, h=4)
        Ssb = cm.get(4 * 512 * 2, BF16, "p (i h v) -> p i h v", i=4, h=4)
        def _alias(off, nbytes, dt):
            return arena[:, off:off + nbytes].bitcast(dt).rearrange("p (h t) -> p h t", h=4)
        th_1 = _alias(_ss_off, 2048, F32)
        kf_1 = _alias(_ss_off + 2048, 2048, F32)
        gl_1 = _alias(_ss_off + 4096, 2048, F32)
        bb_1 = _alias(_ss_off + 6144, 2048, F32)
        qb_1 = _alias(_ss_off + 8192, 1024, BF16)
        th2, kf2, gl2, bb2, qb2 = (th_0, th_1), (kf_0, kf_1), (gl_0, gl_1), (bb_0, bb_1), (qb_0, qb_1)
        ODDKEYS = [("th", 1, h) for h in range(4)] + [("kf", 1), ("gl", 1), ("bb", 1), ("Em", 1), ("qb", 1)]
        _km_off = cm.off
        km = cm.get(4 * 512 * 2, BF16, "p (i c) -> p i c", i=4)
        hm_1 = arena[:, _km_off:_km_off + KC * 128 * 2].bitcast(BF16).rearrange("p (k t) -> p k t", k=KC)
        hm2 = (hm_0, hm_1)
        uext = cm.get(4 * 16 * 19 * 4, F32, "p (g i r) -> p g i r", g=4, i=16)
        ustage = cm.get(4 * 240 * 4, F32, "p (g i r) -> p g i r", g=4, i=16)
        oint = cm.get(256 * 4, F32)

        def page(pg, dt=BF16):
            return wpool[:, pg * PAGE:(pg + 1) * PAGE].bitcast(dt)

        bank_ctr = [0]
        pinned = set()

        pool_ctr = {"s1": 0, "s2": 0, "s3": 0, "t2": 0}
        POOLS = {"s1": (0, 4), "s2": (4, 2), "s3": (6, 1), "t2": (0, 6)}
        FILL_BANK = 7

        def newbank(pool="all"):
            if pool == "all":
                while True:
                    b = bank_ctr[0] % 8
                    bank_ctr[0] += 1
                    if b not in pinned:
                        return b
            base, cnt = POOLS[pool]
            while True:
                b = base + pool_ctr[pool] % cnt
                pool_ctr[pool] += 1
                if b not in pinned:
                    return b

        def psb(b, n=512, parts=128):
            return ps[0:parts, b * 512:b * 512 + n]

        def xkeys(kcs, t0, t1):
            return [("x", k, u) for k in kcs for u in range(t0 // 64, (t1 + 63) // 64)]

        ALLK = range(KC)

        xTv = xT.rearrange("(k p) t -> p k t", p=128)
        P.add("sp", lambda e: e.dma_start(out=vec[:, :], in_=vecs_d[:, :]), writes=["vec"], dma=True)
        P.add("sp", lambda e: e.dma_start(out=cstage[:, :], in_=cst_d[:, :]), writes=["cstage"], dma=True, arena=True)
        for (t0_, tn_) in FT:
            P.add("sp", (lambda e, t0_=t0_, tn_=tn_: e.dma_start(out=xres[:, :, t0_:t0_ + tn_], in_=xTv[:, :, t0_:t0_ + tn_])),
                  writes=xkeys(ALLK, t0_, t0_ + tn_), dma=True)
        P.add("dve", lambda e: e.tensor_copy(out=identb[:, :], in_=cstage[:, C_ID:C_ID + 128]), reads=["cstage"])
        for h in range(4):
            P.add("dve", (lambda e, h=h: e.tensor_copy(out=cmaskb[:, h, :], in_=cstage[:, C_CM:C_CM + 128])),
                  reads=["cstage"])
            P.add("dve", (lambda e, h=h: e.tensor_copy(out=smaskb[:, h, :], in_=cstage[0:64, C_SM:C_SM + 64])),
                  reads=["cstage"])
        P.add("dve", lambda e: e.tensor_copy(out=smcol[:, :], in_=cstage[0:64, C_SMC:C_SMC + 16]), reads=["cstage"])
        P.add("dve", lambda e: e.tensor_copy(out=rmA[:, :], in_=cstage[:, C_RMA:C_RMA + 128]), reads=["cstage"])
        P.add("dve", lambda e: e.tensor_copy(out=rmB[:, :], in_=cstage[:, C_RMB:C_RMB + 80]), reads=["cstage"])
        P.add("dve", lambda e: e.tensor_copy(out=invc[:, :], in_=cstage[:, C_INV:C_INV + 16]), reads=["cstage"],
              writes=["consts"])
        P.add("dve", lambda e: e.memset(onesb[:, :], 1.0), writes=["onesb"])
        P.add("dve", lambda e: e.memset(zerob[:, :], 0.0), writes=["zerob"])
        e_t = lbw[:, 0:16]
        p_t = lbw[:, 16:32]
        lbv = lbw[:, 32:48]
        c1t = lbw[:, 48:64]
        P.add("act", lambda e: e.activation(out=e_t, in_=vec[:, V_LBL:V_LBL + 16], func=AF.Exp),
              reads=["vec"], writes=["lb_e"])
        P.add("dve", lambda e: e.tensor_reduce(out=lbs[:, 0:4], in_=e_t.rearrange("p (h l) -> p h l", h=4),
                                               axis=AX.X, op=ALU.add), reads=["lb_e"], writes=["lb_s"])
        P.add("dve", lambda e: e.reciprocal(out=lbs[:, 4:8], in_=lbs[:, 0:4]), reads=["lb_s"], writes=["lb_r"])
        P.add("dve", lambda e: e.tensor_tensor(out=p_t.rearrange("p (h l) -> p h l", h=4),
                                               in0=e_t.rearrange("p (h l) -> p h l", h=4),
                                               in1=lbs[:, 4:8].unsqueeze(2).broadcast_to([128, 4, 4]), op=ALU.mult),
              reads=["lb_e", "lb_r"], writes=["lb_p"])
        lb3 = lbv.rearrange("p (h l) -> p h l", h=4)
        p3 = p_t.rearrange("p (h l) -> p h l", h=4)
        P.add("dve", lambda e: e.memset(lbv, 0.0), writes=["lb_v"])
        P.add("dve", lambda e: e.tensor_copy(out=lb3[:, :, 1], in_=p3[:, :, 1]), reads=["lb_p", "lb_v"], writes=["lb_v1"])
        P.add("dve", lambda e: e.tensor_tensor(out=lb3[:, :, 2], in0=lb3[:, :, 1], in1=p3[:, :, 2], op=ALU.add),
              reads=["lb_v1", "lb_p"], writes=["lb_v2"])
        P.add("dve", lambda e: e.tensor_tensor(out=lb3[:, :, 3], in0=lb3[:, :, 2], in1=p3[:, :, 3], op=ALU.add),
              reads=["lb_v2", "lb_p"], writes=["lb_v3"])
        P.add("dve", lambda e: e.tensor_scalar(out=c1t, in0=lbv, scalar1=-0.5, scalar2=0.5, op0=ALU.mult, op1=ALU.add),
              reads=["lb_v", "lb_v1", "lb_v2", "lb_v3"], writes=["c1"])
        P.add("dve", lambda e: e.tensor_scalar(out=nc1t[:, :], in0=c1t, scalar1=-1.0, scalar2=None, op0=ALU.mult),
              reads=["c1"], writes=["nc1"])

        def vcol(off):
            return vec[:, off:off + 1]

        def dma_ffn_group(l, which, g):
            j0, cnt = GROUPS[g]
            slot = g % 2
            pg = (3 * slot, 3 * slot + 1, 3 * slot + 2)
            Wi = wf_in[which][l].rearrange("(k p) n -> p k n", p=128)
            Wo = wf_out[which][l].rearrange("(j p) n -> p j n", p=128)
            wgv = page(pg[0]).rearrange("p (k n) -> p k n", k=8)
            wuv = page(pg[1]).rearrange("p (k n) -> p k n", k=8)
            wov = page(pg[2]).rearrange("p (j n) -> p j n", j=4)
            n = cnt * 128
            P.add("pool", lambda e: e.dma_start(out=wgv[:, :, 0:n], in_=Wi[:, :, j0 * 128:j0 * 128 + n]),
                  writes=[("W", pg[0])], dma=True)
            P.add("pool", lambda e: e.dma_start(out=wuv[:, :, 0:n], in_=Wi[:, :, DFF + j0 * 128:DFF + j0 * 128 + n]),
                  writes=[("W", pg[1])], dma=True)
            P.add("pool", lambda e: e.dma_start(out=wov[:, 0:cnt, :], in_=Wo[:, j0:j0 + cnt, :]),
                  writes=[("W", pg[2])], dma=True)

        MIXPG = {"q": 6, "f": 0, "i": 1, "g": 2, "pool": 3, "oA": 4, "oB": 5}

        def dma_mix_w(l, which):
            Wi = w_in_d[l].rearrange("(k p) n -> p k n", p=128)
            Wo = w_out_d[l].rearrange("(k p) n -> p k n", p=128)
            cgi = {"q": 0, "f": 1, "i": 2, "g": 3, "pool": 4}
            if which in cgi:
                pg = MIXPG[which]
                c0 = cgi[which] * 512
                dst = page(pg).rearrange("p (k n) -> p k n", k=8)
                P.add("pool", lambda e: e.dma_start(out=dst, in_=Wi[:, :, c0:c0 + 512]), writes=[("W", pg)], dma=True)
            else:
                pg = MIXPG[which]
                k0 = 0 if which == "oA" else 4
                dst = page(pg).rearrange("p (k n) -> p k n", k=4)
                P.add("pool", lambda e: e.dma_start(out=dst, in_=Wo[:, k0:k0 + 4, :]), writes=[("W", pg)], dma=True)

        def dma_poolw(l):
            src = poolw_d[l].rearrange("g c d -> c g d")
            P.add("pool", lambda e: e.dma_start(out=poolw[:, l % 2, :, :], in_=src), writes=[("poolw", l % 2)], dma=True)

        def norm_stats(t0, tn, nsq, nln, nrstd, rkey, pool="all"):
            bk = newbank(pool)
            PA("act", lambda e: e.activation(out=nsq[:, :, 0:tn], in_=xres[:, :, t0:t0 + tn], func=AF.Square),
               reads=xkeys(ALLK, t0, t0 + tn), writes=["nsq"])

            def mm(e):
                ins = None
                for k in range(KC):
                    ins = e.matmul(psb(bk, tn), lhsT=onesb[:, :], rhs=nsq[:, k, 0:tn], start=(k == 0), stop=(k == KC - 1))
                return ins
            PA("pe", mm, reads=["nsq", "onesb"], writes=[("ps", bk)])
            PA("act", lambda e: e.activation(out=nln[:, 0:tn], in_=psb(bk, tn), func=AF.Ln, bias=EPSB[:, 0:1], scale=1.0 / D),
               reads=[("ps", bk), "epsb"], writes=["nln"])
            PA("act", lambda e: e.activation(out=nrstd[:, 0:tn], in_=nln[:, 0:tn], func=AF.Exp, scale=-0.5),
               reads=["nln"], writes=[rkey])

        def norm_apply(t0, tn, gain_off, nrstd, rkey, out_fn, out_keys_fn):
            for k in range(KC):
                PA("dve", (lambda e, k=k: e.scalar_tensor_tensor(
                    out=out_fn(k), in0=xres[:, k, t0:t0 + tn], scalar=vcol(gain_off + k), in1=nrstd[:, 0:tn],
                    op0=ALU.mult, op1=ALU.mult)),
                    reads=xkeys([k], t0, t0 + tn) + [rkey, "vec"], writes=out_keys_fn(k))

        def norm_tile(t0, tn, gain_off, nsq, nln, nrstd, out_fn, out_keys_fn, pool="all"):
            norm_stats(t0, tn, nsq, nln, nrstd, "nrstd", pool)
            norm_apply(t0, tn, gain_off, nrstd, "nrstd", out_fn, out_keys_fn)

        epsb_t = sb("epsb", [128, 2], F32)
        EPSB = epsb_t
        P.add("dve", lambda e: e.memset(epsb_t[:, :], EPS), writes=["epsb"])

        def ffn(l, which):
            gain_off = (V_NF1 if which == 0 else V_NF2) + l * 8
            normed = set()

            def need_norm(n):
                if n < len(FT) and n not in normed:
                    normed.add(n)
                    t0, tn = FT[n]
                    norm_tile(t0, tn, gain_off, f_nsq, f_nln, f_nrstd,
                              (lambda k, t0=t0, tn=tn: h_full[:, k, t0:t0 + tn]),
                              (lambda k, n=n: [("h", k, n)]))
            need_norm(0)
            need_norm(1)
            sg_ctr = [0]
            def ffn_group(g, j0, cnt):
                if g + 1 < len(GROUPS):
                    dma_ffn_group(l, which, g + 1)
                else:
                    if which == 0:
                        pass
                    elif l + 1 < L:
                        dma_ffn_group(l + 1, 0, 0)
                if which == 0 and g == 0:
                    dma_mix_w(l, "q")
                    dma_poolw(l)
                if which == 0 and g == len(GROUPS) - 1:
                    dma_mix_w(l, "f")
                    dma_mix_w(l, "i")
                    dma_mix_w(l, "g")
                slot = g % 2
                pg = (3 * slot, 3 * slot + 1, 3 * slot + 2)
                wgv = page(pg[0]).rearrange("p (k n) -> p k n", k=8)
                wuv = page(pg[1]).rearrange("p (k n) -> p k n", k=8)
                wov = page(pg[2]).rearrange("p (j n) -> p j n", j=4)

                def emit_in(n):
                    t0, tn = FT[n]
                    ar = n % 3
                    for j in range(cnt):
                        bg = newbank()
                        bu = newbank()

                        def mmg(e, j=j, bg=bg):
                            ins = None
                            for k in range(KC):
                                ins = e.matmul(psb(bg, tn), lhsT=wgv[:, k, j * 128:(j + 1) * 128],
                                               rhs=h_full[:, k, t0:t0 + tn], start=(k == 0), stop=(k == KC - 1))
                            return ins

                        def mmu(e, j=j, bu=bu):
                            ins = None
                            for k in range(KC):
                                ins = e.matmul(psb(bu, tn), lhsT=wuv[:, k, j * 128:(j + 1) * 128],
                                               rhs=h_full[:, k, t0:t0 + tn], start=(k == 0), stop=(k == KC - 1))
                            return ins
                        hk = [("h", k, n) for k in range(KC)]
                        P.add("pe", mmg, reads=hk + [("W", pg[0])], writes=[("ps", bg)])
                        P.add("pe", mmu, reads=hk + [("W", pg[1])], writes=[("ps", bu)])
                        s2 = sg_ctr[0] % 2
                        sg_ctr[0] += 1
                        P.add("act", (lambda e, bg=bg, s2=s2: e.activation(out=sgt[:, s2, 0:tn], in_=psb(bg, tn), func=AF.Silu)),
                              reads=[("ps", bg)], writes=[("sgt", s2)])
                        P.add("dve", (lambda e, bu=bu, s2=s2, j=j: e.tensor_tensor(
                            out=aring[:, ar, j, 0:tn], in0=psb(bu, tn), in1=sgt[:, s2, 0:tn], op=ALU.mult)),
                            reads=[("ps", bu), ("sgt", s2)], writes=[("a", ar, j)])

                def emit_out(n):
                    t0, tn = FT[n]
                    ar = n % 3
                    for m in range(KC):
                        bo = newbank()

                        def mmo(e, m=m, bo=bo):
                            ins = None
                            for j in range(cnt):
                                ins = e.matmul(psb(bo, tn), lhsT=wov[:, j, m * 128:(m + 1) * 128],
                                               rhs=aring[:, ar, j, 0:tn], start=(j == 0), stop=(j == cnt - 1))
                            return ins
                        P.add("pe", mmo, reads=[("a", ar, j) for j in range(cnt)] + [("W", pg[2])], writes=[("ps", bo)])
                        P.add("dve", (lambda e, m=m, bo=bo: e.scalar_tensor_tensor(
                            out=xres[:, m, t0:t0 + tn], in0=psb(bo, tn), scalar=0.5, in1=xres[:, m, t0:t0 + tn],
                            op0=ALU.mult, op1=ALU.add)),
                            reads=[("ps", bo)] + xkeys([m], t0, t0 + tn), writes=xkeys([m], t0, t0 + tn))

                for n in range(len(FT) + 1):
                    if n < len(FT):
                        emit_in(n)
                        need_norm(n + 2)
                    if n >= 1:
                        emit_out(n - 1)
                if which == 0 and g == len(GROUPS) - 1:
                    dma_mix_w(l, "pool")
                    dma_mix_w(l, "oA")
                    dma_mix_w(l, "oB")

            for g, (j0, cnt) in enumerate(GROUPS):
                ffn_group(g, j0, cnt)

        class Rec:
            def __init__(self):
                self.items = []

            def add(self, eng, fn, reads=(), writes=(), dma=False, extra=(), arena=False):
                self.items.append((eng, fn, reads, writes, dma, extra, arena))

        sink = [P]

        def PA(*a, **k):
            return sink[0].add(*a, **k)

        class _FakeIns:
            def then_inc(self, *a, **k):
                return self

        class FakeEng:
            S_SET = (AF.Silu, AF.Tanh)
            B_SET = (AF.Ln, AF.Exp)

            def __init__(self):
                self.cost = 0.0
                self.aset = None

            @staticmethod
            def _n(ap):
                n = 1
                for s in ap.shape[1:]:
                    n *= s
                return n

            def matmul(self, out, lhsT, rhs, **k):
                self.cost += max(self._n(rhs), 128) / 2.4 + 4
                return _FakeIns()

            def transpose(self, out, in_, identity, **k):
                self.cost += 60
                return _FakeIns()

            def activation(self, out, in_, func, **k):
                self.cost += 220 + self._n(out) / 1.2
                if func in self.S_SET:
                    self.aset = "S"
                elif func in self.B_SET:
                    self.aset = "B"
                return _FakeIns()

            def copy(self, out, in_, **k):
                self.cost += 220 + self._n(out) / 1.2
                return _FakeIns()

            def _dve(self, out, f=1.0):
                self.cost += 110 + f * self._n(out) / 0.96
                return _FakeIns()

            def tensor_tensor(self, out, in0, in1, op, **k):
                return self._dve(out)

            def scalar_tensor_tensor(self, out, **k):
                return self._dve(out)

            def tensor_scalar(self, out, **k):
                return self._dve(out, 0.7)

            def tensor_copy(self, out, in_, **k):
                return self._dve(out, 0.7)

            def memset(self, ap, c):
                return self._dve(ap, 0.7)

            def tensor_tensor_scan(self, out, **k):
                return self._dve(out, 2.0)

            def tensor_reduce(self, out, **k):
                return self._dve(out)

            def reciprocal(self, out, in_):
                return self._dve(out)

            def dma_start(self, out, in_, **k):
                self.cost += 60
                return _FakeIns()

            def nop(self):
                self.cost += 20
                return _FakeIns()

        def merge(*recs):
            lists = [r.items for r in recs]
            rw = []
            for items in lists:
                Rk, Wk = set(), set()
                for it in items:
                    Rk.update(it[2])
                    Wk.update(it[3])
                rw.append((Rk, Wk))
            for i_ in range(len(rw)):
                for j_ in range(len(rw)):
                    if i_ != j_:
                        bad = rw[i_][1] & (rw[j_][0] | rw[j_][1])
                        assert not bad, ("concurrently merged lists share written resources", sorted(map(str, bad))[:8])
            info = {}
            per_eng = {}
            for li, items in enumerate(lists):
                lastw, readers = {}, {}
                for idx, it in enumerate(items):
                    eng, fn, reads, writes, dma = it[0], it[1], it[2], it[3], it[4]
                    deps = set()
                    for r in reads:
                        if r in lastw:
                            deps.add(lastw[r])
                    for r in writes:
                        if r in lastw:
                            deps.add(lastw[r])
                        deps.update(readers.get(r, ()))
                    deps.discard(idx)
                    for r in reads:
                        readers.setdefault(r, []).append(idx)
                    for r in writes:
                        lastw[r] = idx
                        readers[r] = []
                    fk = FakeEng()
                    fn(fk)
                    info[(li, idx)] = (eng, fk.cost, fk.aset, deps, dma)
                    per_eng.setdefault((li, eng), []).append(idx)
            ptr = {k: 0 for k in per_eng}
            free = {}
            finish = {}
            cur_set = [None]
            total = sum(len(x) for x in lists)
            order = []
            LAT = 150.0
            engs = sorted({k[1] for k in per_eng})
            while len(order) < total:
                best = None
                for eng in engs:
                    for li in range(len(lists) - 1, -1, -1):
                        lst = per_eng.get((li, eng))
                        if not lst or ptr[(li, eng)] >= len(lst):
                            continue
                        idx = lst[ptr[(li, eng)]]
                        e_, cost, aset, deps, dma = info[(li, idx)]
                        ok = True
                        ready = free.get(eng, 0.0)
                        for d in deps:
                            f = finish.get((li, d))
                            if f is None:
                                ok = False
                                break
                            ready = max(ready, f + LAT)
                        if not ok:
                            continue
                        pen = 0.0
                        if eng == "act" and aset and cur_set[0] and aset != cur_set[0]:
                            pen = 1300.0
                        key = (ready + pen, -li)
                        if best is None or key < best[0]:
                            best = (key, li, idx, eng, ready + pen, cost, aset, dma)
                _, li, idx, eng, start, cost, aset, dma = best
                ptr[(li, eng)] += 1
                if dma:
                    free[eng] = start + cost
                    finish[(li, idx)] = start + 2500.0
                else:
                    free[eng] = start + cost
                    finish[(li, idx)] = start + cost
                if eng == "act" and aset:
                    cur_set[0] = aset
                order.append((li, idx))
            for li, idx in order:
                P.add(*lists[li][idx])

        def gsched(items):
            n = len(items)
            deps = [None] * n
            lastw, readers = {}, {}
            cost = [0.0] * n
            aset = [None] * n
            for idx, it in enumerate(items):
                eng, fn, reads, writes = it[0], it[1], it[2], it[3]
                dset = set()
                for r in reads:
                    if r in lastw:
                        dset.add(lastw[r])
                for r in writes:
                    if r in lastw:
                        dset.add(lastw[r])
                    dset.update(readers.get(r, ()))
                dset.discard(idx)
                for r in reads:
                    readers.setdefault(r, []).append(idx)
                for r in writes:
                    lastw[r] = idx
                    readers[r] = []
                deps[idx] = dset
                fk = FakeEng()
                fn(fk)
                cost[idx] = fk.cost
                aset[idx] = fk.aset
            succ = [[] for _ in range(n)]
            indeg = [0] * n
            for i_, ds in enumerate(deps):
                indeg[i_] = len(ds)
                for d in ds:
                    succ[d].append(i_)
            LAT = GS_LAT
            FILL_MIN, FILL_MARGIN = 1200.0, 500.0
            bl0 = [0.0] * n
            for i_ in range(n - 1, -1, -1):
                m_ = 0.0
                for s_ in succ[i_]:
                    v_ = bl0[s_] + (LAT if items[s_][0] != items[i_][0] else 60.0)
                    if v_ > m_:
                        m_ = v_
                bl0[i_] = m_ + (2500.0 if items[i_][4] else cost[i_])
            indeg0 = list(indeg)

            def simulate(PRIO_WIN, seed):
                import random
                rng = random.Random(seed)
                bl = bl0 if seed == 0 else [v * (1.0 + 0.02 * rng.random()) for v in bl0]
                indeg = list(indeg0)
                ready = {}
                rt = [0.0] * n
                for i_ in range(n):
                    if indeg[i_] == 0:
                        ready.setdefault(items[i_][0], []).append(i_)
                free = {}
                finish = [0.0] * n
                cur_set = None
                order = []
                start_t = [0.0] * n
                while len(order) < n:
                    best = None
                    for eng, lst in ready.items():
                        fe = free.get(eng, 0.0)
                        cands = []
                        est = None
                        for i_ in lst:
                            st = rt[i_] if rt[i_] > fe else fe
                            if eng == "act" and aset[i_] and cur_set and aset[i_] != cur_set:
                                st += 1300.0
                            cands.append((st, i_))
                            if est is None or st < est:
                                est = st
                        if est is None:
                            continue
                        pick = None
                        for st, i_ in cands:
                            if st <= est + PRIO_WIN:
                                k_ = (-bl[i_], st, i_)
                                if pick is None or k_ < pick[0]:
                                    pick = (k_, st, i_)
                        key = (pick[1], pick[2])
                        if best is None or key < best:
                            best = key
                    st, i_ = best
                    start_t[i_] = st
                    eng = items[i_][0]
                    ready[eng].remove(i_)
                    dma = items[i_][4]
                    free[eng] = st + cost[i_]
                    finish[i_] = st + (2500.0 if dma else cost[i_])
                    if eng == "act" and aset[i_]:
                        cur_set = aset[i_]
                    order.append(i_)
                    for s_ in succ[i_]:
                        indeg[s_] -= 1
                        lat = LAT if items[s_][0] != eng else 60.0
                        if finish[i_] + lat > rt[s_]:
                            rt[s_] = finish[i_] + lat
                        if indeg[s_] == 0:
                            ready.setdefault(items[s_][0], []).append(s_)
                return max(finish), order, start_t

            bestres = None
            for win_, seed_ in GS_TRIES:
                res = simulate(win_, seed_)
                if bestres is None or res[0] < bestres[0]:
                    bestres = res
            _, order, start_t = bestres
            z_rhs = zerob[:, :].unsqueeze(1).broadcast_to([128, 4, 128])

            def filler(k):
                def f(e):
                    ins = None
                    for _ in range(k):
                        ins = e.matmul(psb(FILL_BANK).rearrange("p (a b) -> p a b", a=4), lhsT=zerob[:, :], rhs=z_rhs, start=True, stop=True)
                    return ins
                return f
            pe_end = None
            for i_ in order:
                it = items[i_]
                if it[0] == "pe":
                    st = start_t[i_]
                    if pe_end is not None and st - pe_end > FILL_MIN:
                        nf = int((st - pe_end - FILL_MARGIN) / 217.0)
                        while nf > 0:
                            k = min(nf, 4)
                            P.add("pe", filler(k), reads=["zerob"], writes=[("ps", FILL_BANK)])
                            nf -= k
                    pe_end = st + cost[i_]
                P.add(*it)

        def mixing(l):
            gain_off = V_NMX + l * 8
            w_q = page(MIXPG["q"]).rearrange("p (k n) -> p k n", k=8)
            w_f = page(MIXPG["f"]).rearrange("p (k n) -> p k n", k=8)
            w_i = page(MIXPG["i"]).rearrange("p (k n) -> p k n", k=8)
            w_g = page(MIXPG["g"]).rearrange("p (k n) -> p k n", k=8)
            w_p = page(MIXPG["pool"]).rearrange("p (k n) -> p k n", k=8)
            w_oA = page(MIXPG["oA"]).rearrange("p (k n) -> p k n", k=4)
            w_oB = page(MIXPG["oB"]).rearrange("p (k n) -> p k n", k=4)
            pw = poolw[:, l % 2, :, :]
            hgn = lambda h: vcol(V_HGN + l * 4 + h)
            psc = lambda g: vcol(V_PSC + l * 4 + g)
            c1c = lambda h: lbw[:, 48 + h * 4 + l:48 + h * 4 + l + 1]
            nc1c = lambda h: nc1t[:, h * 4 + l:h * 4 + l + 1]

            P.add("dve", lambda e: e.memset(Sst.rearrange("p h v -> p (h v)"), 0.0), writes=["S"])
            P.add("dve", lambda e: e.memset(Sbf[:, 0, :, :].rearrange("p h v -> p (h v)"), 0.0), writes=[("Sbf", 0)])
            P.add("dve", lambda e: e.memset(U[:, :, 0:16], 0.0), writes=["Uhalo"])
            sbf_ctr = [0]

            def zproj(wv, cg_key, tn, evac, pool, hm, hmkey):
                bk = newbank(pool)
                for h in range(4):
                    def mm(e, h=h, bk=bk):
                        ins = None
                        for k in range(KC):
                            ins = e.matmul(ps[:, bk * 512 + h * tn:bk * 512 + (h + 1) * tn], lhsT=wv[:, k, h * 128:(h + 1) * 128],
                                           rhs=hm[:, k, 0:tn], start=(k == 0), stop=(k == KC - 1))
                        return ins
                    PA("pe", mm, reads=[hmkey, ("W", MIXPG[cg_key])] + ([("ps", bk)] if h else []), writes=[("ps", bk)])
                evac(bk)

            def hgrn_part(par, c_lo, ncols, chunks, pool):
                ktp, qtp, Epp, vtp = kt2[:, par], qt2[:, par], Ep2[:, par], vtok2[:, par]
                cs = slice(c_lo, c_lo + ncols)
                bT = newbank(pool)
                psT = ps[0:ncols, bT * 512:(bT + 1) * 512].bitcast(BF16)

                def tr(e):
                    ins = None
                    for h in range(4):
                        ins = e.transpose(out=psT[:, h * 128:(h + 1) * 128], in_=ktp[:, h, cs], identity=identb[:, :])
                    return ins
                PA("pe", tr, reads=[("kt", par), "consts"], writes=[("ps", bT)])
                PA("act", lambda e: e.copy(out=ktok[0:ncols, :], in_=psT[:, 0:512]), reads=[("ps", bT)], writes=["ktok"])
                bS = newbank(pool)

                def mms(e):
                    ins = None
                    for h in range(4):
                        ins = e.matmul(ps[0:ncols, bS * 512 + h * ncols:bS * 512 + (h + 1) * ncols],
                                       lhsT=ktp[:, h, cs], rhs=qtp[:, h, cs], start=True, stop=True)
                    return ins
                PA("pe", mms, reads=[("kt", par), ("qt", par)], writes=[("ps", bS)])
                scv = scm[0:ncols, 0:4 * ncols].rearrange("p (h t) -> p h t", h=4)
                PA("dve", lambda e: e.tensor_tensor(
                    out=scv, in0=ps[0:ncols, bS * 512:bS * 512 + 4 * ncols].rearrange("p (h t) -> p h t", h=4),
                    in1=cmaskb[0:ncols, :, 0:ncols], op=ALU.mult),
                    reads=[("ps", bS), "consts"], writes=["scm"])
                slots_in = []
                for (off, ln) in chunks:
                    cur = sbf_ctr[0] % 4
                    slots_in.append(cur)
                    bA = newbank(pool)
                    rows = slice(off, off + ln)

                    def mma(e, rows=rows, bA=bA):
                        ins = None
                        for h in range(4):
                            ins = e.matmul(ps[:, bA * 512 + h * 128:bA * 512 + (h + 1) * 128],
                                           lhsT=ktok[rows, h * 128:(h + 1) * 128], rhs=vtp[rows, h * 128:(h + 1) * 128],
                                           start=True, stop=True)
                        return ins
                    PA("pe", mma, reads=["ktok", ("vtok", par)], writes=[("ps", bA)])
                    PA("dve", (lambda e, bA=bA: e.tensor_tensor(
                        out=tmpS.rearrange("p h v -> p (h v)"), in0=psb(bA), in1=Sst.rearrange("p h v -> p (h v)"), op=ALU.add)),
                        reads=[("ps", bA), "S"], writes=["tmpS"])
                    lastcol = c_lo + off + ln - 1
                    dec = Epp[:, :, lastcol:lastcol + 1].broadcast_to([128, 4, 128])
                    PA("dve", (lambda e, dec=dec: e.tensor_tensor(out=Sst[:, :, :], in0=tmpS[:, :, :], in1=dec, op=ALU.mult)),
                       reads=["tmpS", ("Ep", par)], writes=["S"])
                    nxt = (sbf_ctr[0] + 1) % 4
                    sbf_ctr[0] += 1
                    PA("act", (lambda e, nxt=nxt: e.copy(out=Sbf[:, nxt, :, :].rearrange("p h v -> p (h v)"),
                                                       in_=Sst.rearrange("p h v -> p (h v)"))),
                       reads=["S"], writes=[("Sbf", nxt)])
                bO = newbank(pool)

                def mmo(e):
                    ins = None
                    for h in range(4):
                        base = bO * 512 + h * ncols
                        e.matmul(ps[:, base:base + ncols], lhsT=vtp[0:ncols, h * 128:(h + 1) * 128],
                                 rhs=scm[0:ncols, h * ncols:(h + 1) * ncols], start=True, stop=False)
                        for ci, (off, ln) in enumerate(chunks):
                            ins = e.matmul(ps[:, base + off:base + off + ln], lhsT=Sbf[:, slots_in[ci], h, :],
                                           rhs=qtp[:, h, c_lo + off:c_lo + off + ln], start=False,
                                           stop=(ci == len(chunks) - 1))
                    return ins
                PA("pe", mmo, reads=[("vtok", par), "scm", ("qt", par)] + [("Sbf", s) for s in slots_in], writes=[("ps", bO)])
                return bO

            def onorm(par, par3, src_ap, src_keys, n4, ncols, c_lo, pool):
                PA("act", lambda e: e.activation(out=osq[:, 0:n4], in_=src_ap, func=AF.Square),
                   reads=src_keys, writes=["osq"])
                bk = newbank(pool)
                PA("pe", lambda e: e.matmul(psb(bk, n4), lhsT=onesb[:, :], rhs=osq[:, 0:n4], start=True, stop=True),
                   reads=["osq", "onesb"], writes=[("ps", bk)])
                PA("act", lambda e: e.activation(out=olnv[:, 0:n4], in_=psb(bk, n4), func=AF.Ln, bias=EPSB[:, 0:1], scale=1.0 / 128),
                   reads=[("ps", bk), "epsb"], writes=["olnv"])
                PA("act", lambda e: e.activation(out=orstd[:, 0:n4], in_=olnv[:, 0:n4], func=AF.Exp, scale=-0.5),
                   reads=["olnv"], writes=["orstd"])
                PA("dve", lambda e: e.tensor_tensor(out=t1[:, 0:n4], in0=src_ap, in1=orstd[:, 0:n4], op=ALU.mult),
                   reads=src_keys + ["orstd"], writes=["t1"])
                for h in range(4):
                    PA("dve", (lambda e, h=h: e.scalar_tensor_tensor(
                        out=mixed2[:, par3, h, c_lo:c_lo + ncols], in0=t1[:, h * ncols:(h + 1) * ncols], scalar=hgn(h),
                        in1=gate2[:, par, h, c_lo:c_lo + ncols], op0=ALU.mult, op1=ALU.mult)),
                        reads=["t1", ("gate", par), "vec", ("mixed", par3, h)], writes=[("mixed", par3, h)])

            def windows(X, Wd, outs, xkey):
                for g in range(4):
                    w = 2 << g
                    bufs = [wa, wb]
                    src = X[:, g, 0:Wd]
                    cur = None
                    sh, bi, lo = 1, 0, 0
                    while sh < w:
                        dst = bufs[bi]
                        prev = src if cur is None else cur
                        lo2 = lo + sh
                        PA("dve", (lambda e, dst=dst, prev=prev, lo2=lo2, sh=sh: e.tensor_tensor(
                            out=dst[:, lo2:Wd], in0=prev[:, lo2:Wd], in1=prev[:, lo2 - sh:Wd - sh], op=ALU.add)),
                            reads=[xkey, ("wbuf", 1 - bi)], writes=[("wbuf", bi)])
                        cur, lo, sh, bi = dst, lo2, sh * 2, 1 - bi
                    dst_ap, c_lo, ncol = outs[g]
                    PA("dve", (lambda e, cur=cur, dst_ap=dst_ap, c_lo=c_lo, ncol=ncol, w=w, src=src: e.scalar_tensor_tensor(
                        out=dst_ap, in0=cur[:, c_lo:c_lo + ncol], scalar=1.0 / w, in1=src[:, c_lo:c_lo + ncol],
                        op0=ALU.mult, op1=ALU.subtract)),
                        reads=[xkey, ("wbuf", 0), ("wbuf", 1)], writes=[("dd", g)])
                    yield g, cur

            def stage1(ti):
                tail = (ti == 16)
                par = ti % 2
                par3 = ti % 3
                pool = "s1"
                t0 = ti * 128
                tn = 80 if tail else 128
                ktp, qtp, Epp, gtp = kt2[:, par], qt2[:, par], Ep2[:, par], gate2[:, par]
                th, kf, gl, bb, qb = th2[par], kf2[par], gl2[par], bb2[par], qb2[par]
                hm = hm2[par]
                Em = th
                norm_apply(t0, tn, gain_off, m_nrstd2[:, par], ("nrstd", par),
                           (lambda k: hm[:, k, 0:tn]), (lambda k: [("hm", par)]))

                def ps4(bk):
                    return ps[:, bk * 512:bk * 512 + 4 * tn].rearrange("p (h t) -> p h t", h=4)

                def evac_f(bk):
                    PA("act", (lambda e: e.activation(out=th[:, :, 0:tn], in_=ps4(bk), func=AF.Tanh, scale=0.5)),
                       reads=[("ps", bk)], writes=[("th", par, h) for h in range(4)])
                    for h in range(4):
                        PA("act", (lambda e, h=h: e.activation(out=kf[:, h, 0:tn], in_=th[:, h, 0:tn], func=AF.Identity, scale=nc1c(h), bias=c1c(h))),
                           reads=[("th", par, h), "c1", "nc1"], writes=[("kf", par)])
                zproj(w_f, "f", tn, evac_f, pool, hm, ("hm", par))
                zproj(w_p, "pool", tn, lambda bk: PA(
                    "act", (lambda e: e.copy(out=U[:, :, 16:16 + tn], in_=ps4(bk))),
                    reads=[("ps", bk)], writes=["Ux"]), pool, hm, ("hm", par))
                zproj(w_q, "q", tn, lambda bk: PA(
                    "act", (lambda e: e.activation(out=qb[:, :, 0:tn], in_=ps4(bk), func=AF.Silu)),
                    reads=[("ps", bk)], writes=[("qb", par)]), pool, hm, ("hm", par))
                zproj(w_g, "g", tn, lambda bk: PA(
                    "act", (lambda e: e.activation(out=gtp[:, :, 0:tn], in_=ps4(bk), func=AF.Silu)),
                    reads=[("ps", bk)], writes=[("gate", par)]), pool, hm, ("hm", par))
                nrow = 16 if tail else 128
                bV = newbank(pool)

                def mmv(e):
                    ins = None
                    for k in range(KC):
                        ins = e.matmul(ps[0:nrow, bV * 512:(bV + 1) * 512], lhsT=hm[:, k, 0:nrow], rhs=w_i[:, k, :],
                                       start=(k == 0), stop=(k == KC - 1))
                    return ins
                PA("pe", mmv, reads=[("hm", par), ("W", MIXPG["i"])], writes=[("ps", bV)])
                PA("dve", lambda e: e.tensor_copy(out=vtok2[0:nrow, par, :], in_=ps[0:nrow, bV * 512:(bV + 1) * 512]),
                   reads=[("ps", bV)], writes=[("vtok", par)])
                PA("act", lambda e: e.activation(out=gl[:, :, 0:tn], in_=kf[:, :, 0:tn], func=AF.Ln, bias=ONEB[:, 0:1], scale=-1.0),
                   reads=[("kf", par), "oneb"], writes=[("gl", par)])
                for h in range(4):
                    rm = rmB[:, 0:80] if tail else rmA[:, 0:128]
                    PA("dve", (lambda e, h=h, rm=rm: e.tensor_tensor_scan(
                        out=bb[:, h, 0:tn], data0=rm, data1=gl[:, h, 0:tn],
                        initial=0.0, op0=ALU.mult, op1=ALU.add)), reads=[("gl", par), "consts"], writes=[("bb", par)])
                PA("act", lambda e: e.activation(out=Em[:, :, 0:tn], in_=bb[:, :, 0:tn], func=AF.Exp, scale=-1.0),
                   reads=[("bb", par)] + [("th", par, h) for h in range(4)], writes=[("Em", par)] + [("th", par, h) for h in range(4)])
                PA("act", lambda e: e.activation(out=Epp[:, :, 0:tn], in_=bb[:, :, 0:tn], func=AF.Exp),
                   reads=[("bb", par)], writes=[("Ep", par)])
                PA("dve", lambda e: e.tensor_tensor(out=ktp[:, :, 0:tn], in0=kf[:, :, 0:tn], in1=Em[:, :, 0:tn], op=ALU.mult),
                   reads=[("kf", par), ("Em", par)] + [("th", par, h) for h in range(4)], writes=[("kt", par)])
                PA("dve", lambda e: e.tensor_tensor(out=qtp[:, :, 0:tn], in0=qb[:, :, 0:tn], in1=Epp[:, :, 0:tn], op=ALU.mult),
                   reads=[("qb", par), ("Ep", par)], writes=[("qt", par)])
                if not tail:
                    outs = [(dd[:, g, 0:128], 16, 128) for g in range(4)]
                    for g, cur in windows(U, 144, outs, "Ux"):
                        if ti == 0 and g > 0:
                            w = 2 << g
                            PA("dve", (lambda e, cur=cur, w=w: e.tensor_tensor(
                                out=wfix[:, 0:w - 1], in0=cur[:, 16:16 + w - 1], in1=invc[:, 0:w - 1], op=ALU.mult)),
                                reads=[("wbuf", 0), ("wbuf", 1), "consts"], writes=["wfix"])
                            PA("dve", (lambda e, g=g, w=w: e.tensor_tensor(
                                out=dd[:, g, 0:w - 1], in0=wfix[:, 0:w - 1], in1=U[:, g, 16:16 + w - 1], op=ALU.subtract)),
                                reads=["wfix", "Ux", ("dd", g)], writes=[("dd", g)])
                        elif ti == 0 and g == 0:
                            PA("dve", lambda e: e.memset(dd[:, 0, 0:1], 0.0), reads=[("dd", 0)], writes=[("dd", 0)])
                else:
                    outs = [(dd[:, g, 0:16], 16, 16) for g in range(4)]
                    for _ in windows(U, 32, outs, "Ux"):
                        pass
                    PA("sp", lambda e: e.dma_start(out=plp_d[l].rearrange("g p r -> p g r"), in_=U[:, :, 17:32]),
                       reads=["Ux"], dma=True, arena=True)
                    PA("sp", lambda e: e.dma_start(out=ustage.rearrange("p g i r -> p g (i r)")[:, :, 0:240],
                                                   in_=spl_d[l].rearrange("g p n -> p g n")),
                       writes=["ustage"], dma=True, arena=True)
                    PA("dve", lambda e: e.tensor_copy(out=uext[:, :, :, 0:15], in_=ustage[:, :, :, 0:15]),
                       reads=["ustage"], writes=["uext"])
                    PA("dve", lambda e: e.tensor_copy(out=uext[:, :, :, 15:19],
                                                      in_=U[:, :, 32:96].rearrange("p g (i r) -> p g i r", i=16)),
                       reads=["Ux", "uext"], writes=["uext"])
                    PA("dve", lambda e: e.tensor_copy(out=ustage[:, :, :, 0:15], in_=uext[:, :, :, 4:19]),
                       reads=["uext", "ustage"], writes=["ustage"])
                    PA("sp", lambda e: e.dma_start(out=pls_d[l].rearrange("g p n -> p g n"),
                                                   in_=ustage.rearrange("p g i r -> p g (i r)")[:, :, 0:240]),
                       reads=["ustage"], dma=True, arena=True)
                    uflat = uext.rearrange("p g i r -> p g (i r)")
                    for g in range(4):
                        w = 2 << g
                        bufs = [wa, wb]
                        for half in range(2):
                            src = uflat[:, g, half * 152:(half + 1) * 152]
                            cur = None
                            sh, bi, lo = 1, 0, 0
                            while sh < w:
                                dst = bufs[bi]
                                prev = src if cur is None else cur
                                lo2 = lo + sh
                                PA("dve", (lambda e, dst=dst, prev=prev, lo2=lo2, sh=sh: e.tensor_tensor(
                                    out=dst[:, lo2:152], in0=prev[:, lo2:152], in1=prev[:, lo2 - sh:152 - sh], op=ALU.add)),
                                    reads=["uext", ("wbuf", 1 - bi)], writes=[("wbuf", bi)])
                                cur, lo, sh, bi = dst, lo2, sh * 2, 1 - bi
                            dst_ap = dd[:, g, 16 + half * 32:16 + half * 32 + 32].rearrange("p (i r) -> p i r", i=8)
                            curv = cur[:, 0:152].rearrange("p (i r) -> p i r", i=8)[:, :, 15:19]
                            srcv = src.rearrange("p (i r) -> p i r", i=8)[:, :, 15:19]
                            PA("dve", (lambda e, dst_ap=dst_ap, curv=curv, srcv=srcv, w=w: e.scalar_tensor_tensor(
                                out=dst_ap, in0=curv, scalar=1.0 / w, in1=srcv, op0=ALU.mult, op1=ALU.subtract)),
                                reads=["uext", ("wbuf", 0), ("wbuf", 1), ("dd", g)], writes=[("dd", g)])
                for g in range(4):
                    bk = newbank(pool)
                    PA("pe", (lambda e, g=g, bk=bk: e.matmul(psb(bk, tn), lhsT=pw[:, g, :], rhs=dd[:, g, 0:tn], start=True, stop=True)),
                       reads=[("dd", g), ("poolw", l % 2)], writes=[("ps", bk)])
                    PA("act", (lambda e, g=g, bk=bk: e.activation(out=mixed2[:, par3, 4 + g, 0:tn], in_=psb(bk, tn), func=AF.Copy, scale=psc(g))),
                       reads=[("ps", bk), "vec"], writes=[("mixed", par3, 4 + g)])
                if not tail:
                    PA("dve", lambda e: e.tensor_copy(out=U[:, :, 0:16], in_=U[:, :, 128:144]), reads=["Ux"], writes=["Uhalo", "Ux"])
                    tn2 = 80 if ti + 1 == 16 else 128
                    norm_stats(t0 + 128, tn2, m_nsq, m_nln, m_nrstd2[:, 1 - par], ("nrstd", 1 - par), pool)

            def stage2(ti):
                tail = (ti == 16)
                par = ti % 2
                par3 = ti % 3
                pool = "t2" if tail else "s2"
                if not tail:
                    bO = hgrn_part(par, 0, 128, [(0, 64), (64, 64)], pool)
                    onorm(par, par3, psb(bO), [("ps", bO)], 512, 128, 0, pool)
                else:
                    bO = hgrn_part(par, 0, 16, [(0, 16)], pool)
                    onorm(par, par3, psb(bO, 64), [("ps", bO)], 64, 16, 0, pool)
                    PA("sp", lambda e: e.dma_start(out=hgp_d[l].rearrange("h d v -> d h v"), in_=Sst[:, :, :]),
                       reads=["S"], dma=True, arena=True)
                    sample_part(l, par, par3)

            def stage3(ti):
                tail = (ti == 16)
                par3 = ti % 3
                pool = "s3"
                t0 = ti * 128
                tn = 80 if tail else 128
                for half in range(2):
                    bk = newbank(pool)
                    for j in range(4):
                        m = half * 4 + j

                        def mmw(e, m=m, j=j, bk=bk):
                            ins = None
                            for k in range(KC):
                                wv = w_oA if k < 4 else w_oB
                                ins = e.matmul(ps[:, bk * 512 + j * tn:bk * 512 + (j + 1) * tn], lhsT=wv[:, k % 4, m * 128:(m + 1) * 128],
                                               rhs=mixed2[:, par3, k, 0:tn], start=(k == 0), stop=(k == KC - 1))
                            return ins
                        PA("pe", mmw, reads=[("mixed", par3, k) for k in range(KC)] + [("W", MIXPG["oA"]), ("W", MIXPG["oB"])]
                           + ([("ps", bk)] if j else []), writes=[("ps", bk)])
                    ms = range(half * 4, half * 4 + 4)
                    PA("dve", (lambda e, half=half, bk=bk: e.tensor_tensor(
                        out=xres[:, half * 4:half * 4 + 4, t0:t0 + tn],
                        in0=ps[:, bk * 512:bk * 512 + 4 * tn].rearrange("p (m t) -> p m t", m=4),
                        in1=xres[:, half * 4:half * 4 + 4, t0:t0 + tn], op=ALU.add)),
                        reads=[("ps", bk)] + xkeys(ms, t0, t0 + tn), writes=xkeys(ms, t0, t0 + tn))

            NT = 17
            allrec = Rec()
            sink[0] = allrec
            norm_stats(0, 128, m_nsq, m_nln, m_nrstd2[:, 0], ("nrstd", 0), "s1")
            for ti in range(NT):
                stage1(ti)
                stage2(ti)
                stage3(ti)
            sink[0] = P
            gsched(allrec.items)

        def sample_part(l, par, par3):
            w_i = page(MIXPG["i"]).rearrange("p (k n) -> p k n", k=8)
            ktp, qtp, Epp, gtp = kt2[:, par], qt2[:, par], Ep2[:, par], gate2[:, par]
            vtp = vtok2[:, par]
            pool = "t2"
            hm = hm2[par]
            cs = slice(16, 80)
            bT = newbank(pool)
            psT = ps[0:64, bT * 512:(bT + 1) * 512].bitcast(BF16)

            def tr(e):
                ins = None
                for h in range(4):
                    ins = e.transpose(out=psT[:, h * 128:(h + 1) * 128], in_=ktp[:, h, cs], identity=identb[:, :])
                return ins
            PA("pe", tr, reads=[("kt", par), "consts"], writes=[("ps", bT)])
            PA("act", lambda e: e.copy(out=ktok[0:64, :], in_=psT[:, 0:512]), reads=[("ps", bT)], writes=["ktok"])
            bV = newbank(pool)

            def mmv(e):
                ins = None
                for k in range(KC):
                    ins = e.matmul(ps[0:64, bV * 512:(bV + 1) * 512], lhsT=hm[:, k, cs], rhs=w_i[:, k, :],
                                   start=(k == 0), stop=(k == KC - 1))
                return ins
            PA("pe", mmv, reads=[("hm", par), ("W", MIXPG["i"])], writes=[("ps", bV)])
            PA("dve", lambda e: e.tensor_copy(out=vtp[0:64, :], in_=ps[0:64, bV * 512:(bV + 1) * 512]),
               reads=[("ps", bV)], writes=[("vtok", par)])
            bS = newbank(pool)

            def mms(e):
                ins = None
                for h in range(4):
                    ins = e.matmul(ps[0:64, bS * 512 + h * 64:bS * 512 + (h + 1) * 64], lhsT=ktp[:, h, cs], rhs=qtp[:, h, cs],
                                   start=True, stop=True)
                return ins
            PA("pe", mms, reads=[("kt", par), ("qt", par)], writes=[("ps", bS)])
            PA("dve", lambda e: e.tensor_tensor(
                out=scm[0:64, 0:256].rearrange("p (h t) -> p h t", h=4),
                in0=ps[0:64, bS * 512:bS * 512 + 256].rearrange("p (h t) -> p h t", h=4),
                in1=smaskb[:, :, :], op=ALU.mult), reads=[("ps", bS), "consts"], writes=["scm"])
            bOi = newbank(pool)
            bOx = newbank(pool)
            pinned.update((bOi, bOx))

            def mmoi(e):
                ins = None
                for h in range(4):
                    ins = e.matmul(ps[:, bOi * 512 + h * 64:bOi * 512 + (h + 1) * 64], lhsT=vtp[0:64, h * 128:(h + 1) * 128],
                                   rhs=scm[0:64, h * 64:(h + 1) * 64], start=True, stop=True)
                return ins
            PA("pe", mmoi, reads=[("vtok", par), "scm"], writes=[("ps", bOi)])
            for b in range(8):
                hb = b % 2
                sl = slice(2 * hb, 2 * hb + 2)
                src = shg_d[l, 2 * b:2 * b + 2].rearrange("i h d v -> d (i h) v")
                PA("sp", (lambda e, src=src, sl=sl: e.dma_start(out=Ss[:, sl].rearrange("p i h v -> p (i h) v"), in_=src)),
                   writes=[("Ss", hb)] + ODDKEYS, dma=True, arena=True)
                PA("act", (lambda e, sl=sl: e.copy(out=Ssb[:, sl].rearrange("p i h v -> p (i h v)"),
                                                 in_=Ss[:, sl].rearrange("p i h v -> p (i h v)"))),
                   reads=[("Ss", hb)], writes=[("Ssb", hb)] + ODDKEYS)

                def mmx(e, b=b, hb=hb):
                    ins = None
                    for ii in range(2):
                        i = 2 * b + ii
                        for h in range(4):
                            c = bOx * 512 + h * 64 + 4 * i
                            ins = e.matmul(ps[:, c:c + 4], lhsT=Ssb[:, 2 * hb + ii, h, :], rhs=qtp[:, h, 16 + 4 * i:16 + 4 * i + 4],
                                           start=True, stop=True, skip_group_check=True)
                    return ins
                PA("pe", mmx, reads=[("Ssb", hb), ("qt", par), ("ps", bOx)], writes=[("ps", bOx)])
                for ii in range(2):
                    i = 2 * b + ii
                    si = 2 * hb + ii
                    PA("dve", (lambda e, i=i, si=si: e.tensor_scalar(out=km[0:64, si, :], in0=ktok[0:64, :], scalar1=smcol[:, i:i + 1],
                                                                     scalar2=None, op0=ALU.mult)),
                       reads=["ktok", "consts"], writes=[("km", si), ("hm", 1 - par)])
                    bA = newbank(pool)

                    def mma(e, si=si, bA=bA):
                        ins = None
                        for h in range(4):
                            ins = e.matmul(ps[:, bA * 512 + h * 128:bA * 512 + (h + 1) * 128], lhsT=km[0:64, si, h * 128:(h + 1) * 128],
                                           rhs=vtp[0:64, h * 128:(h + 1) * 128], start=True, stop=True)
                        return ins
                    PA("pe", mma, reads=[("km", si), ("vtok", par)], writes=[("ps", bA)])
                    PA("dve", (lambda e, si=si, bA=bA: e.tensor_tensor(
                        out=tmpS.rearrange("p h v -> p (h v)"), in0=psb(bA), in1=Ss[:, si, :, :].rearrange("p h v -> p (h v)"), op=ALU.add)),
                        reads=[("ps", bA), ("Ss", hb)], writes=["tmpS"])
                    lastcol = 16 + 4 * i + 3
                    dec = Epp[:, :, lastcol:lastcol + 1].broadcast_to([128, 4, 128])
                    PA("dve", (lambda e, si=si, dec=dec: e.tensor_tensor(out=Ss[:, si, :, :], in0=tmpS[:, :, :], in1=dec, op=ALU.mult)),
                       reads=["tmpS", ("Ep", par), ("Ssb", hb), ("Ss", hb)], writes=[("Ss", hb)])
                dst = hgs_d[l, 2 * b:2 * b + 2].rearrange("i h d v -> d (i h) v")
                PA("sp", (lambda e, dst=dst, sl=sl: e.dma_start(out=dst, in_=Ss[:, sl].rearrange("p i h v -> p (i h) v"))),
                   reads=[("Ss", hb)], dma=True, arena=True)
            pinned.clear()
            PA("act", lambda e: e.copy(out=oint[:, 0:256], in_=psb(bOx, 256)), reads=[("ps", bOx)], writes=["oint"])
            PA("dve", lambda e: e.tensor_tensor(out=osum[:, 0:256], in0=psb(bOi, 256), in1=oint[:, 0:256], op=ALU.add),
               reads=[("ps", bOi), "oint"], writes=["osum", ("wbuf", 0), ("wbuf", 1)])
            hgn = lambda h: vcol(V_HGN + l * 4 + h)
            PA("act", lambda e: e.activation(out=osq[:, 0:256], in_=osum[:, 0:256], func=AF.Square), reads=["osum"], writes=["osq"])
            bk = newbank(pool)
            PA("pe", lambda e: e.matmul(psb(bk, 256), lhsT=onesb[:, :], rhs=osq[:, 0:256], start=True, stop=True),
               reads=["osq", "onesb"], writes=[("ps", bk)])
            PA("act", lambda e: e.activation(out=olnv[:, 0:256], in_=psb(bk, 256), func=AF.Ln, bias=EPSB[:, 0:1], scale=1.0 / 128),
               reads=[("ps", bk), "epsb"], writes=["olnv"])
            PA("act", lambda e: e.activation(out=orstd[:, 0:256], in_=olnv[:, 0:256], func=AF.Exp, scale=-0.5),
               reads=["olnv"], writes=["orstd"])
            PA("dve", lambda e: e.tensor_tensor(out=t1[:, 0:256], in0=osum[:, 0:256], in1=orstd[:, 0:256], op=ALU.mult),
               reads=["osum", "orstd"], writes=["t1"])
            for h in range(4):
                PA("dve", (lambda e, h=h: e.scalar_tensor_tensor(
                    out=mixed2[:, par3, h, 16:80], in0=t1[:, h * 64:(h + 1) * 64], scalar=hgn(h), in1=gtp[:, h, 16:80],
                    op0=ALU.mult, op1=ALU.mult)), reads=["t1", ("gate", par), "vec", ("mixed", par3, h)], writes=[("mixed", par3, h)])

        oneb_t = sb("oneb", [128, 2], F32)
        ONEB = oneb_t
        P.add("dve", lambda e: e.memset(oneb_t[:, :], 1.0), writes=["oneb"])

        dma_ffn_group(0, 0, 0)
        for l in range(L):
            ffn(l, 0)
            P.barrier()
            mixing(l)
            dma_ffn_group(l, 1, 0)
            P.barrier()
            ffn(l, 1)
        P.barrier()
        for n, (t0, tn) in enumerate(FT):
            norm_tile(t0, tn, V_NFIN, f_nsq, f_nln, f_nrstd,
                      (lambda k, tn=tn: f_ytmp[:, k, 0:tn]), (lambda k: [("ytmp", k)]))
            yv = yT.rearrange("(k p) t -> p k t", p=128)
            P.add("sp", (lambda e, t0=t0, tn=tn: e.dma_start(out=yv[:, :, t0:t0 + tn], in_=f_ytmp[:, :, 0:tn])),
                  reads=[("ytmp", k) for k in range(KC)], dma=True, arena=True)

        P.finalize()
        with nc.Block() as block:
            P.emit(nc, block, esem, qsem)
    return nc


def _consts():
    c = np.zeros((128, NCST), np.float32)
    c[:, C_ID:C_ID + 128] = np.eye(128, dtype=np.float32)
    s = np.arange(128)[:, None]
    t = np.arange(128)[None, :]
    c[:, C_CM:C_CM + 128] = ((s <= t) & (s // 64 == t // 64)).astype(np.float32)
    s = np.arange(64)[:, None]
    t = np.arange(64)[None, :]
    c[0:64, C_SM:C_SM + 64] = ((s <= t) & (s // 4 == t // 4)).astype(np.float32)
    c[0:64, C_SMC:C_SMC + 16] = (np.arange(64)[:, None] // 4 == np.arange(16)[None, :]).astype(np.float32)
    ra = np.ones(512, np.float32)
    ra[0::64] = 0.0
    c[:, C_RMA:C_RMA + 512] = ra[None, :]
    rb = np.ones(80, np.float32)
    rb[0] = 0.0
    rb[16::4] = 0.0
    c[:, C_RMB:C_RMB + 320] = np.tile(rb, 4)[None, :]
    c[:, C_INV:C_INV + 16] = (1.0 / np.arange(1, 17, dtype=np.float32))[None, :]
    return c


def _vecs(norm_ffn1, norm_mix, norm_ffn2, norm_final, hg_norm, pool_scale, lb_logits):
    v = np.zeros((128, NV), np.float32)
    for l in range(4):
        v[:, V_NF1 + l * 8:V_NF1 + l * 8 + 8] = norm_ffn1[l].reshape(8, 128).T
        v[:, V_NMX + l * 8:V_NMX + l * 8 + 8] = norm_mix[l].reshape(8, 128).T
        v[:, V_NF2 + l * 8:V_NF2 + l * 8 + 8] = norm_ffn2[l].reshape(8, 128).T
        v[:, V_HGN + l * 4:V_HGN + l * 4 + 4] = hg_norm[l].T
        v[:, V_PSC + l * 4:V_PSC + l * 4 + 4] = pool_scale[l].reshape(4, 128).T
    v[:, V_NFIN:V_NFIN + 8] = norm_final.reshape(8, 128).T
    v[:, V_LBL:V_LBL + 16] = lb_logits.reshape(4, 4, 128).transpose(2, 1, 0).reshape(128, 16)
    return v


_NC_CACHE = {}


def make_in_maps(inputs, cores):
    f = lambda a: np.ascontiguousarray(np.asarray(a, dtype=np.float32))
    x_prompt, x_sample, meta = f(inputs["x_prompt"]), f(inputs["x_sample"]), f(inputs["meta"])
    state_hgrn, state_pool = f(inputs["state_hgrn"]), f(inputs["state_pool"])
    cst = _consts()
    vecs = _vecs(f(inputs["norm_ffn1"]), f(inputs["norm_mix"]), f(inputs["norm_ffn2"]), f(inputs["norm_final"]),
                 f(inputs["hg_norm"]), f(inputs["pool_scale"]), f(inputs["lb_logits"]))
    shared = {k: f(inputs[k]) for k in ("w_ffn1_in", "w_ffn1_out", "w_ffn2_in", "w_ffn2_out", "w_in", "w_out", "pool_w")}
    maps = []
    for c in cores:
        xs = x_sample[NSEQ * c:NSEQ * (c + 1)].reshape(NSM, D)
        xall = np.concatenate([meta, x_prompt[c], xs], axis=0)
        sp = state_pool[:, NSEQ * c:NSEQ * (c + 1)]
        spT = np.ascontiguousarray(sp.transpose(0, 3, 1, 2).reshape(4, 4, 128, NSEQ * 15))
        m = dict(shared)
        m.update({
            "xT": np.ascontiguousarray(xall.T),
            "vecs": vecs, "cst": cst,
            "state_hgrn": np.ascontiguousarray(state_hgrn[:, NSEQ * c:NSEQ * (c + 1)]),
            "state_poolT": spT,
        })
        maps.append(m)
    return maps


def assemble(results, ncores):
    y_prompt = np.zeros((ncores, 2048, D), np.float32)
    y_sample = np.zeros((ncores * NSEQ, 4, D), np.float32)
    hg_p = np.zeros((4, ncores, NH, 128, 128), np.float32)
    pool_p = np.zeros((4, ncores, 15, 512), np.float32)
    hg_s = np.zeros((4, ncores * NSEQ, NH, 128, 128), np.float32)
    pool_s = np.zeros((4, ncores * NSEQ, 15, 512), np.float32)
    for c, r in enumerate(results):
        y = np.asarray(r["yT"]).T
        y_prompt[c] = y[16:NPR]
        y_sample[NSEQ * c:NSEQ * (c + 1)] = y[NPR:].reshape(NSEQ, 4, D)
        hg_p[:, c] = np.asarray(r["hgp"])
        pool_p[:, c] = np.asarray(r["poolpT"]).reshape(4, 512, 15).transpose(0, 2, 1)
        hg_s[:, NSEQ * c:NSEQ * (c + 1)] = np.asarray(r["hgs"])
        pool_s[:, NSEQ * c:NSEQ * (c + 1)] = np.asarray(r["poolsT"]).reshape(4, 512, NSEQ, 15).transpose(0, 2, 3, 1)
    return (y_prompt, y_sample, hg_p, pool_p, hg_s, pool_s)


def kernel(**inputs):
    if "nc" not in _NC_CACHE:
        _NC_CACHE["nc"] = build_program(4)
    nc = _NC_CACHE["nc"]
    cores = list(range(NCORES))
    in_maps = make_in_maps(inputs, cores)
    res = run_bass_kernel_spmd(nc, in_maps, core_ids=cores)
    return assemble(res.results, NCORES)
```
